# Optimizing a Trainium2 kernel written in Bass

```python
import jax, jax.numpy as jnp
from jax import lax
import numpy as np

D_MODEL = 1024
BATCH = 8
SEQ = 2048
DEPTH = 1
DEC_BATCH = 128
DEC_SEQ = 8
PAST_LEN = 16384
PAGE_SIZE = 128

HGRN_HEADS = 4
HGRN_KEY = 128
HGRN_VAL = 128
HGRN_FDIM = HGRN_HEADS * HGRN_KEY
HGRN_WIDTH = HGRN_HEADS * HGRN_VAL
HGRN_CHUNK = 64
GMLP_GROUPS = 4
GMLP_CHUNK = 128
GMLP_WIDTH = 512
GMLP_GW = GMLP_WIDTH // GMLP_GROUPS
D_FF = 2816
CONV_W = 3
PLE_DIM = 256
N_IN = 2 * HGRN_FDIM + 2 * HGRN_WIDTH + 2 * GMLP_WIDTH + 2 * D_MODEL
EPS = 1e-6

kernel_name = "hgrn2_chunkmlp_convffn_hybrid_step"


def rmsnorm(x, w):
    xf = x.astype(jnp.float32)
    y = xf * lax.rsqrt(jnp.mean(xf * xf, axis=-1, keepdims=True) + EPS)
    return (y * w.astype(jnp.float32)).astype(x.dtype)


def layernorm(x, w, b):
    xf = x.astype(jnp.float32)
    xc = xf - jnp.mean(xf, axis=-1, keepdims=True)
    y = xc * lax.rsqrt(jnp.mean(xc * xc, axis=-1, keepdims=True) + EPS)
    return (y * w.astype(jnp.float32) + b.astype(jnp.float32)).astype(x.dtype)


def hgrn2_recurrence(q, k, v, log_f, s0):
    B, T, H, K = q.shape
    c = min(HGRN_CHUNK, T)
    n = T // c

    def chunks(a):
        return a.reshape(B, n, c, H, a.shape[-1]).transpose(1, 0, 3, 2, 4)

    causal = jnp.tril(jnp.ones((c, c), dtype=bool))[:, :, None]

    def step(S, blk):
        qb, kb, vb, gb = blk
        b = jnp.cumsum(gb, axis=2)
        o_inter = jnp.einsum("bhtk,bhkv->bhtv", qb * jnp.exp(b), S)
        rel = jnp.where(causal, b[:, :, :, None, :] - b[:, :, None, :, :], -jnp.inf)
        scores = jnp.einsum("bhtk,bhsk,bhtsk->bhts", qb, kb, jnp.exp(rel))
        o = o_inter + jnp.einsum("bhts,bhsv->bhtv", scores, vb)
        b_end = b[:, :, -1, :]
        S = jnp.exp(b_end)[..., None] * S + jnp.einsum(
            "bhsk,bhsv->bhkv", kb * jnp.exp(b_end[:, :, None, :] - b), vb)
        return S, o

    S, o = lax.scan(step, s0, (chunks(q), chunks(k), chunks(v), chunks(log_f)))
    return o.transpose(1, 0, 3, 2, 4).reshape(B, T, H * v.shape[-1]), S


def chunk_spatial_gate(u, v, w_s, b_s):
    B, T, _ = v.shape
    c = min(GMLP_CHUNK, T)
    n = T // c
    vg = v.reshape(B, n, c, GMLP_GROUPS, GMLP_GW)
    w = jnp.tril(w_s[:, :c, :c])
    mixed = jnp.einsum("gts,bnsgc->bntgc", w, vg) + b_s[:, :c].T[None, None, :, :, None]
    return u * mixed.reshape(B, T, GMLP_WIDTH)


def conv_ffn(xn, w_up, conv_w, conv_b, w_down, conv_state):
    T = xn.shape[1]
    h = xn @ w_up
    hp = jnp.concatenate([conv_state.astype(h.dtype), h], axis=1)
    y = conv_b + hp[:, 0:T] * conv_w[0]
    for j in range(1, CONV_W):
        y = y + hp[:, j:j + T] * conv_w[j]
    a, b = jnp.split(y, 2, axis=-1)
    return (jax.nn.gelu(a, approximate=False) * b) @ w_down, hp[:, T:]


def layer_forward(x, p, s_hgrn, s_conv, lb, norm_mix_w, w_in, hgrn_norm_w, ln_v_w, ln_v_b,
                  w_spatial, b_spatial, w_a_out, w_b_out, w_o, norm_ffn_w, w_up, conv_w,
                  conv_b, w_down, norm_ple_w, w_ple_gate, w_ple_proj):
    B, T, _ = x.shape
    f32 = jnp.float32
    xn = rmsnorm(x, norm_mix_w)
    z = xn @ w_in
    o1 = HGRN_FDIM
    o2 = o1 + HGRN_FDIM
    o3 = o2 + HGRN_WIDTH
    o4 = o3 + HGRN_WIDTH
    o5 = o4 + GMLP_WIDTH
    o6 = o5 + GMLP_WIDTH
    o7 = o6 + D_MODEL
    q_pre, f_pre, i_in, og, u_pre, v_pre, g_a, g_b = jnp.split(z, [o1, o2, o3, o4, o5, o6, o7], axis=-1)

    fg = lb + (1.0 - lb) * jax.nn.sigmoid(f_pre.astype(f32))
    o_rec, s_new = hgrn2_recurrence(
        jax.nn.silu(q_pre.astype(f32)).reshape(B, T, HGRN_HEADS, HGRN_KEY),
        (1.0 - fg).reshape(B, T, HGRN_HEADS, HGRN_KEY),
        i_in.astype(f32).reshape(B, T, HGRN_HEADS, HGRN_VAL),
        jnp.log(fg).reshape(B, T, HGRN_HEADS, HGRN_KEY),
        s_hgrn.astype(f32))
    o_a = rmsnorm(o_rec.astype(x.dtype) * jax.nn.sigmoid(og), hgrn_norm_w)

    u = jax.nn.gelu(u_pre, approximate=False)
    v = layernorm(jax.nn.gelu(v_pre, approximate=False), ln_v_w, ln_v_b)
    o_b = chunk_spatial_gate(u, v, w_spatial, b_spatial)

    mix = (jax.nn.sigmoid(g_a) * (o_a @ w_a_out) + jax.nn.sigmoid(g_b) * (o_b @ w_b_out)) @ w_o
    h = x + mix

    ff, conv_new = conv_ffn(rmsnorm(h, norm_ffn_w), w_up, conv_w, conv_b, w_down, s_conv)
    h = h + ff

    h = h + jax.nn.sigmoid(rmsnorm(h, norm_ple_w) @ w_ple_gate) * (p @ w_ple_proj)
    return h, s_new.astype(s_hgrn.dtype), conv_new, v


def setup_inputs(seed: int = 0) -> dict:
    key = jax.random.key(seed)
    ks = iter(jax.random.split(key, 32))

    def nrm(shape, scale):
        return jax.random.normal(next(ks), shape, jnp.float32) * scale

    L = DEPTH
    return {
        "x_prompt": nrm((BATCH, SEQ, D_MODEL), 1.0),
        "x_sample": nrm((DEC_BATCH, DEC_SEQ, D_MODEL), 1.0),
        "p_prompt": nrm((L, BATCH, SEQ, PLE_DIM), 1.0),
        "p_sample": nrm((L, DEC_BATCH, DEC_SEQ, PLE_DIM), 1.0),
        "state_hgrn": nrm((L, DEC_BATCH, HGRN_HEADS, HGRN_KEY, HGRN_VAL), 0.5),
        "state_conv": nrm((L, DEC_BATCH, CONV_W - 1, 2 * D_FF), 1.0),
        "lb_logits": nrm((L + 1, HGRN_FDIM), 0.5),
        "norm_mix_w": 1.0 + nrm((L, D_MODEL), 0.02),
        "w_in": nrm((L, D_MODEL, N_IN), D_MODEL ** -0.5),
        "hgrn_norm_w": 1.0 + nrm((L, HGRN_WIDTH), 0.02),
        "ln_v_w": 1.0 + nrm((L, GMLP_WIDTH), 0.02),
        "ln_v_b": nrm((L, GMLP_WIDTH), 0.02),
        "w_spatial": nrm((L, GMLP_GROUPS, GMLP_CHUNK, GMLP_CHUNK), GMLP_CHUNK ** -0.5),
        "b_spatial": 1.0 + nrm((L, GMLP_GROUPS, GMLP_CHUNK), 0.02),
        "w_a_out": nrm((L, HGRN_WIDTH, D_MODEL), HGRN_WIDTH ** -0.5),
        "w_b_out": nrm((L, GMLP_WIDTH, D_MODEL), GMLP_WIDTH ** -0.5),
        "w_o": nrm((L, D_MODEL, D_MODEL), D_MODEL ** -0.5),
        "norm_ffn_w": 1.0 + nrm((L, D_MODEL), 0.02),
        "w_up": nrm((L, D_MODEL, 2 * D_FF), D_MODEL ** -0.5),
        "conv_w": nrm((L, CONV_W, 2 * D_FF), CONV_W ** -0.5),
        "conv_b": nrm((L, 2 * D_FF), 0.02),
        "w_down": nrm((L, D_FF, D_MODEL), D_FF ** -0.5),
        "norm_ple_w": 1.0 + nrm((L, D_MODEL), 0.02),
        "w_ple_gate": nrm((L, D_MODEL, D_MODEL), D_MODEL ** -0.5),
        "w_ple_proj": nrm((L, PLE_DIM, D_MODEL), PLE_DIM ** -0.5),
        "final_norm_w": 1.0 + nrm((D_MODEL,), 0.02),
    }


def reference(x_prompt, x_sample, p_prompt, p_sample, state_hgrn, state_conv, lb_logits,
              norm_mix_w, w_in, hgrn_norm_w, ln_v_w, ln_v_b, w_spatial, b_spatial, w_a_out,
              w_b_out, w_o, norm_ffn_w, w_up, conv_w, conv_b, w_down, norm_ple_w, w_ple_gate,
              w_ple_proj, final_norm_w):
    lb_all = jnp.cumsum(jax.nn.softmax(lb_logits.astype(jnp.float32), axis=0), axis=0)
    hp, hs = x_prompt, x_sample
    hgrn_p, hgrn_s, conv_p, conv_s, v_s = [], [], [], [], []
    for l in range(DEPTH):
        lw = (norm_mix_w[l], w_in[l], hgrn_norm_w[l], ln_v_w[l], ln_v_b[l], w_spatial[l],
              b_spatial[l], w_a_out[l], w_b_out[l], w_o[l], norm_ffn_w[l], w_up[l], conv_w[l],
              conv_b[l], w_down[l], norm_ple_w[l], w_ple_gate[l], w_ple_proj[l])
        s0_p = jnp.zeros((BATCH, HGRN_HEADS, HGRN_KEY, HGRN_VAL), jnp.float32)
        c0_p = jnp.zeros((BATCH, CONV_W - 1, 2 * D_FF), hp.dtype)
        hp, sp, cp, _ = layer_forward(hp, p_prompt[l], s0_p, c0_p, lb_all[l], *lw)
        hs, ss, cs, vs = layer_forward(hs, p_sample[l], state_hgrn[l], state_conv[l], lb_all[l], *lw)
        hgrn_p.append(sp)
        hgrn_s.append(ss)
        conv_p.append(cp)
        conv_s.append(cs)
        v_s.append(vs)
    y_prompt = rmsnorm(hp, final_norm_w)
    y_sample = rmsnorm(hs, final_norm_w)
    hgrn_state_prompt = jnp.stack(hgrn_p, axis=0)
    hgrn_state_sample = jnp.stack(hgrn_s, axis=0)
    conv_state_prompt = jnp.stack(conv_p, axis=0)
    conv_state_sample = jnp.stack(conv_s, axis=0)
    gmlp_v_sample = jnp.stack(v_s, axis=0)
    return (y_prompt, y_sample, hgrn_state_prompt, hgrn_state_sample, conv_state_prompt, conv_state_sample, gmlp_v_sample)
```

```python
import contextlib
import numpy as np
import ml_dtypes
import concourse.bass as bass
import concourse.mybir as mybir
from concourse.bass_utils import run_bass_kernel_spmd

F32 = mybir.dt.float32
BF16 = mybir.dt.bfloat16
AF = mybir.ActivationFunctionType
ALU = mybir.AluOpType

P = 128
D = 1024
NIN = 5120
DFF = 2816
NJ = 22
SEQ = 2048
NSEQ_S = 16
LS = 8
EPS = 1e-6
NT_P = 4
NSLOT = 6
NDMA = 24
STRICT_SAME_ENGINE = True


class Res:
    __slots__ = ("name", "writer", "readers")

    def __init__(self, name):
        self.name = name
        self.writer = None
        self.readers = []


class Sched:
    def __init__(self, nc, stack):
        self.nc = nc
        self.names = ["pe", "act", "dve", "pool", "sp"]
        self.lists = {e: [] for e in self.names}
        self.sems = {e: stack.enter_context(nc.semaphore("s_" + e)) for e in self.names}
        self.cnt = {e: 0 for e in self.names}
        self.waited = {e: {} for e in self.names}
        self.dsems = [stack.enter_context(nc.semaphore("d%d" % i)) for i in range(NDMA)]
        self.dcnt = [0] * NDMA
        self.drr = {"sp": 0, "pool": 0}
        self.res = {}
        self.out_tokens = []

    def R(self, *key):
        r = self.res.get(key)
        if r is None:
            r = Res(key)
            self.res[key] = r
        return r

    def _semof(self, tok):
        if tok[0] == "e":
            return ("e", tok[1]), self.sems[tok[1]]
        return ("d", tok[1]), self.dsems[tok[1]]

    @staticmethod
    def _flat(xs):
        out = []
        for x in xs:
            if isinstance(x, (list, tuple)):
                out.extend(Sched._flat(x))
            else:
                out.append(x)
        return out

    def emit(self, eng, fn, reads=(), writes=(), dma=False, is_out=False):
        reads = self._flat(reads)
        writes = self._flat(writes)
        deps = set()
        for r in reads:
            if r.writer is not None:
                deps.add(r.writer)
        for w in writes:
            if w.writer is not None and (STRICT_SAME_ENGINE or not (w.writer[0] == "e" and w.writer[1] == eng and not dma)):
                deps.add(w.writer)
            for t in w.readers:
                if STRICT_SAME_ENGINE or not (t[0] == "e" and t[1] == eng and not dma):
                    deps.add(t)
        need = {}
        for tok in deps:
            if tok[0] == "e" and tok[1] == eng and not dma and eng == "pe":
                continue
            key, sem = self._semof(tok)
            if need.get(key, (None, 0))[1] < tok[2]:
                need[key] = (sem, tok[2])
        waits = []
        wd = self.waited[eng]
        for key, (sem, val) in need.items():
            if wd.get(key, 0) < val:
                waits.append((sem, val))
                wd[key] = val
        if dma:
            if eng == "pool":
                s = 16 + self.drr["pool"]
                self.drr["pool"] = (self.drr["pool"] + 1) % (NDMA - 16)
            else:
                s = self.drr["sp"]
                self.drr["sp"] = (self.drr["sp"] + 1) % 16
            prior = self.dcnt[s] * 16
            key = ("d", s)
            if prior > 0 and wd.get(key, 0) < prior:
                waits.append((self.dsems[s], prior))
                wd[key] = prior
            self.dcnt[s] += 1
            tok = ("d", s, self.dcnt[s] * 16)
            inc = (self.dsems[s], 16)
            if is_out:
                self.out_tokens.append(tok)
        else:
            self.cnt[eng] += 1
            tok = ("e", eng, self.cnt[eng])
            inc = (self.sems[eng], 1)
        self.lists[eng].append((waits, fn, inc))
        for r in reads:
            r.readers.append(tok)
        for w in writes:
            w.writer = tok
            w.readers = []
        return tok

    def finish(self):
        waits = []
        for s in range(NDMA):
            if self.dcnt[s] > 0:
                waits.append((self.dsems[s], self.dcnt[s] * 16))
        self.lists["sp"].append((waits, None, None))

    def replay(self):
        nc = self.nc
        lists = self.lists

        needed = {}
        for name in self.names:
            for waits, fn, inc in lists[name]:
                for sem, val in waits:
                    needed.setdefault(id(sem), set()).add(val)

        def run(e, items):
            pending = 0
            count = 0
            for waits, fn, inc in items:
                for sem, val in waits:
                    e.wait_ge(sem, val)
                if fn is not None:
                    ins = fn(e)
                    if inc[1] == 16:
                        ins.then_inc(inc[0], 16)
                        continue
                    count += 1
                    pending += 1
                    if count in needed.get(id(inc[0]), ()):
                        ins.then_inc(inc[0], pending)
                        pending = 0

        with nc.Block() as block:
            @block.tensor
            def _(e):
                run(e, lists["pe"])

            @block.scalar
            def _(e):
                run(e, lists["act"])

            @block.vector
            def _(e):
                run(e, lists["dve"])

            @block.gpsimd
            def _(e):
                run(e, lists["pool"])

            @block.sync
            def _(e):
                run(e, lists["sp"])


def build_nc(debug=None):
    nc = bass.Bass("TRN2", target_bir_lowering=False)
    stack = contextlib.ExitStack()
    with stack:
        _build(nc, stack, debug)
    return nc


def _build(nc, stack, debug):
    S = Sched(nc, stack)
    R = S.R

    def din(name, shape, dt=F32):
        return nc.dram_tensor(name, list(shape), dt, kind="ExternalInput").ap()

    def dout(name, shape, dt=F32):
        return nc.dram_tensor(name, list(shape), dt, kind="ExternalOutput").ap()

    xp = din("xp", [SEQ, D]); xs = din("xs", [P, D])
    pp = din("pp", [SEQ, 256]); ps_ = din("ps", [P, 256])
    st_h = din("st_h", [NSEQ_S, 4, P, P]); st_c = din("st_c", [32, 2 * DFF])
    w_in = din("w_in", [D, NIN]); w_a = din("w_a", [512, D]); w_b = din("w_b", [512, D])
    w_o = din("w_o", [D, D]); w_up = din("w_up", [D, 2 * DFF]); w_dn = din("w_dn", [DFF, D])
    w_pg = din("w_pg", [D, D]); w_pp = din("w_pp", [256, D])
    c_identb = din("c_identb", [P, P], BF16); c_identf = din("c_identf", [P, P])
    c_maskp = din("c_maskp", [P, P]); c_masks = din("c_masks", [P, P]); c_maskf = din("c_maskf", [P, P])
    c_scanp = din("c_scanp", [P, 512]); c_scans = din("c_scans", [P, 512])
    c_bmask = din("c_bmask", [P, 16]); c_onesf = din("c_onesf", [P, 1])
    v_nw = din("v_nw", [P, 3, 8])
    v_hgw = din("v_hgw", [P, 4]); v_lbl = din("v_lbl", [P, 2, 4])
    v_lnw = din("v_lnw", [512]); v_lnb = din("v_lnb", [512]); v_fnw = din("v_fnw", [D])
    v_bsp = din("v_bsp", [512]); v_bss = din("v_bss", [512])
    v_cw = din("v_cw", [P, NJ, 2, 3]); v_cb = din("v_cb", [P, NJ, 2])
    v_wsp = din("v_wsp", [P, 4, P]); v_wss = din("v_wss", [P, 4, P])

    y_p = dout("y_p", [SEQ, D]); y_s = dout("y_s", [P, D])
    o_hp = dout("o_hp", [4, P, P]); o_hs = dout("o_hs", [NSEQ_S, 4, P, P])
    o_cp = dout("o_cp", [2, 2 * DFF]); o_cs = dout("o_cs", [32, 2 * DFF])
    o_gv = dout("o_gv", [P, 512])
    dbg = {}
    if debug:
        for name, shape in debug.items():
            dbg[name] = dout("dbg_" + name, shape)

    def sb(name, shape, dt=F32):
        return stack.enter_context(nc.sbuf_tensor(name, list(shape), dt))

    NT = NT_P
    TT = NT * P
    NT_MAX = NT + 1
    TTM = NT_MAX * P
    banks = [stack.enter_context(nc.psum_tensor("bank%d" % i, [P, 512], F32)) for i in range(7)]
    psT = stack.enter_context(nc.psum_tensor("psT", [P, 1024], BF16))
    RB = [R("bank", i) for i in range(7)]
    RpsT = R("psT")
    bank_ap = [b[:, :] for b in banks] + [psT[:].bitcast(F32)]
    RBall = RB + [RpsT]

    identb = sb("identb", [P, P], BF16); identf = sb("identf", [P, P])
    maskp = sb("maskp", [P, P]); masks = sb("masks", [P, P]); maskf = sb("maskf", [P, P])
    scanp = sb("scanp", [P, 512]); scans = sb("scans", [P, 512])
    bmask = sb("bmask", [P, 16]); onesf = sb("onesf", [P, 1])
    nw = sb("nw", [P, 3, 8]); hgw = sb("hgw", [P, 4]); lbl = sb("lbl", [P, 2, 4])
    oml = sb("oml", [P, 4]); lbt = sb("lbt", [P, 4])
    lnw = sb("lnw", [P, 512]); lnb = sb("lnb", [P, 512]); fnw = sb("fnw", [P, D])
    bsp = sb("bsp", [P, 512]); bss = sb("bss", [P, 512])
    cw = sb("cw", [P, NJ, 2, 3]); cb = sb("cb", [P, NJ, 2])
    wsp = sb("wsp", [P, 4, P], BF16); wss = sb("wss", [P, 4, P], BF16)
    Sf = sb("Sf", [P, 4, P]); Sb = sb("Sb", [P, 4, P], BF16)
    hh_p = sb("hh_p", [P, 2 * NJ, 2, 4])
    small = sb("small", [P, 256])
    epsc = sb("epsc", [P, 1])
    rstda = sb("rstda", [P, 8])

    xt_all = sb("xt_all", [P, NT_MAX, D])
    xt = [xt_all[:, t, :] for t in range(NT_MAX)]
    xnT = sb("xnT", [P, 8, TTM], BF16)
    mT = xnT
    arena_g = sb("arena_g", [P, NJ * TTM], BF16)
    gT = arena_g[:, :].rearrange("p (j t) -> p j t", t=TTM)
    ogT = arena_g[:, 0:4 * TTM].rearrange("p (h t) -> p h t", t=TTM)
    obT = arena_g[:, 4 * TTM:8 * TTM].rearrange("p (h t) -> p h t", t=TTM)
    t1 = arena_g[:, 8 * TTM:8 * TTM + NT_MAX * D].rearrange("p (n d) -> p n d", d=D)
    RG = [R("arena_g", j) for j in range(NJ)]
    RogT = RG[0:4]; RobT = RG[4:8]
    nt1 = (NT_MAX * D + TTM - 1) // TTM
    Rt1 = RG[8:8 + nt1]
    slots = [sb("slot%d" % i, [P, 4096], BF16) for i in range(NSLOT)]
    S0f = sb("S0f", [P, 16, P]); S0b = sb("S0b", [P, 16, P], BF16)
    vblk = sb("vblk", [P, 4, P], BF16)
    scs = sb("scs", [P, NJ, 2, 32]); cso = sb("cso", [P, NJ, 2, 32])
    cstg = [sb("cstg%d" % i, [32, 512]) for i in range(2)]

    ARENA_COLS = 11776
    arena = sb("arena", [P, ARENA_COLS])

    class Arena:
        def __init__(self):
            self.off = 0
            self.epoch = 0
            self.cur = []
            self.carry = []

        def reset(self):
            toks = set()
            for r in self.cur:
                if r.writer is not None:
                    toks.add(r.writer)
                toks.update(r.readers)
            toks.update(self.carry)
            best = {}
            for tk in toks:
                key = (tk[0], tk[1])
                if key not in best or best[key][2] < tk[2]:
                    best[key] = tk
            self.carry = list(best.values())
            self.cur = []
            self.off = 0
            self.epoch += 1

        def _res(self):
            r = R("arena", self.epoch, len(self.cur))
            r.readers = list(self.carry)
            self.cur.append(r)
            return r

        def f(self, cols):
            a = arena[:, self.off:self.off + cols]
            self.off += cols
            assert self.off <= ARENA_COLS, self.off
            return a, self._res()

        def b(self, cols):
            n = (cols + 1) // 2
            a = arena[:, self.off:self.off + n].bitcast(BF16)[:, 0:cols]
            self.off += n
            assert self.off <= ARENA_COLS, self.off
            return a, self._res()

    AR = Arena()
    Rsmall = {}

    def rsm(i):
        if i not in Rsmall:
            Rsmall[i] = R("small", i)
        return Rsmall[i]

    E = S.emit

    def load(dst_ap, src_ap, writes, eng="sp", reads=()):
        return E(eng, lambda e: e.dma_start(out=dst_ap, in_=src_ap), reads=reads, writes=writes, dma=True)

    def store(dst_ap, src_ap, reads, eng="sp"):
        return E(eng, lambda e: e.dma_start(out=dst_ap, in_=src_ap), reads=reads, writes=(), dma=True, is_out=True)

    def act(out, in_, func, reads, writes, bias=None, scale=None, accum_out=None):
        kw = {}
        if bias is not None:
            kw["bias"] = bias
        if scale is not None:
            kw["scale"] = scale
        if accum_out is not None:
            kw["accum_out"] = accum_out
        return E("act", lambda e: e.activation(out=out, in_=in_, func=func, **kw), reads=reads, writes=writes)

    def tt(out, in0, in1, op, reads, writes, eng="dve"):
        return E(eng, lambda e: e.tensor_tensor(out=out, in0=in0, in1=in1, op=op), reads=reads, writes=writes)

    def ts(out, in0, s1, s2, op0, op1, reads, writes, eng="dve"):
        if op1 is None:
            return E(eng, lambda e: e.tensor_scalar(out=out, in0=in0, scalar1=s1, scalar2=None, op0=op0),
                     reads=reads, writes=writes)
        return E(eng, lambda e: e.tensor_scalar(out=out, in0=in0, scalar1=s1, scalar2=s2, op0=op0, op1=op1),
                 reads=reads, writes=writes)

    def stt(out, in0, scalar, in1, op0, op1, reads, writes, eng="dve"):
        return E(eng, lambda e: e.scalar_tensor_tensor(out=out, in0=in0, scalar=scalar, in1=in1, op0=op0, op1=op1),
                 reads=reads, writes=writes)

    def mm(out, lhsT, rhs, start, stop, reads, writes):
        return E("pe", lambda e: e.matmul(out, lhsT, rhs, start=start, stop=stop), reads=reads, writes=writes)

    def tr(out, in_, ident, reads, writes):
        return E("pe", lambda e: e.transpose(out, in_, ident), reads=reads, writes=writes)

    def cp(out, in_, reads, writes, eng="dve"):
        return E(eng, lambda e: e.tensor_copy(out=out, in_=in_), reads=reads, writes=writes)

    def rsqrt_eps(out, in_, reads, writes):
        act(out, in_, AF.Ln, reads=list(reads) + [R("epsc")], writes=writes, bias=epsc[:, 0:1])
        return act(out, out, AF.Exp, reads=writes, writes=writes, scale=-0.5)

    def memset(ap, val, writes, eng="dve"):
        return E(eng, lambda e: e.memset(ap, val), reads=(), writes=writes)

    def RC(n):
        return R("c", n)

    for t in range(NT):
        load(xt[t], xp[t * P:(t + 1) * P, :], writes=[R("xt", t)])
    for dst, src in [(identb, c_identb), (identf, c_identf), (maskp, c_maskp), (masks, c_masks),
                     (maskf, c_maskf), (scanp, c_scanp), (scans, c_scans), (bmask, c_bmask),
                     (onesf, c_onesf), (nw, v_nw), (hgw, v_hgw), (lbl, v_lbl), (cw, v_cw), (cb, v_cb)]:
        load(dst[:], src, writes=[R("c", dst.name)])
    for dst, src in [(lnw, v_lnw), (lnb, v_lnb), (fnw, v_fnw), (bsp, v_bsp), (bss, v_bss)]:
        load(dst[:], src.partition_broadcast(P), writes=[R("c", dst.name)])
    memset(epsc[:], EPS, writes=[R("epsc")])
    tt(lbt[:], lbl[:, 0, :], lbl[:, 1, :], ALU.subtract, reads=[R("c", "lbl")], writes=[R("c", "lbt")])
    act(lbt[:], lbt[:], AF.Sigmoid, reads=[R("c", "lbt")], writes=[R("c", "lbt")])
    ts(oml[:], lbt[:], -1.0, 1.0, ALU.mult, ALU.add, reads=[R("c", "lbt")], writes=[R("c", "oml")])
    memset(Sf[:], 0.0, writes=[R("Sf")])
    memset(Sb[:], 0.0, writes=[R("Sb")])
    memset(hh_p[:], 0.0, writes=[R("cr", a, j) for a in range(2) for j in range(NJ)])

    spatial_done = [False]

    def prep_spatial():
        if spatial_done[0]:
            return
        spatial_done[0] = True
        AR.reset()
        wstage, Rwstage = AR.f(512)
        wstage3 = wstage.rearrange("p (g t) -> p g t", t=P)
        for dstw, srcw, msk in [(wsp, v_wsp, maskf), (wss, v_wss, masks)]:
            load(wstage3, srcw, writes=[Rwstage])
            tt(dstw[:], wstage3, msk[:].unsqueeze(1).broadcast_to([P, 4, P]), ALU.mult,
               reads=[Rwstage, R("c", msk.name)], writes=[R("c", dstw.name)])

    ring = {"next": 0}

    def wslot():
        i = ring["next"]
        ring["next"] = (i + 1) % NSLOT
        return i

    class WStream:
        def __init__(self):
            self.order = []
            self.emitted = 0
            self.loaded = {}
            self.index = {}

        def add(self, key, kind, src, k, n):
            self.index[key] = len(self.order)
            self.order.append((key, kind, src, k, n))

        def _emit_one(self):
            key, kind, src, k, n = self.order[self.emitted]
            self.emitted += 1
            i = wslot()
            RW = [R("slot", i), R("slot", i, "b")]
            if kind == "plain":
                view = slots[i][:, 0:k * n].rearrange("p (k n) -> p k n", n=n)
                first_reads = [R("xt", t) for t in range(NT)] if self.emitted == 1 else ()
                load(view, src, writes=RW, eng="pool", reads=first_reads)
            else:
                view = slots[i][:, 0:8 * 256].rearrange("p (k n) -> p k n", n=256)
                load(view, src, writes=RW, eng="pool")
            self.loaded[key] = (view, RW)

        def get(self, key, pf):
            n = self.index[key]
            while self.emitted <= min(n + pf, len(self.order) - 1):
                self._emit_one()
            return self.loaded.pop(key)

    WS = WStream()

    w_in_v = w_in.rearrange("(k p) n -> p k n", p=P)
    w_up_v = w_up.rearrange("(k p) n -> p k n", p=P)
    w_o_v = w_o.rearrange("(k p) n -> p k n", p=P)
    w_pg_v = w_pg.rearrange("(k p) n -> p k n", p=P)
    w_a_v = w_a.rearrange("(k p) n -> p k n", p=P)
    w_b_v = w_b.rearrange("(k p) n -> p k n", p=P)
    w_pp_v = w_pp.rearrange("(k p) n -> p k n", p=P)
    w_dn_v = w_dn.rearrange("(k p) n -> p k n", p=P)

    def register_weights(tag, n_dn_pass=1):
        for nm, c0 in [("q", 0), ("f", 512), ("i", 1024), ("og", 1536), ("u", 2048), ("v", 2560),
                       ("ga0", 3072), ("ga1", 3584)]:
            WS.add((tag, nm), "plain", w_in_v[:, :, c0:c0 + 512], 8, 512)
        WS.add((tag, "wa"), "plain", w_a_v, 4, D)
        WS.add((tag, "gb0"), "plain", w_in_v[:, :, 4096:4608], 8, 512)
        WS.add((tag, "gb1"), "plain", w_in_v[:, :, 4608:5120], 8, 512)
        WS.add((tag, "wb"), "plain", w_b_v, 4, D)
        WS.add((tag, "wo0"), "plain", w_o_v[:, :, 0:512], 8, 512)
        WS.add((tag, "wo1"), "plain", w_o_v[:, :, 512:1024], 8, 512)
        for j in range(NJ):
            WS.add((tag, "up", j), "up", w_up_v[:, :, j * 2 * P:(j + 1) * 2 * P], 8, 256)
        for ps in range(n_dn_pass):
            for q in range(6):
                k0, k1 = q * 4, min(NJ, (q + 1) * 4)
                WS.add((tag, "dn", ps, q), "plain", w_dn_v[:, k0:k1, :], k1 - k0, D)
        WS.add((tag, "pg0"), "plain", w_pg_v[:, :, 0:512], 8, 512)
        WS.add((tag, "pg1"), "plain", w_pg_v[:, :, 512:1024], 8, 512)
        WS.add((tag, "pp"), "plain", w_pp_v, 2, D)

    def norm_transpose_all(nt, nwi):
        AR.reset()
        junk, Rjunk = AR.b(D)
        xnb = [AR.b(D) for _ in range(2)]
        for t in range(nt):
            act(junk, xt[t], AF.Square, reads=[R("xt", t)], writes=[Rjunk, rsm(("ss", t))], accum_out=small[:, t:t + 1],
                scale=float(D) ** -0.5)
        rsqrt_eps(small[:, 8:8 + nt], small[:, 0:nt], reads=[rsm(("ss", t)) for t in range(nt)], writes=[rsm("rs")])
        for t in range(nt):
            xb, Rxb = xnb[t % 2]
            ts(xb, xt[t], small[:, 8 + t:9 + t], None, ALU.mult, None, reads=[R("xt", t), rsm("rs")], writes=[Rxb])
            for j in range(8):
                tr(psT[:, j * P:(j + 1) * P], xb[:, j * P:(j + 1) * P], identb[:],
                   reads=[Rxb, RC("identb")], writes=[RpsT])
            tt(xnT[:, :, t * P:(t + 1) * P], psT[:].rearrange("p (j t) -> p j t", t=P),
               nw[:, nwi, :].unsqueeze(2).broadcast_to([P, 8, P]), ALU.mult,
               reads=[RpsT, RC("nw")], writes=[R("xnT", t)])

    def run_supertile(kinds, sp_idx, preloaded=False, next_x=None):
        nt = len(kinds)
        npt = kinds.count("p")
        has_s = "s" in kinds
        last_prompt = (npt > 0) and (sp_idx == NSP - 1)
        tag = ("st", sp_idx)

        class TP:
            pass

        tps = []
        for t_, kd in enumerate(kinds):
            o = TP()
            o.sample = (kd == "s")
            o.C = LS if o.sample else 64
            o.NCH = P // o.C
            o.mid = o.C // 2 - 1
            o.maskt, o.Rmask = (masks, RC("masks")) if o.sample else (maskp, RC("maskp"))
            o.scant, o.Rscan = (scans, RC("scans")) if o.sample else (scanp, RC("scanp"))
            o.wst, o.Rwst = (wss, RC("wss")) if o.sample else (wsp, RC("wsp"))
            o.bst, o.Rbst = (bss, RC("bss")) if o.sample else (bsp, RC("bsp"))
            if o.sample:
                o.xrows, o.prows, o.yrows = xs[0:P, :], ps_[0:P, :], y_s[0:P, :]
            else:
                r0 = sp_idx * TT + t_ * P
                o.xrows, o.prows, o.yrows = xp[r0:r0 + P, :], pp[r0:r0 + P, :], y_p[r0:r0 + P, :]
            tps.append(o)

        if not preloaded:
            norm_transpose_all(nt, 0)

        Wq, RWq = WS.get((tag, "q"), 2)
        Wf, RWf = WS.get((tag, "f"), 2)
        Wi, RWi = WS.get((tag, "i"), 2)
        Wog, RWog = WS.get((tag, "og"), 2)
        AR.reset()
        pA = [dict(qT=AR.f(512), sk=AR.f(512), sog=AR.f(512), vtm=AR.b(512)) for _ in range(2)]
        pC = [dict(qtl=AR.b(512), ktl=AR.b(512), khT=AR.b(512), qin=AR.b(512)) for _ in range(2)]
        (kT, RkT), (lf, Rlf), (bT, RbT), (bm1, Rbm1), (bm2, Rbm2), (ogf, Rogf), (tmpf, Rtmpf) = \
            [AR.f(512) for _ in range(7)]
        Es = [AR.f(512) for _ in range(4)]
        (khat, Rkhat), (scTm, RscTm) = [AR.b(512) for _ in range(2)]

        def proj_pieces(t):
            tcols = slice(t * P, (t + 1) * P)
            Rxn = R("xnT", t)
            for (W, RW, bank_i) in ((Wq, RWq, 0), (Wf, RWf, 1), (Wog, RWog, 2)):
                for h in range(4):
                    for dk in range(8):
                        mm(banks[bank_i][:, h * P:(h + 1) * P], W[:, dk, h * P:(h + 1) * P], xnT[:, dk, tcols],
                           start=(dk == 0), stop=(dk == 7), reads=[RW, Rxn], writes=[RB[bank_i]])
                yield
            for dk in range(8):
                mm(banks[3][:, :], xnT[:, dk, tcols], Wi[:, dk, :],
                   start=(dk == 0), stop=(dk == 7), reads=[RWi, Rxn], writes=[RB[3]])
            yield

        def evac(t):
            pb = pA[t % 2]
            qT, RqT = pb["qT"]
            act(qT, banks[0][:], AF.Sigmoid, reads=[RB[0]], writes=[RqT])
            sk, Rsk = pb["sk"]
            act(sk, banks[1][:], AF.Sigmoid, reads=[RB[1]], writes=[Rsk], scale=-1.0)
            sog, Rsog = pb["sog"]
            act(sog, banks[2][:], AF.Sigmoid, reads=[RB[2]], writes=[Rsog])
            vtm, Rvtm = pb["vtm"]
            act(vtm, banks[3][:], AF.Copy, reads=[RB[3]], writes=[Rvtm])
            tt(qT, banks[0][:], qT, ALU.mult, reads=[RB[0], RqT], writes=[RqT])

        def chain(t):
            o = tps[t]
            C, NCH, mid, scant, Rscan = o.C, o.NCH, o.mid, o.scant, o.Rscan
            pb, pc = pA[t % 2], pC[t % 2]
            (qT, RqT), (sk, Rsk) = pb["qT"], pb["sk"]
            (qtl, Rqtl), (ktl, Rktl), (khT, RkhT), (qin, Rqin) = pc["qtl"], pc["ktl"], pc["khT"], pc["qin"]
            (E0, RE0), (E1, RE1), (E2, RE2), (E3, RE3) = Es
            ebend = small[:, 128 + (t % 2) * 64:128 + (t % 2) * 64 + 4 * NCH]
            Reb = rsm(("eb", t % 2))
            tt(kT.rearrange("p (h t) -> p h t", t=P), sk.rearrange("p (h t) -> p h t", t=P),
               oml[:].unsqueeze(2).broadcast_to([P, 4, P]), ALU.mult, reads=[Rsk, RC("oml")], writes=[RkT])
            act(lf, kT, AF.Ln, reads=[RkT], writes=[Rlf], scale=-1.0, bias=1.0)
            yield
            E("dve", lambda e: e.tensor_tensor_scan(out=bT, data0=scant[:], data1=lf, initial=0.0,
                                                    op0=ALU.mult, op1=ALU.add),
              reads=[Rscan, Rlf], writes=[RbT])
            yield
            b4 = bT.rearrange("p (h n c) -> p h n c", h=4, c=C)
            bm14 = bm1.rearrange("p (h n c) -> p h n c", h=4, c=C)
            bm24 = bm2.rearrange("p (h n c) -> p h n c", h=4, c=C)
            act(E0, bT, AF.Exp, reads=[RbT], writes=[RE0])
            tt(bm14, b4, b4[:, :, :, mid:mid + 1].broadcast_to([P, 4, NCH, C]), ALU.subtract,
               reads=[RbT], writes=[Rbm1])
            yield
            act(ebend.rearrange("p (h n) -> p h n", h=4), b4[:, :, :, C - 1], AF.Exp, reads=[RbT], writes=[Reb])
            tt(bm24, b4[:, :, :, C - 1:C].broadcast_to([P, 4, NCH, C]), b4, ALU.subtract,
               reads=[RbT], writes=[Rbm2])
            yield
            act(E1, bm1, AF.Exp, reads=[Rbm1], writes=[RE1])
            tt(qin, qT, E0, ALU.mult, reads=[RqT, RE0], writes=[Rqin])
            yield
            act(E2, bm1, AF.Exp, reads=[Rbm1], writes=[RE2], scale=-1.0)
            tt(qtl, qT, E1, ALU.mult, reads=[RqT, RE1], writes=[Rqtl])
            yield
            act(E3, bm2, AF.Exp, reads=[Rbm2], writes=[RE3])
            tt(ktl, kT, E2, ALU.mult, reads=[RkT, RE2], writes=[Rktl])
            yield
            tt(khT, kT, E3, ALU.mult, reads=[RkT, RE3], writes=[RkhT])
            yield

        def seq(t):
            o = tps[t]
            sample, C, NCH, maskt, Rmask = o.sample, o.C, o.NCH, o.maskt, o.Rmask
            tcols = slice(t * P, (t + 1) * P)
            pb, pc = pA[t % 2], pC[t % 2]
            (sog, Rsog), (vtm, Rvtm) = pb["sog"], pb["vtm"]
            (qtl, Rqtl), (ktl, Rktl), (khT, RkhT), (qin, Rqin) = pc["qtl"], pc["ktl"], pc["khT"], pc["qin"]
            ebend = small[:, 128 + (t % 2) * 64:128 + (t % 2) * 64 + 4 * NCH]
            Reb = rsm(("eb", t % 2))
            for h in range(4):
                tr(psT[:, h * P:(h + 1) * P], khT[:, h * P:(h + 1) * P], identb[:],
                   reads=[RkhT, RC("identb")], writes=[RpsT])
            for h in range(4):
                hs = slice(h * P, (h + 1) * P)
                mm(banks[4][:, hs], ktl[:, hs], qtl[:, hs], start=True, stop=True,
                   reads=[Rktl, Rqtl], writes=[RB[4]])
            act(khat, psT[:, 0:512], AF.Copy, reads=[RpsT], writes=[Rkhat])
            tt(scTm.rearrange("p (h t) -> p h t", t=P), banks[4][:].rearrange("p (h t) -> p h t", t=P),
               maskt[:].unsqueeze(1).broadcast_to([P, 4, P]), ALU.mult, reads=[RB[4], Rmask], writes=[RscTm])
            yield
            if not sample:
                for c in range(NCH):
                    cs_ = slice(c * C, (c + 1) * C)
                    for h in range(4):
                        osl = slice(h * P + c * C, h * P + (c + 1) * C)
                        mm(banks[5][:, osl], vtm[cs_, h * P:(h + 1) * P], scTm[cs_, osl],
                           start=True, stop=False, reads=[Rvtm, RscTm], writes=[RB[5]])
                        mm(banks[5][:, osl], Sb[:, h, :], qin[:, osl], start=False, stop=True,
                           reads=[R("Sb"), Rqin], writes=[RB[5]])
                    for h in range(4):
                        hs = slice(h * P, (h + 1) * P)
                        mm(banks[6][:, hs], khat[cs_, hs], vtm[cs_, hs], start=True, stop=True,
                           reads=[Rkhat, Rvtm], writes=[RB[6]])
                    yield
                    eb = ebend.rearrange("p (h n) -> p h n", h=4)[:, :, c:c + 1].broadcast_to([P, 4, P])
                    tt(Sf[:], Sf[:], eb, ALU.mult, reads=[R("Sf"), Reb], writes=[R("Sf")])
                    tt(Sf[:], Sf[:], banks[6][:].rearrange("p (h v) -> p h v", v=P), ALU.add,
                       reads=[R("Sf"), RB[6]], writes=[R("Sf")])
                    act(Sb[:], Sf[:], AF.Copy, reads=[R("Sf")], writes=[R("Sb")])
                    yield
            else:
                first = [True]

                def mmo(out, lhsT, rhs, last, reads):
                    st = first[0]
                    first[0] = False
                    E("pe", lambda e: e.matmul(out, lhsT, rhs, start=st, stop=last, skip_group_check=True),
                      reads=reads, writes=[RB[5]])

                for h in range(4):
                    hs = slice(h * P, (h + 1) * P)
                    mmo(banks[5][:, hs], vtm[:, hs], scTm[:, hs], False, [Rvtm, RscTm])
                eb3 = ebend.rearrange("p (h n) -> p h n", h=4)
                def ld_state(g):
                    hb = g % 2
                    load(S0f[:, hb * 8:(hb + 1) * 8, :], st_h[g * 2:(g + 1) * 2].rearrange("j h k v -> k (j h) v"),
                         writes=[R("S0f", hb)])

                ld_state(0)
                for g in range(8):
                    hb = g % 2
                    S0fh = S0f[:, hb * 8:(hb + 1) * 8, :]
                    S0bh = S0b[:, hb * 8:(hb + 1) * 8, :]
                    RS0f, RS0b = R("S0f", hb), R("S0b", hb)
                    if g + 1 < 8:
                        ld_state(g + 1)
                    act(S0bh, S0fh, AF.Copy, reads=[RS0f], writes=[RS0b])
                    for jj in range(2):
                        j = g * 2 + jj
                        for h in range(4):
                            osl = slice(h * P + j * LS, h * P + (j + 1) * LS)
                            mmo(banks[5][:, osl], S0bh[:, jj * 4 + h, :], qin[:, osl],
                                (g == 7 and jj == 1 and h == 3), [RS0b, Rqin])
                    for h in range(4):
                        hs = slice(h * P, (h + 1) * P)
                        vi = (g * 4 + h) % 2
                        vb = vblk[:, vi * 2:(vi + 1) * 2, :]
                        Rvb = R("vblk", vi)
                        bk = 6 if vi == 0 else 4
                        tt(vb, vtm[:, hs].unsqueeze(1).broadcast_to([P, 2, P]),
                           bmask[:, g * 2:(g + 1) * 2].unsqueeze(2).broadcast_to([P, 2, P]), ALU.mult,
                           reads=[Rvtm, RC("bmask")], writes=[Rvb])
                        mm(banks[bk][:, 0:2 * P], khat[:, hs], vb.rearrange("p j v -> p (j v)"),
                           start=True, stop=True, reads=[Rkhat, Rvb], writes=[RB[bk]])
                        for jj in range(2):
                            j = g * 2 + jj
                            stt(S0fh[:, jj * 4 + h, :], S0fh[:, jj * 4 + h, :], eb3[:, h, j:j + 1],
                                banks[bk][:, jj * P:(jj + 1) * P], ALU.mult, ALU.add,
                                reads=[RS0f, Reb, RB[bk]], writes=[RS0f])
                    store(o_hs[g * 2:(g + 1) * 2].rearrange("j h k v -> k (j h) v"), S0fh, reads=[RS0f])
                    yield
            tt(ogf, banks[5][:], sog, ALU.mult, reads=[RB[5], Rsog], writes=[Rogf])
            act(tmpf, ogf, AF.Square, reads=[Rogf], writes=[Rtmpf], scale=512.0 ** -0.5)
            yield
            for h in range(4):
                mm(banks[4][:, 0:1], tmpf[:, h * P:(h + 1) * P], onesf[:, 0:1], start=(h == 0), stop=(h == 3),
                   reads=[Rtmpf, RC("onesf")], writes=[RB[4]])
            cp(small[:, 16 + t:17 + t], banks[4][:, 0:1], reads=[RB[4]], writes=[rsm(("ssa", t))])
            tt(ogT[:, :, tcols], ogf.rearrange("p (h t) -> p h t", t=P),
               hgw[:].unsqueeze(2).broadcast_to([P, 4, P]), ALU.mult, reads=[Rogf, RC("hgw")],
               writes=[R("ogT", t)] + RogT)
            yield

        def interleave(*gens):
            gens = [g for g in gens if g is not None]
            while gens:
                for g in list(gens):
                    try:
                        next(g)
                    except StopIteration:
                        gens.remove(g)

        interleave(proj_pieces(0))
        if pending_late[0] is not None:
            pending_late[0]()
            pending_late[0] = None
        evac(0)
        for k in range(nt + 1):
            interleave(chain(k) if k < nt else None,
                       seq(k - 1) if k >= 1 else None,
                       proj_pieces(k + 1) if k + 1 < nt else None)
            if k + 1 < nt:
                evac(k + 1)
        rsqrt_eps(rstda[:, 0:nt], small[:, 16:16 + nt], reads=[rsm(("ssa", t)) for t in range(nt)],
                  writes=[R("rstda")])
        if last_prompt:
            store(o_hp.rearrange("h k v -> k h v"), Sf[:], reads=[R("Sf")])

        prep_spatial()
        Wu, RWu = WS.get((tag, "u"), 3)
        Wv, RWv = WS.get((tag, "v"), 3)
        AR.reset()
        uTall, RuTall = AR.f(4 * TTM)
        uT4 = uTall.rearrange("p (g t) -> p g t", t=TTM)
        RuTt = [AR._res() for _ in range(nt)]
        vgs = [AR.f(512) for _ in range(nt)]
        vns = [AR.f(512) for _ in range(2)]
        tmps = [AR.f(512) for _ in range(2)]
        vvs = [AR.b(512) for _ in range(2)]
        sgs = [AR.f(512) for _ in range(2)]
        (tmpf, Rtmpf) = tmps[0]
        ugroups = []
        if npt > 0:
            ugroups.append((slice(0, npt * P), list(range(npt))))
        if has_s:
            ugroups.append((slice(npt * P, (npt + 1) * P), [npt]))
        for (ucols, utiles) in ugroups:
            nU = ucols.stop - ucols.start
            for g in range(4):
                for dk in range(8):
                    mm(banks[g][:, 0:nU], Wu[:, dk, g * P:(g + 1) * P], xnT[:, dk, ucols],
                       start=(dk == 0), stop=(dk == 7), reads=[RWu] + [R("xnT", q) for q in utiles],
                       writes=[RB[g]])
                act(uT4[:, g, ucols], banks[g][:, 0:nU], AF.Gelu, reads=[RB[g]],
                    writes=[RuTt[q] for q in utiles])
        for t in range(nt):
            tcols = slice(t * P, (t + 1) * P)
            Rxn = R("xnT", t)
            (vg, Rvg) = vgs[t]
            bv = 4 + (t % 2)
            for dk in range(8):
                mm(banks[bv][:, :], xnT[:, dk, tcols], Wv[:, dk, :], start=(dk == 0), stop=(dk == 7),
                   reads=[RWv, Rxn], writes=[RB[bv]])
            act(vg, banks[bv][:], AF.Gelu, reads=[RB[bv]], writes=[Rvg, rsm(("s1", t))],
                accum_out=small[:, 40 + t:41 + t])
            act(tmpf, vg, AF.Square, reads=[Rvg], writes=[Rtmpf, rsm(("s2", t))], accum_out=small[:, 48 + t:49 + t])
        s1 = small[:, 40:40 + nt]; s2 = small[:, 48:48 + nt]
        mean = small[:, 56:56 + nt]; msq = small[:, 64:64 + nt]; var = small[:, 72:72 + nt]
        rs = small[:, 80:80 + nt]; nmr = small[:, 88:88 + nt]
        ts(mean, s1, 1.0 / 512, None, ALU.mult, None, reads=[rsm(("s1", t)) for t in range(nt)], writes=[rsm("mean")])
        tt(msq, mean, mean, ALU.mult, reads=[rsm("mean")], writes=[rsm("msq")])
        stt(var, s2, 1.0 / 512, msq, ALU.mult, ALU.subtract,
            reads=[rsm(("s2", t)) for t in range(nt)] + [rsm("msq")], writes=[rsm("var")])
        rsqrt_eps(rs, var, reads=[rsm("var")], writes=[rsm("lrs")])
        stt(nmr, mean, -1.0, rs, ALU.mult, ALU.mult, reads=[rsm("mean"), rsm("lrs")], writes=[rsm("nmr")])

        Wga0, RWga0 = WS.get((tag, "ga0"), 2)
        Wga1, RWga1 = WS.get((tag, "ga1"), 2)
        Wa, RWa = WS.get((tag, "wa"), 2)
        Wga = [(Wga0, RWga0), (Wga1, RWga1)]
        unit = [0]

        def m1_pe(t, half):
            tcols = slice(t * P, (t + 1) * P)
            u = unit[0]
            bg, ba = (u % 2) * 2, (u % 2) * 2 + 1
            W, RW = Wga[half]
            for dk in range(8):
                mm(bank_ap[bg], xnT[:, dk, tcols], W[:, dk, :], start=(dk == 0), stop=(dk == 7),
                   reads=[RW, R("xnT", t)], writes=[RBall[bg]])
            for h in range(4):
                mm(bank_ap[ba], ogT[:, h, tcols], Wa[:, h, half * 512:(half + 1) * 512],
                   start=(h == 0), stop=(h == 3), reads=[RWa, R("ogT", t)] + RogT, writes=[RBall[ba]])
            unit[0] += 1
            return bg, ba, u

        def m1_ew(t, half, bg, ba, u):
            hsl = slice(half * 512, (half + 1) * 512)
            sga, Rsga = sgs[u % 2]
            act(sga, bank_ap[bg], AF.Sigmoid, reads=[RBall[bg]], writes=[Rsga])
            stt(t1[:, t, hsl], bank_ap[ba], rstda[:, t:t + 1], sga, ALU.mult, ALU.mult,
                reads=[RBall[ba], R("rstda"), Rsga], writes=[R("t1", t)] + Rt1)

        for t in range(nt):
            tcols = slice(t * P, (t + 1) * P)
            (vg, Rvg) = vgs[t]
            RuT = RuTt[t]
            (vn, Rvn) = vns[t % 2]
            (tmpf, Rtmpf) = tmps[t % 2]
            (vv, Rvv) = vvs[t % 2]
            sample, wst, Rwst, bst, Rbst = tps[t].sample, tps[t].wst, tps[t].Rwst, tps[t].bst, tps[t].Rbst
            u0 = m1_pe(t, 0)
            act(vn, vg, AF.Identity, reads=[Rvg, rsm("lrs"), rsm("nmr")], writes=[Rvn],
                scale=rs[:, t:t + 1], bias=nmr[:, t:t + 1])
            tt(vn, vn, lnw[:], ALU.mult, reads=[Rvn, RC("lnw")], writes=[Rvn])
            tt(vn, vn, lnb[:], ALU.add, reads=[Rvn, RC("lnb")], writes=[Rvn])
            act(vv, vn, AF.Copy, reads=[Rvn], writes=[Rvv])
            if sample:
                store(o_gv, vn, reads=[Rvn])
            m1_ew(t, 0, *u0)
            u1 = m1_pe(t, 1)
            for g in range(4):
                gs = slice(g * P, (g + 1) * P)
                mm(banks[6][:, gs], vv[:, gs], wst[:, g, :], start=True, stop=True,
                   reads=[Rvv, Rwst], writes=[RB[6]])
            tt(tmpf, banks[6][:], bst[:], ALU.add, reads=[RB[6], Rbst], writes=[Rtmpf])
            tt(obT[:, :, tcols], tmpf.rearrange("p (g t) -> p g t", t=P), uT4[:, :, tcols],
               ALU.mult, reads=[Rtmpf, RuT], writes=[R("obT", t)] + RobT)
            m1_ew(t, 1, *u1)

        Wgb0, RWgb0 = WS.get((tag, "gb0"), 2)
        Wgb1, RWgb1 = WS.get((tag, "gb1"), 2)
        Wb, RWb = WS.get((tag, "wb"), 2)
        AR.reset()
        sgs2 = [AR.f(D) for _ in range(2)]
        t2s = [AR.f(D) for _ in range(2)]
        mbs = [AR.b(D) for _ in range(2)]
        Rsgb2 = [[AR._res() for _ in range(2)] for _ in range(2)]
        Rt22 = [[AR._res() for _ in range(2)] for _ in range(2)]

        def m2_front(t):
            tcols = slice(t * P, (t + 1) * P)
            Rxn = R("xnT", t)
            sgb, Rsgb = sgs2[t % 2]
            t2, Rt2 = t2s[t % 2]
            mb, Rmb = mbs[t % 2]
            for half, (W, RW) in enumerate([(Wgb0, RWgb0), (Wgb1, RWgb1)]):
                bg = (t % 2) * 2 + half
                for dk in range(8):
                    mm(bank_ap[bg], xnT[:, dk, tcols], W[:, dk, :], start=(dk == 0), stop=(dk == 7),
                       reads=[RW, Rxn], writes=[RBall[bg]])
            for half in range(2):
                ba = 4 + half
                for h in range(4):
                    mm(bank_ap[ba], obT[:, h, tcols], Wb[:, h, half * 512:(half + 1) * 512],
                       start=(h == 0), stop=(h == 3), reads=[RWb, R("obT", t)] + RobT, writes=[RBall[ba]])
            for half in range(2):
                bg = (t % 2) * 2 + half
                hsl = slice(half * 512, (half + 1) * 512)
                act(sgb[:, hsl], bank_ap[bg], AF.Sigmoid, reads=[RBall[bg]], writes=[Rsgb2[t % 2][half]])
            for half in range(2):
                ba = 4 + half
                hsl = slice(half * 512, (half + 1) * 512)
                tt(t2[:, hsl], bank_ap[ba], sgb[:, hsl], ALU.mult, reads=[RBall[ba], Rsgb2[t % 2][half]],
                   writes=[Rt22[t % 2][half]])
            tt(mb, t1[:, t, :], t2, ALU.add, reads=[R("t1", t)] + Rt22[t % 2] + Rt1, writes=[Rmb])

        def m2_back(t):
            tcols = slice(t * P, (t + 1) * P)
            mb, Rmb = mbs[t % 2]
            for j in range(8):
                tr(psT[:, j * P:(j + 1) * P], mb[:, j * P:(j + 1) * P], identb[:],
                   reads=[Rmb, RC("identb")], writes=[RpsT])
            act(mT[:, :, tcols], psT[:].rearrange("p (j t) -> p j t", t=P), AF.Copy,
                reads=[RpsT], writes=[R("xnT", t)])

        for t in range(nt + 1):
            if t < nt:
                m2_front(t)
            if t >= 1:
                m2_back(t - 1)

        Wo0, RWo0 = WS.get((tag, "wo0"), 3)
        Wo1, RWo1 = WS.get((tag, "wo1"), 3)
        AR.reset()
        junk, Rjunk = AR.b(D)
        xnb = [AR.b(D) for _ in range(2)]

        def m3_front(t):
            tcols = slice(t * P, (t + 1) * P)
            for half, (W, RW) in enumerate([(Wo0, RWo0), (Wo1, RWo1)]):
                bo = (t % 2) * 2 + half
                hsl = slice(half * 512, (half + 1) * 512)
                for dk in range(8):
                    mm(bank_ap[bo], mT[:, dk, tcols], W[:, dk, :], start=(dk == 0), stop=(dk == 7),
                       reads=[RW, R("xnT", t)], writes=[RBall[bo]])
                tt(xt[t][:, hsl], xt[t][:, hsl], bank_ap[bo], ALU.add,
                   reads=[R("xt", t), RBall[bo]], writes=[R("xt", t)])

        def ffn_norm(t):
            act(junk, xt[t], AF.Square, reads=[R("xt", t)], writes=[Rjunk, rsm(("ss", t))], accum_out=small[:, t:t + 1],
                scale=float(D) ** -0.5)
            rsqrt_eps(small[:, 8 + t:9 + t], small[:, t:t + 1], reads=[rsm(("ss", t))], writes=[rsm(("rs1", t))])
            xb, Rxb = xnb[t % 2]
            ts(xb, xt[t], small[:, 8 + t:9 + t], None, ALU.mult, None, reads=[R("xt", t), rsm(("rs1", t))],
               writes=[Rxb])
            for j in range(8):
                tr(psT[:, j * P:(j + 1) * P], xb[:, j * P:(j + 1) * P], identb[:],
                   reads=[Rxb, RC("identb")], writes=[RpsT])
            tt(xnT[:, :, t * P:(t + 1) * P], psT[:].rearrange("p (j t) -> p j t", t=P),
               nw[:, 1, :].unsqueeze(2).broadcast_to([P, 8, P]), ALU.mult,
               reads=[RpsT, RC("nw")], writes=[R("xnT", t)])

        for t in range(nt + 1):
            if t < nt:
                m3_front(t)
            if t >= 1:
                ffn_norm(t - 1)

        if has_s:
            for g in range(11):
                stg = cstg[g % 2]; Rstg = R("cstg", g % 2)
                load(stg[:], st_c[:, g * 512:(g + 1) * 512], writes=[Rstg])
                for jj in range(4):
                    tr(banks[g % 2][:, jj * 32:(jj + 1) * 32], stg[0:32, jj * P:(jj + 1) * P], identf[0:32, 0:32],
                       reads=[Rstg, RC("identf")], writes=[RB[g % 2]])
                for jj in range(4):
                    c = g * 4 + jj
                    act(scs[:, c % NJ, c // NJ, :], banks[g % 2][:, jj * 32:(jj + 1) * 32], AF.Copy,
                        reads=[RB[g % 2]], writes=[R("scs")])
        RS0fa = [R("S0f", 0), R("S0f", 1)]
        RS0ba = [R("S0b", 0), R("S0b", 1)]
        pT_ = []
        for t in range(nt):
            pT_.append((S0b[:, 2 * t:2 * t + 2, :].rearrange("p a b -> p (a b)"), RS0ba))

        def p_load(t):
            ptile = S0f[:, 2 * t:2 * t + 2, :].rearrange("p a b -> p (a b)")
            pbf = S0f[:, 10 + t, :].bitcast(BF16)
            load(ptile, tps[t].prows, writes=RS0fa)
            act(pbf, ptile, AF.Copy, reads=RS0fa, writes=RS0fa)

        def p_tr(t):
            pbf = S0f[:, 10 + t, :].bitcast(BF16)
            pTb = pT_[t][0]
            for kk in range(2):
                tr(psT[:, kk * P:(kk + 1) * P], pbf[:, kk * P:(kk + 1) * P], identb[:],
                   reads=RS0fa + [RC("identb")], writes=[RpsT])
            act(pTb, psT[:, 0:256], AF.Copy, reads=[RpsT], writes=RS0ba)

        groups = []
        if npt > 0:
            groups.append((slice(0, npt * P), list(range(npt)), False, 1, npt * P))
        if has_s:
            groups.append((slice(npt * P, (npt + 1) * P), [npt], True, NSEQ_S, LS))
        AR.reset()
        HS_ = {False: 512, True: P}
        npar_k = {False: (2 if has_s else 3), True: 2}
        yall_k = {False: [AR.f(1024)[0] for _ in range(npar_k[False])], True: [AR.f(2 * P)[0] for _ in range(2)]}
        Ryh_k = {kk_: [[AR._res() for _ in range(2)] for _ in range(npar_k[kk_])] for kk_ in (False, True)}
        Ryb_k = {kk_: [[AR._res() for _ in range(2)] for _ in range(npar_k[kk_])] for kk_ in (False, True)}
        bank0_k = {False: 0, True: 4}
        hhs = [AR.f(2 * NSEQ_S * 4)[0] for _ in range(2)] if has_s else None
        Rhhs = [[AR._res() for _ in range(2)] for _ in range(2)]
        it = 0
        pending_tail = [None]
        for j in range(NJ):
            Wup, RWup2 = WS.get((tag, "up", j), 4)
            if 1 <= j < 1 + nt:
                p_load(j - 1)
            if 8 <= j < 8 + nt:
                p_tr(j - 8)
            for (gcols, gtiles, sample, NS_, L_) in groups:
                TG = NS_ * L_
                Rxg = [R("xnT", q) for q in gtiles]
                par = j % npar_k[sample]
                yall, Ryh, Ryb, HS, b0 = yall_k[sample], Ryh_k[sample], Ryb_k[sample], HS_[sample], bank0_k[sample]
                y4 = yall[par].rearrange("p (a c) -> p a c", a=2)[:, :, 0:NS_ * L_].rearrange(
                    "p a (s l) -> p a s l", l=L_)
                if sample:
                    hh4 = hhs[par].rearrange("p (a s r) -> p a s r", a=2, r=4)
                    Rhh = Rhhs[par]
                    cp(hh4[:, :, :, 0:2], scs[:, j, :, :].rearrange("p a (s r) -> p a s r", r=2),
                       reads=[R("scs")], writes=Rhh)
                else:
                    hh4 = hh_p[:, (sp_idx % 2) * NJ + j, :, :].unsqueeze(2)
                    Rhh = [R("cr", sp_idx % 2, j)] * 2
                pvs = []
                for half in range(2):
                    bk = b0 + par * 2 + half
                    for dk in range(8):
                        mm(bank_ap[bk][:, 0:TG], Wup[:, dk, half * P:(half + 1) * P], xnT[:, dk, gcols],
                           start=(dk == 0), stop=(dk == 7), reads=[RWup2[half]] + Rxg, writes=[RBall[bk]])
                    pvs.append(bank_ap[bk][:, 0:NS_ * L_].rearrange("p (s l) -> p s l", l=L_))
                for half in range(2):
                    bk = b0 + par * 2 + half
                    act(y4[:, half], pvs[half], AF.Identity, reads=[RBall[bk], RC("cw"), RC("cb")],
                        writes=[Ryh[par][half], Ryb[par][half]], scale=cw[:, j, half, 2:3], bias=cb[:, j, half:half + 1])
                    act(hh4[:, half, :, 2:4], pvs[half][:, :, 0:2], AF.Copy, reads=[RBall[bk]], writes=[Rhh[half]])
                    if sample:
                        act(cso[:, j, half, :].rearrange("p (s r) -> p s r", r=2), pvs[half][:, :, L_ - 2:L_],
                            AF.Copy, reads=[RBall[bk]], writes=[R("cso")])
                    else:
                        act(hh_p[:, ((sp_idx + 1) % 2) * NJ + j, half, 0:2].unsqueeze(1), pvs[half][:, :, L_ - 2:L_],
                            AF.Copy, reads=[RBall[bk]], writes=[R("cr", (sp_idx + 1) % 2, j)])
                for half in range(2):
                    bk = b0 + par * 2 + half
                    for k in (1, 0):
                        stt(y4[:, half, :, 2:L_], pvs[half][:, :, k:L_ - 2 + k], cw[:, j, half, k:k + 1],
                            y4[:, half, :, 2:L_], ALU.mult, ALU.add,
                            reads=[RBall[bk], RC("cw"), Ryb[par][half]], writes=[Ryb[par][half]])
                        stt(y4[:, half, :, 0:2], hh4[:, half, :, k:k + 2], cw[:, j, half, k:k + 1],
                            y4[:, half, :, 0:2], ALU.mult, ALU.add,
                            reads=[Rhh[half], RC("cw"), Ryh[par][half]], writes=[Ryh[par][half]])

                def tail(j=j, par=par, gcols=gcols, TG=TG, yall=yall, Ryh=Ryh, Ryb=Ryb, HS=HS):
                    ya = yall[par][:, 0:TG]
                    yb = yall[par][:, HS:HS + TG]
                    act(ya, ya, AF.Gelu, reads=[Ryh[par][0], Ryb[par][0]], writes=[Ryh[par][0], Ryb[par][0]])
                    tt(gT[:, j, gcols], ya, yb, ALU.mult,
                       reads=[Ryh[par][0], Ryb[par][0], Ryh[par][1], Ryb[par][1]], writes=[RG[j]], eng="pool")

                if pending_tail[0] is not None:
                    pending_tail[0]()
                pending_tail[0] = tail
                it += 1
        if pending_tail[0] is not None:
            pending_tail[0]()
            pending_tail[0] = None
        def conv_out_groups(bank_ids):
            outs_ = []
            if has_s:
                outs_.append(True)
            if last_prompt:
                outs_.append(False)
            gidx = 0
            for sample in outs_:
                NR = 32 if sample else 2
                odst = o_cs if sample else o_cp
                for g in range(11):
                    bk = bank_ids[gidx % 2]
                    stg = cstg[gidx % 2]; Rstg = R("cstg", gidx % 2)
                    gidx += 1
                    for jj in range(4):
                        c = g * 4 + jj
                        if sample:
                            srcap = cso[:, c % NJ, c // NJ, :]
                            Rsrc = [R("cso")]
                        else:
                            srcap = hh_p[:, (NSP % 2) * NJ + c % NJ, c // NJ, 0:2]
                            Rsrc = [R("cr", NSP % 2, c % NJ)]
                        tr(bank_ap[bk][0:NR, jj * P:(jj + 1) * P], srcap, identf[:],
                           reads=Rsrc + [RC("identf")], writes=[RBall[bk]])
                    act(stg[0:NR, :], bank_ap[bk][0:NR, :], AF.Copy, reads=[RBall[bk]], writes=[Rstg])
                    store(odst[:, g * 512:(g + 1) * 512], stg[0:NR, :], reads=[Rstg])
                    yield

        if nt <= 4:
            passes = [list(range(nt))]
            cgen = conv_out_groups([0, 1])
            for _ in cgen:
                pass
            cgen = None
        else:
            passes = [list(range(3)), list(range(3, nt))]
            cgen = conv_out_groups([6, 7])
        n_pass = len(passes)
        dn_loaded = {}
        for ps, ptiles in enumerate(passes):
            for q in range(6):
                kcs = list(range(q * 4, min(NJ, (q + 1) * 4)))
                if ps == 0:
                    dn_loaded[q] = WS.get((tag, "dn", 0, q), 4 if n_pass == 1 else 5 - q)
                Wd, RWd = dn_loaded[q]
                for ci, kc in enumerate(kcs):
                    for t in ptiles:
                        tcols = slice(t * P, (t + 1) * P)
                        for half in range(2):
                            bk = (t - ptiles[0]) * 2 + half
                            mm(bank_ap[bk], gT[:, kc, tcols], Wd[:, ci, half * 512:(half + 1) * 512],
                               start=(kc == 0), stop=(kc == NJ - 1), reads=[RWd, RG[kc]], writes=[RBall[bk]])
                    if cgen is not None and ps == 0:
                        try:
                            next(cgen)
                        except StopIteration:
                            cgen = None
            for t in ptiles:
                for half in range(2):
                    bk = (t - ptiles[0]) * 2 + half
                    hsl = slice(half * 512, (half + 1) * 512)
                    tt(xt[t][:, hsl], xt[t][:, hsl], bank_ap[bk], ALU.add, reads=[R("xt", t), RBall[bk]],
                       writes=[R("xt", t)])
        if cgen is not None:
            for _ in cgen:
                pass

        Wg0, RWg0 = WS.get((tag, "pg0"), 3)
        Wg1, RWg1 = WS.get((tag, "pg1"), 3)
        Wp, RWp = WS.get((tag, "pp"), 3)
        AR.reset()
        xnbP = [AR.b(D) for _ in range(2)]
        for t in range(nt):
            xb, Rxb = xnbP[t % 2]
            cp(xb, xt[t], reads=[R("xt", t)], writes=[Rxb])
            for j in range(8):
                tr(psT[:, j * P:(j + 1) * P], xb[:, j * P:(j + 1) * P], identb[:],
                   reads=[Rxb, RC("identb")], writes=[RpsT])
            tt(xnT[:, :, t * P:(t + 1) * P], psT[:].rearrange("p (j t) -> p j t", t=P),
               nw[:, 2, :].unsqueeze(2).broadcast_to([P, 8, P]), ALU.mult,
               reads=[RpsT, RC("nw")], writes=[R("xnT", t)])
        sg_ = [AR.f(D) for _ in range(2)]
        yb_ = [AR.f(D) for _ in range(2)]
        junk, Rjunk = AR.b(D)
        xnb2 = [AR.b(D) for _ in range(2)]
        for t in range(nt):
            act(junk, xt[t], AF.Square, reads=[R("xt", t)], writes=[Rjunk, rsm(("pss", t))],
                accum_out=small[:, 24 + t:25 + t], scale=float(D) ** -0.5)
        rsqrt_eps(small[:, 32:32 + nt], small[:, 24:24 + nt], reads=[rsm(("pss", t)) for t in range(nt)],
                  writes=[rsm("prs")])

        def p4_front(t):
            tcols = slice(t * P, (t + 1) * P)
            pTb, RpT = pT_[t]
            for half, (W, RW) in enumerate([(Wg0, RWg0), (Wg1, RWg1)]):
                hsl = slice(half * 512, (half + 1) * 512)
                bg = (t % 2) * 4 + half
                bp = (t % 2) * 4 + 2 + half
                for dk in range(8):
                    mm(bank_ap[bg], xnT[:, dk, tcols], W[:, dk, :], start=(dk == 0), stop=(dk == 7),
                       reads=[RW, R("xnT", t)], writes=[RBall[bg]])
                for kk in range(2):
                    mm(bank_ap[bp], pTb[:, kk * P:(kk + 1) * P], Wp[:, kk, hsl], start=(kk == 0), stop=(kk == 1),
                       reads=[RWp, RpT], writes=[RBall[bp]])

        hf_ = [AR.f(D) for _ in range(nt)]
        n_next = len(next_x) if next_x is not None else 0

        Rsgh = [[AR._res() for _ in range(2)] for _ in range(2)]
        Rtmph = [[AR._res() for _ in range(2)] for _ in range(2)]

        def p4_back_a(t):
            sg, _ = sg_[t % 2]
            tmp, _ = yb_[t % 2]
            hf, Rhf = hf_[t]
            for half in range(2):
                hsl = slice(half * 512, (half + 1) * 512)
                bg = (t % 2) * 4 + half
                act(sg[:, hsl], bank_ap[bg], AF.Sigmoid, reads=[RBall[bg], rsm("prs")], writes=[Rsgh[t % 2][half]],
                    scale=small[:, 32 + t:33 + t])
            for half in range(2):
                hsl = slice(half * 512, (half + 1) * 512)
                bp = (t % 2) * 4 + 2 + half
                tt(tmp[:, hsl], bank_ap[bp], sg[:, hsl], ALU.mult, reads=[RBall[bp], Rsgh[t % 2][half]],
                   writes=[Rtmph[t % 2][half]])
            tt(hf, xt[t], tmp, ALU.add, reads=[R("xt", t)] + Rtmph[t % 2], writes=[Rhf])
            if t < n_next:
                load(xt[t], next_x[t], writes=[R("xt", t)])

        def p4_back_b(t):
            hf, Rhf = hf_[t]
            act(junk, hf, AF.Square, reads=[Rhf], writes=[Rjunk, rsm(("fs", t))], accum_out=small[:, 96 + t:97 + t],
                scale=float(D) ** -0.5)

        def next_sq(t, junk=junk, Rjunk=Rjunk):
            act(junk, xt[t], AF.Square, reads=[R("xt", t)], writes=[Rjunk, rsm(("nss", t))],
                accum_out=small[:, 112 + t:113 + t], scale=float(D) ** -0.5)

        def next_tr(t, xbuf=None):
            xb, Rxb = xbuf if xbuf is not None else xnb2[t % 2]
            ts(xb, xt[t], small[:, 120 + t:121 + t], None, ALU.mult, None, reads=[R("xt", t), rsm("nrs")],
               writes=[Rxb])
            for j in range(8):
                tr(psT[:, j * P:(j + 1) * P], xb[:, j * P:(j + 1) * P], identb[:],
                   reads=[Rxb, RC("identb")], writes=[RpsT])
            tt(xnT[:, :, t * P:(t + 1) * P], psT[:].rearrange("p (j t) -> p j t", t=P),
               nw[:, 0, :].unsqueeze(2).broadcast_to([P, 8, P]), ALU.mult,
               reads=[RpsT, RC("nw")], writes=[R("xnT", t)])

        def fin(t):
            hf, Rhf = hf_[t]
            stt(hf, hf, small[:, 104 + t:105 + t], fnw[:], ALU.mult, ALU.mult,
                reads=[Rhf, rsm("frs"), RC("fnw")], writes=[Rhf])
            store(tps[t].yrows, hf, reads=[Rhf])

        for t in range(nt, n_next):
            load(xt[t], next_x[t], writes=[R("xt", t)])
        p4_front(0)
        for t in range(nt):
            if t + 1 < nt:
                p4_front(t + 1)
            p4_back_a(t)
            if 0 <= t - 1 < n_next:
                next_sq(t - 1)
            p4_back_b(t)
        late = nt - 1 if nt - 1 < n_next else None
        for t in range(nt, n_next):
            next_sq(t)
        lo = min(max(nt - 1, 0), n_next)
        if lo > 0:
            rsqrt_eps(small[:, 120:120 + lo], small[:, 112:112 + lo],
                      reads=[rsm(("nss", t)) for t in range(lo)], writes=[rsm("nrs")])
        if n_next > nt:
            rsqrt_eps(small[:, 120 + nt:120 + n_next], small[:, 112 + nt:112 + n_next],
                      reads=[rsm(("nss", t)) for t in range(nt, n_next)], writes=[rsm("nrs")])
        for t in range(n_next):
            if t != late:
                next_tr(t)
        if late is not None:
            def late_norm(t=late):
                jk = S0f[:, 4:8, :].rearrange("p a b -> p (a b)").bitcast(BF16)
                xbl = S0f[:, 0:4, :].rearrange("p a b -> p (a b)").bitcast(BF16)
                next_sq(t, jk, R("S0f", 0))
                rsqrt_eps(small[:, 120 + t:121 + t], small[:, 112 + t:113 + t], reads=[rsm(("nss", t))],
                          writes=[rsm("nrs")])
                next_tr(t, (xbl, R("S0f", 0)))
            pending_late[0] = late_norm
        rsqrt_eps(small[:, 104:104 + nt], small[:, 96:96 + nt], reads=[rsm(("fs", t)) for t in range(nt)],
                  writes=[rsm("frs")])
        for t in range(nt):
            fin(t)

    NSP = SEQ // TT
    pending_late = [None]
    kinds_of = [["p"] * NT for _ in range(NSP)]
    kinds_of[-1] = kinds_of[-1] + ["s"]
    for sp_idx in range(NSP):
        register_weights(("st", sp_idx), n_dn_pass=1)
    for sp_idx in range(NSP):
        nx = None
        if sp_idx + 1 < NSP:
            nx = [xp[(sp_idx + 1) * TT + t * P:(sp_idx + 1) * TT + (t + 1) * P, :] for t in range(NT)]
            if sp_idx + 1 == NSP - 1:
                nx.append(xs[0:P, :])
        run_supertile(kinds_of[sp_idx], sp_idx, preloaded=(sp_idx > 0), next_x=nx)

    S.finish()
    S.replay()


_NC_CACHE = {}


def _consts():
    i = np.arange(P)
    c = {}
    c["c_identb"] = np.eye(P, dtype=np.float32).astype(ml_dtypes.bfloat16)
    c["c_identf"] = np.eye(P, dtype=np.float32)
    s, t = np.meshgrid(i, i, indexing="ij")
    c["c_maskp"] = ((s // 64 == t // 64) & (s <= t)).astype(np.float32)
    c["c_masks"] = ((s // 8 == t // 8) & (s <= t)).astype(np.float32)
    c["c_maskf"] = (s <= t).astype(np.float32)
    tcol = np.arange(512) % 128
    c["c_scanp"] = np.broadcast_to((tcol % 64 != 0).astype(np.float32), (P, 512)).copy()
    c["c_scans"] = np.broadcast_to((tcol % 8 != 0).astype(np.float32), (P, 512)).copy()
    c["c_bmask"] = (i[:, None] // 8 == np.arange(16)[None, :]).astype(np.float32)
    c["c_onesf"] = np.ones((P, 1), np.float32)
    return c


def _fm(v, nchunk):
    return np.ascontiguousarray(np.asarray(v, np.float32).reshape(nchunk, P).T)


def kernel(x_prompt, x_sample, p_prompt, p_sample, state_hgrn, state_conv, lb_logits,
           norm_mix_w, w_in, hgrn_norm_w, ln_v_w, ln_v_b, w_spatial, b_spatial, w_a_out,
           w_b_out, w_o, norm_ffn_w, w_up, conv_w, conv_b, w_down, norm_ple_w, w_ple_gate,
           w_ple_proj, final_norm_w, _debug=None):
    f32 = np.float32
    A = lambda a: np.ascontiguousarray(np.asarray(a, dtype=f32))
    if "nc" not in _NC_CACHE or _debug is not None:
        nc = build_nc(_debug)
        if _debug is None:
            _NC_CACHE["nc"] = nc
    else:
        nc = _NC_CACHE["nc"]
    shared = _consts()
    shared["w_in"] = A(w_in[0]); shared["w_a"] = A(w_a_out[0]); shared["w_b"] = A(w_b_out[0])
    shared["w_o"] = A(w_o[0]); shared["w_up"] = np.ascontiguousarray(
        np.asarray(w_up[0], dtype=f32).reshape(D, 2, NJ, P).transpose(0, 2, 1, 3).reshape(D, 2 * DFF)); shared["w_dn"] = A(w_down[0])
    shared["w_pg"] = A(w_ple_gate[0]); shared["w_pp"] = A(w_ple_proj[0])
    shared["v_nw"] = np.ascontiguousarray(np.stack(
        [_fm(norm_mix_w[0], 8), _fm(norm_ffn_w[0], 8), _fm(norm_ple_w[0], 8)], axis=1))
    shared["v_hgw"] = _fm(hgrn_norm_w[0], 4)
    lbl = np.asarray(lb_logits, f32)
    shared["v_lbl"] = np.ascontiguousarray(np.stack([_fm(lbl[0], 4), _fm(lbl[1], 4)], axis=1))
    shared["v_lnw"] = A(ln_v_w[0]); shared["v_lnb"] = A(ln_v_b[0]); shared["v_fnw"] = A(final_norm_w)
    bs = np.asarray(b_spatial[0], f32)
    shared["v_bsp"] = np.ascontiguousarray(bs.reshape(512))
    shared["v_bss"] = np.ascontiguousarray(np.tile(bs[:, :LS], (1, NSEQ_S)).reshape(512))
    cwn = np.asarray(conv_w[0], f32)
    shared["v_cw"] = np.ascontiguousarray(cwn.T.reshape(2, NJ, P, 3).transpose(2, 1, 0, 3))
    shared["v_cb"] = np.ascontiguousarray(np.asarray(conv_b[0], f32).reshape(2, NJ, P).transpose(2, 1, 0))
    ws = np.asarray(w_spatial[0], f32)
    shared["v_wsp"] = np.ascontiguousarray(ws.transpose(2, 0, 1))
    wss = np.zeros((P, 4, P), f32)
    sub = ws[:, :LS, :LS].transpose(2, 0, 1)
    for j in range(NSEQ_S):
        wss[j * LS:(j + 1) * LS, :, j * LS:(j + 1) * LS] = sub
    shared["v_wss"] = wss

    xpn = np.asarray(x_prompt, f32); xsn = np.asarray(x_sample, f32)
    ppn = np.asarray(p_prompt, f32)[0]; psn = np.asarray(p_sample, f32)[0]
    sth = np.asarray(state_hgrn, f32)[0]; stc = np.asarray(state_conv, f32)[0]
    in_maps = []
    for c in range(8):
        m = dict(shared)
        m["xp"] = np.ascontiguousarray(xpn[c])
        m["xs"] = np.ascontiguousarray(xsn[c * 16:(c + 1) * 16].reshape(P, D))
        m["pp"] = np.ascontiguousarray(ppn[c])
        m["ps"] = np.ascontiguousarray(psn[c * 16:(c + 1) * 16].reshape(P, 256))
        m["st_h"] = np.ascontiguousarray(sth[c * 16:(c + 1) * 16])
        m["st_c"] = np.ascontiguousarray(stc[c * 16:(c + 1) * 16].reshape(32, 2 * DFF))
        in_maps.append(m)
    res = run_bass_kernel_spmd(nc, in_maps, core_ids=list(range(8)))
    rs = res.results
    y_prompt = np.stack([r["y_p"] for r in rs], 0).astype(f32)
    y_sample = np.concatenate([r["y_s"].reshape(16, LS, D) for r in rs], 0).astype(f32)
    hgrn_p = np.stack([r["o_hp"] for r in rs], 0)[None].astype(f32)
    hgrn_s = np.concatenate([r["o_hs"] for r in rs], 0)[None].astype(f32)
    conv_p = np.stack([r["o_cp"] for r in rs], 0)[None].astype(f32)
    conv_s = np.concatenate([r["o_cs"].reshape(16, 2, 2 * DFF) for r in rs], 0)[None].astype(f32)
    gv_s = np.concatenate([r["o_gv"].reshape(16, LS, 512) for r in rs], 0)[None].astype(f32)
    if _debug is not None:
        return rs
    return (y_prompt, y_sample, hgrn_p, hgrn_s, conv_p, conv_s, gv_s)
```

```python
import contextlib
import numpy as np
import ml_dtypes
import concourse.bass as bass
import concourse.mybir as mybir
from concourse.bass_utils import run_bass_kernel_spmd

F32 = mybir.dt.float32
BF16 = mybir.dt.bfloat16
AF = mybir.ActivationFunctionType
ALU = mybir.AluOpType

P = 128
D = 1024
NIN = 5120
DFF = 2816
NJ = 22
SEQ = 2048
NSEQ_S = 16
LS = 8
EPS = 1e-6
NT_P = 4
NSLOT = 6
NDMA = 24
STRICT_SAME_ENGINE = True


class Res:
    __slots__ = ("name", "writer", "readers")

    def __init__(self, name):
        self.name = name
        self.writer = None
        self.readers = []


class Sched:
    def __init__(self, nc, stack):
        self.nc = nc
        self.names = ["pe", "act", "dve", "pool", "sp"]
        self.lists = {e: [] for e in self.names}
        self.sems = {e: stack.enter_context(nc.semaphore("s_" + e)) for e in self.names}
        self.cnt = {e: 0 for e in self.names}
        self.waited = {e: {} for e in self.names}
        self.dsems = [stack.enter_context(nc.semaphore("d%d" % i)) for i in range(NDMA)]
        self.dcnt = [0] * NDMA
        self.drr = {"sp": 0, "pool": 0}
        self.res = {}
        self.out_tokens = []

    def R(self, *key):
        r = self.res.get(key)
        if r is None:
            r = Res(key)
            self.res[key] = r
        return r

    def _semof(self, tok):
        if tok[0] == "e":
            return ("e", tok[1]), self.sems[tok[1]]
        return ("d", tok[1]), self.dsems[tok[1]]

    @staticmethod
    def _flat(xs):
        out = []
        for x in xs:
            if isinstance(x, (list, tuple)):
                out.extend(Sched._flat(x))
            else:
                out.append(x)
        return out

    def emit(self, eng, fn, reads=(), writes=(), dma=False, is_out=False):
        reads = self._flat(reads)
        writes = self._flat(writes)
        deps = set()
        for r in reads:
            if r.writer is not None:
                deps.add(r.writer)
        for w in writes:
            if w.writer is not None and (STRICT_SAME_ENGINE or not (w.writer[0] == "e" and w.writer[1] == eng and not dma)):
                deps.add(w.writer)
            for t in w.readers:
                if STRICT_SAME_ENGINE or not (t[0] == "e" and t[1] == eng and not dma):
                    deps.add(t)
        need = {}
        for tok in deps:
            if tok[0] == "e" and tok[1] == eng and not dma and eng == "pe":
                continue
            key, sem = self._semof(tok)
            if need.get(key, (None, 0))[1] < tok[2]:
                need[key] = (sem, tok[2])
        waits = []
        wd = self.waited[eng]
        for key, (sem, val) in need.items():
            if wd.get(key, 0) < val:
                waits.append((sem, val))
                wd[key] = val
        if dma:
            if eng == "pool":
                s = 16 + self.drr["pool"]
                self.drr["pool"] = (self.drr["pool"] + 1) % (NDMA - 16)
            else:
                s = self.drr["sp"]
                self.drr["sp"] = (self.drr["sp"] + 1) % 16
            prior = self.dcnt[s] * 16
            key = ("d", s)
            if prior > 0 and wd.get(key, 0) < prior:
                waits.append((self.dsems[s], prior))
                wd[key] = prior
            self.dcnt[s] += 1
            tok = ("d", s, self.dcnt[s] * 16)
            inc = (self.dsems[s], 16)
            if is_out:
                self.out_tokens.append(tok)
        else:
            self.cnt[eng] += 1
            tok = ("e", eng, self.cnt[eng])
            inc = (self.sems[eng], 1)
        self.lists[eng].append((waits, fn, inc))
        for r in reads:
            r.readers.append(tok)
        for w in writes:
            w.writer = tok
            w.readers = []
        return tok

    def finish(self):
        waits = []
        for s in range(NDMA):
            if self.dcnt[s] > 0:
                waits.append((self.dsems[s], self.dcnt[s] * 16))
        self.lists["sp"].append((waits, None, None))

    def replay(self):
        nc = self.nc
        lists = self.lists

        needed = {}
        for name in self.names:
            for waits, fn, inc in lists[name]:
                for sem, val in waits:
                    needed.setdefault(id(sem), set()).add(val)

        def run(e, items):
            pending = 0
            count = 0
            for waits, fn, inc in items:
                for sem, val in waits:
                    e.wait_ge(sem, val)
                if fn is not None:
                    ins = fn(e)
                    if inc[1] == 16:
                        ins.then_inc(inc[0], 16)
                        continue
                    count += 1
                    pending += 1
                    if count in needed.get(id(inc[0]), ()):
                        ins.then_inc(inc[0], pending)
                        pending = 0

        with nc.Block() as block:
            @block.tensor
            def _(e):
                run(e, lists["pe"])

            @block.scalar
            def _(e):
                run(e, lists["act"])

            @block.vector
            def _(e):
                run(e, lists["dve"])

            @block.gpsimd
            def _(e):
                run(e, lists["pool"])

            @block.sync
            def _(e):
                run(e, lists["sp"])


def build_nc(debug=None):
    nc = bass.Bass("TRN2", target_bir_lowering=False)
    stack = contextlib.ExitStack()
    with stack:
        _build(nc, stack, debug)
    return nc


def _build(nc, stack, debug):
    S = Sched(nc, stack)
    R = S.R

    def din(name, shape, dt=F32):
        return nc.dram_tensor(name, list(shape), dt, kind="ExternalInput").ap()

    def dout(name, shape, dt=F32):
        return nc.dram_tensor(name, list(shape), dt, kind="ExternalOutput").ap()

    xp = din("xp", [SEQ, D]); xs = din("xs", [P, D])
    pp = din("pp", [SEQ, 256]); ps_ = din("ps", [P, 256])
    st_h = din("st_h", [NSEQ_S, 4, P, P]); st_c = din("st_c", [32, 2 * DFF])
    w_in = din("w_in", [D, NIN]); w_a = din("w_a", [512, D]); w_b = din("w_b", [512, D])
    w_o = din("w_o", [D, D]); w_up = din("w_up", [D, 2 * DFF]); w_dn = din("w_dn", [DFF, D])
    w_pg = din("w_pg", [D, D]); w_pp = din("w_pp", [256, D])
    c_identb = din("c_identb", [P, P], BF16); c_identf = din("c_identf", [P, P])
    c_maskp = din("c_maskp", [P, P]); c_masks = din("c_masks", [P, P]); c_maskf = din("c_maskf", [P, P])
    c_scanp = din("c_scanp", [P, 512]); c_scans = din("c_scans", [P, 512])
    c_bmask = din("c_bmask", [P, 16]); c_onesf = din("c_onesf", [P, 1])
    v_nw = din("v_nw", [P, 3, 8])
    v_hgw = din("v_hgw", [P, 4]); v_lbl = din("v_lbl", [P, 2, 4])
    v_lnw = din("v_lnw", [512]); v_lnb = din("v_lnb", [512]); v_fnw = din("v_fnw", [D])
    v_bsp = din("v_bsp", [512]); v_bss = din("v_bss", [512])
    v_cw = din("v_cw", [P, NJ, 2, 3]); v_cb = din("v_cb", [P, NJ, 2])
    v_wsp = din("v_wsp", [P, 4, P]); v_wss = din("v_wss", [P, 4, P])

    y_p = dout("y_p", [SEQ, D]); y_s = dout("y_s", [P, D])
    o_hp = dout("o_hp", [4, P, P]); o_hs = dout("o_hs", [NSEQ_S, 4, P, P])
    o_cp = dout("o_cp", [2, 2 * DFF]); o_cs = dout("o_cs", [32, 2 * DFF])
    o_gv = dout("o_gv", [P, 512])
    dbg = {}
    if debug:
        for name, shape in debug.items():
            dbg[name] = dout("dbg_" + name, shape)

    def sb(name, shape, dt=F32):
        return stack.enter_context(nc.sbuf_tensor(name, list(shape), dt))

    NT = NT_P
    TT = NT * P
    NT_MAX = NT + 1
    TTM = NT_MAX * P
    banks = [stack.enter_context(nc.psum_tensor("bank%d" % i, [P, 512], F32)) for i in range(7)]
    psT = stack.enter_context(nc.psum_tensor("psT", [P, 1024], BF16))
    RB = [R("bank", i) for i in range(7)]
    RpsT = R("psT")
    bank_ap = [b[:, :] for b in banks] + [psT[:].bitcast(F32)]
    RBall = RB + [RpsT]

    identb = sb("identb", [P, P], BF16); identf = sb("identf", [P, P])
    maskp = sb("maskp", [P, P]); masks = sb("masks", [P, P]); maskf = sb("maskf", [P, P])
    scanp = sb("scanp", [P, 512]); scans = sb("scans", [P, 512])
    bmask = sb("bmask", [P, 16]); onesf = sb("onesf", [P, 1])
    nw = sb("nw", [P, 3, 8]); hgw = sb("hgw", [P, 4]); lbl = sb("lbl", [P, 2, 4])
    oml = sb("oml", [P, 4]); lbt = sb("lbt", [P, 4])
    lnw = sb("lnw", [P, 512]); lnb = sb("lnb", [P, 512]); fnw = sb("fnw", [P, D])
    bsp = sb("bsp", [P, 512]); bss = sb("bss", [P, 512])
    cw = sb("cw", [P, NJ, 2, 3]); cb = sb("cb", [P, NJ, 2])
    wsp = sb("wsp", [P, 4, P], BF16); wss = sb("wss", [P, 4, P], BF16)
    Sf = sb("Sf", [P, 4, P]); Sb = sb("Sb", [P, 4, P], BF16)
    hh_p = sb("hh_p", [P, 2 * NJ, 2, 4])
    small = sb("small", [P, 256])
    epsc = sb("epsc", [P, 1])
    rstda = sb("rstda", [P, 8])

    xt_all = sb("xt_all", [P, NT_MAX, D])
    xt = [xt_all[:, t, :] for t in range(NT_MAX)]
    xnT = sb("xnT", [P, 8, TTM], BF16)
    mT = xnT
    arena_g = sb("arena_g", [P, NJ * TTM], BF16)
    gT = arena_g[:, :].rearrange("p (j t) -> p j t", t=TTM)
    ogT = arena_g[:, 0:4 * TTM].rearrange("p (h t) -> p h t", t=TTM)
    obT = arena_g[:, 4 * TTM:8 * TTM].rearrange("p (h t) -> p h t", t=TTM)
    t1 = arena_g[:, 8 * TTM:8 * TTM + NT_MAX * D].rearrange("p (n d) -> p n d", d=D)
    RG = [R("arena_g", j) for j in range(NJ)]
    RogT = RG[0:4]; RobT = RG[4:8]
    nt1 = (NT_MAX * D + TTM - 1) // TTM
    Rt1 = RG[8:8 + nt1]
    slots = [sb("slot%d" % i, [P, 4096], BF16) for i in range(NSLOT)]
    S0f = sb("S0f", [P, 16, P]); S0b = sb("S0b", [P, 16, P], BF16)
    vblk = sb("vblk", [P, 4, P], BF16)
    scs = sb("scs", [P, NJ, 2, 32]); cso = sb("cso", [P, NJ, 2, 32])
    cstg = [sb("cstg%d" % i, [32, 512]) for i in range(2)]

    ARENA_COLS = 11776
    arena = sb("arena", [P, ARENA_COLS])

    class Arena:
        def __init__(self):
            self.off = 0
            self.epoch = 0
            self.cur = []
            self.carry = []

        def reset(self):
            toks = set()
            for r in self.cur:
                if r.writer is not None:
                    toks.add(r.writer)
                toks.update(r.readers)
            toks.update(self.carry)
            best = {}
            for tk in toks:
                key = (tk[0], tk[1])
                if key not in best or best[key][2] < tk[2]:
                    best[key] = tk
            self.carry = list(best.values())
            self.cur = []
            self.off = 0
            self.epoch += 1

        def _res(self):
            r = R("arena", self.epoch, len(self.cur))
            r.readers = list(self.carry)
            self.cur.append(r)
            return r

        def f(self, cols):
            a = arena[:, self.off:self.off + cols]
            self.off += cols
            assert self.off <= ARENA_COLS, self.off
            return a, self._res()

        def b(self, cols):
            n = (cols + 1) // 2
            a = arena[:, self.off:self.off + n].bitcast(BF16)[:, 0:cols]
            self.off += n
            assert self.off <= ARENA_COLS, self.off
            return a, self._res()

    AR = Arena()
    Rsmall = {}

    def rsm(i):
        if i not in Rsmall:
            Rsmall[i] = R("small", i)
        return Rsmall[i]

    E = S.emit

    def load(dst_ap, src_ap, writes, eng="sp", reads=()):
        return E(eng, lambda e: e.dma_start(out=dst_ap, in_=src_ap), reads=reads, writes=writes, dma=True)

    def store(dst_ap, src_ap, reads, eng="sp"):
        return E(eng, lambda e: e.dma_start(out=dst_ap, in_=src_ap), reads=reads, writes=(), dma=True, is_out=True)

    def act(out, in_, func, reads, writes, bias=None, scale=None, accum_out=None):
        kw = {}
        if bias is not None:
            kw["bias"] = bias
        if scale is not None:
            kw["scale"] = scale
        if accum_out is not None:
            kw["accum_out"] = accum_out
        return E("act", lambda e: e.activation(out=out, in_=in_, func=func, **kw), reads=reads, writes=writes)

    def tt(out, in0, in1, op, reads, writes, eng="dve"):
        return E(eng, lambda e: e.tensor_tensor(out=out, in0=in0, in1=in1, op=op), reads=reads, writes=writes)

    def ts(out, in0, s1, s2, op0, op1, reads, writes, eng="dve"):
        if op1 is None:
            return E(eng, lambda e: e.tensor_scalar(out=out, in0=in0, scalar1=s1, scalar2=None, op0=op0),
                     reads=reads, writes=writes)
        return E(eng, lambda e: e.tensor_scalar(out=out, in0=in0, scalar1=s1, scalar2=s2, op0=op0, op1=op1),
                 reads=reads, writes=writes)

    def stt(out, in0, scalar, in1, op0, op1, reads, writes, eng="dve"):
        return E(eng, lambda e: e.scalar_tensor_tensor(out=out, in0=in0, scalar=scalar, in1=in1, op0=op0, op1=op1),
                 reads=reads, writes=writes)

    def mm(out, lhsT, rhs, start, stop, reads, writes):
        return E("pe", lambda e: e.matmul(out, lhsT, rhs, start=start, stop=stop), reads=reads, writes=writes)

    def tr(out, in_, ident, reads, writes):
        return E("pe", lambda e: e.transpose(out, in_, ident), reads=reads, writes=writes)

    def cp(out, in_, reads, writes, eng="dve"):
        return E(eng, lambda e: e.tensor_copy(out=out, in_=in_), reads=reads, writes=writes)

    def rsqrt_eps(out, in_, reads, writes):
        act(out, in_, AF.Ln, reads=list(reads) + [R("epsc")], writes=writes, bias=epsc[:, 0:1])
        return act(out, out, AF.Exp, reads=writes, writes=writes, scale=-0.5)

    def memset(ap, val, writes, eng="dve"):
        return E(eng, lambda e: e.memset(ap, val), reads=(), writes=writes)

    def RC(n):
        return R("c", n)

    for t in range(NT):
        load(xt[t], xp[t * P:(t + 1) * P, :], writes=[R("xt", t)])
    for dst, src in [(identb, c_identb), (identf, c_identf), (maskp, c_maskp), (masks, c_masks),
                     (maskf, c_maskf), (scanp, c_scanp), (scans, c_scans), (bmask, c_bmask),
                     (onesf, c_onesf), (nw, v_nw), (hgw, v_hgw), (lbl, v_lbl), (cw, v_cw), (cb, v_cb)]:
        load(dst[:], src, writes=[R("c", dst.name)])
    for dst, src in [(lnw, v_lnw), (lnb, v_lnb), (fnw, v_fnw), (bsp, v_bsp), (bss, v_bss)]:
        load(dst[:], src.partition_broadcast(P), writes=[R("c", dst.name)])
    memset(epsc[:], EPS, writes=[R("epsc")])
    tt(lbt[:], lbl[:, 0, :], lbl[:, 1, :], ALU.subtract, reads=[R("c", "lbl")], writes=[R("c", "lbt")])
    act(lbt[:], lbt[:], AF.Sigmoid, reads=[R("c", "lbt")], writes=[R("c", "lbt")])
    ts(oml[:], lbt[:], -1.0, 1.0, ALU.mult, ALU.add, reads=[R("c", "lbt")], writes=[R("c", "oml")])
    memset(Sf[:], 0.0, writes=[R("Sf")])
    memset(Sb[:], 0.0, writes=[R("Sb")])
    memset(hh_p[:], 0.0, writes=[R("cr", a, j) for a in range(2) for j in range(NJ)])

    spatial_done = [False]

    def prep_spatial():
        if spatial_done[0]:
            return
        spatial_done[0] = True
        AR.reset()
        wstage, Rwstage = AR.f(512)
        wstage3 = wstage.rearrange("p (g t) -> p g t", t=P)
        for dstw, srcw, msk in [(wsp, v_wsp, maskf), (wss, v_wss, masks)]:
            load(wstage3, srcw, writes=[Rwstage])
            tt(dstw[:], wstage3, msk[:].unsqueeze(1).broadcast_to([P, 4, P]), ALU.mult,
               reads=[Rwstage, R("c", msk.name)], writes=[R("c", dstw.name)])

    ring = {"next": 0}

    def wslot():
        i = ring["next"]
        ring["next"] = (i + 1) % NSLOT
        return i

    class WStream:
        def __init__(self):
            self.order = []
            self.emitted = 0
            self.loaded = {}
            self.index = {}

        def add(self, key, kind, src, k, n):
            self.index[key] = len(self.order)
            self.order.append((key, kind, src, k, n))

        def _emit_one(self):
            key, kind, src, k, n = self.order[self.emitted]
            self.emitted += 1
            i = wslot()
            RW = [R("slot", i), R("slot", i, "b")]
            if kind == "plain":
                view = slots[i][:, 0:k * n].rearrange("p (k n) -> p k n", n=n)
                first_reads = [R("xt", t) for t in range(NT)] if self.emitted == 1 else ()
                load(view, src, writes=RW, eng="pool", reads=first_reads)
            else:
                view = slots[i][:, 0:8 * 256].rearrange("p (k n) -> p k n", n=256)
                load(view, src, writes=RW, eng="pool")
            self.loaded[key] = (view, RW)

        def get(self, key, pf):
            n = self.index[key]
            while self.emitted <= min(n + pf, len(self.order) - 1):
                self._emit_one()
            return self.loaded.pop(key)

    WS = WStream()

    w_in_v = w_in.rearrange("(k p) n -> p k n", p=P)
    w_up_v = w_up.rearrange("(k p) n -> p k n", p=P)
    w_o_v = w_o.rearrange("(k p) n -> p k n", p=P)
    w_pg_v = w_pg.rearrange("(k p) n -> p k n", p=P)
    w_a_v = w_a.rearrange("(k p) n -> p k n", p=P)
    w_b_v = w_b.rearrange("(k p) n -> p k n", p=P)
    w_pp_v = w_pp.rearrange("(k p) n -> p k n", p=P)
    w_dn_v = w_dn.rearrange("(k p) n -> p k n", p=P)

    def register_weights(tag, n_dn_pass=1):
        for nm, c0 in [("q", 0), ("f", 512), ("i", 1024), ("og", 1536), ("u", 2048), ("v", 2560),
                       ("ga0", 3072), ("ga1", 3584)]:
            WS.add((tag, nm), "plain", w_in_v[:, :, c0:c0 + 512], 8, 512)
        WS.add((tag, "wa"), "plain", w_a_v, 4, D)
        WS.add((tag, "gb0"), "plain", w_in_v[:, :, 4096:4608], 8, 512)
        WS.add((tag, "gb1"), "plain", w_in_v[:, :, 4608:5120], 8, 512)
        WS.add((tag, "wb"), "plain", w_b_v, 4, D)
        WS.add((tag, "wo0"), "plain", w_o_v[:, :, 0:512], 8, 512)
        WS.add((tag, "wo1"), "plain", w_o_v[:, :, 512:1024], 8, 512)
        for j in range(NJ):
            WS.add((tag, "up", j), "up", w_up_v[:, :, j * 2 * P:(j + 1) * 2 * P], 8, 256)
        for ps in range(n_dn_pass):
            for q in range(6):
                k0, k1 = q * 4, min(NJ, (q + 1) * 4)
                WS.add((tag, "dn", ps, q), "plain", w_dn_v[:, k0:k1, :], k1 - k0, D)
        WS.add((tag, "pg0"), "plain", w_pg_v[:, :, 0:512], 8, 512)
        WS.add((tag, "pg1"), "plain", w_pg_v[:, :, 512:1024], 8, 512)
        WS.add((tag, "pp"), "plain", w_pp_v, 2, D)

    def norm_transpose_all(nt, nwi):
        AR.reset()
        junk, Rjunk = AR.b(D)
        xnb = [AR.b(D) for _ in range(2)]
        for t in range(nt):
            act(junk, xt[t], AF.Square, reads=[R("xt", t)], writes=[Rjunk, rsm(("ss", t))], accum_out=small[:, t:t + 1],
                scale=float(D) ** -0.5)
        rsqrt_eps(small[:, 8:8 + nt], small[:, 0:nt], reads=[rsm(("ss", t)) for t in range(nt)], writes=[rsm("rs")])
        for t in range(nt):
            xb, Rxb = xnb[t % 2]
            ts(xb, xt[t], small[:, 8 + t:9 + t], None, ALU.mult, None, reads=[R("xt", t), rsm("rs")], writes=[Rxb])
            for j in range(8):
                tr(psT[:, j * P:(j + 1) * P], xb[:, j * P:(j + 1) * P], identb[:],
                   reads=[Rxb, RC("identb")], writes=[RpsT])
            tt(xnT[:, :, t * P:(t + 1) * P], psT[:].rearrange("p (j t) -> p j t", t=P),
               nw[:, nwi, :].unsqueeze(2).broadcast_to([P, 8, P]), ALU.mult,
               reads=[RpsT, RC("nw")], writes=[R("xnT", t)])

    def run_supertile(kinds, sp_idx, preloaded=False, next_x=None):
        nt = len(kinds)
        npt = kinds.count("p")
        has_s = "s" in kinds
        last_prompt = (npt > 0) and (sp_idx == NSP - 1)
        tag = ("st", sp_idx)

        class TP:
            pass

        tps = []
        for t_, kd in enumerate(kinds):
            o = TP()
            o.sample = (kd == "s")
            o.C = LS if o.sample else 64
            o.NCH = P // o.C
            o.mid = o.C // 2 - 1
            o.maskt, o.Rmask = (masks, RC("masks")) if o.sample else (maskp, RC("maskp"))
            o.scant, o.Rscan = (scans, RC("scans")) if o.sample else (scanp, RC("scanp"))
            o.wst, o.Rwst = (wss, RC("wss")) if o.sample else (wsp, RC("wsp"))
            o.bst, o.Rbst = (bss, RC("bss")) if o.sample else (bsp, RC("bsp"))
            if o.sample:
                o.xrows, o.prows, o.yrows = xs[0:P, :], ps_[0:P, :], y_s[0:P, :]
            else:
                r0 = sp_idx * TT + t_ * P
                o.xrows, o.prows, o.yrows = xp[r0:r0 + P, :], pp[r0:r0 + P, :], y_p[r0:r0 + P, :]
            tps.append(o)

        if not preloaded:
            norm_transpose_all(nt, 0)

        Wq, RWq = WS.get((tag, "q"), 2)
        Wf, RWf = WS.get((tag, "f"), 2)
        Wi, RWi = WS.get((tag, "i"), 2)
        Wog, RWog = WS.get((tag, "og"), 2)
        AR.reset()
        pA = [dict(qT=AR.f(512), sk=AR.f(512), sog=AR.f(512), vtm=AR.b(512)) for _ in range(2)]
        pC = [dict(qtl=AR.b(512), ktl=AR.b(512), khT=AR.b(512), qin=AR.b(512)) for _ in range(2)]
        (kT, RkT), (lf, Rlf), (bT, RbT), (bm1, Rbm1), (bm2, Rbm2), (ogf, Rogf), (tmpf, Rtmpf) = \
            [AR.f(512) for _ in range(7)]
        Es = [AR.f(512) for _ in range(4)]
        (khat, Rkhat), (scTm, RscTm) = [AR.b(512) for _ in range(2)]

        def proj_pieces(t):
            tcols = slice(t * P, (t + 1) * P)
            Rxn = R("xnT", t)
            for (W, RW, bank_i) in ((Wq, RWq, 0), (Wf, RWf, 1), (Wog, RWog, 2)):
                for h in range(4):
                    for dk in range(8):
                        mm(banks[bank_i][:, h * P:(h + 1) * P], W[:, dk, h * P:(h + 1) * P], xnT[:, dk, tcols],
                           start=(dk == 0), stop=(dk == 7), reads=[RW, Rxn], writes=[RB[bank_i]])
                yield
            for dk in range(8):
                mm(banks[3][:, :], xnT[:, dk, tcols], Wi[:, dk, :],
                   start=(dk == 0), stop=(dk == 7), reads=[RWi, Rxn], writes=[RB[3]])
            yield

        def evac(t):
            pb = pA[t % 2]
            qT, RqT = pb["qT"]
            act(qT, banks[0][:], AF.Sigmoid, reads=[RB[0]], writes=[RqT])
            sk, Rsk = pb["sk"]
            act(sk, banks[1][:], AF.Sigmoid, reads=[RB[1]], writes=[Rsk], scale=-1.0)
            sog, Rsog = pb["sog"]
            act(sog, banks[2][:], AF.Sigmoid, reads=[RB[2]], writes=[Rsog])
            vtm, Rvtm = pb["vtm"]
            act(vtm, banks[3][:], AF.Copy, reads=[RB[3]], writes=[Rvtm])
            tt(qT, banks[0][:], qT, ALU.mult, reads=[RB[0], RqT], writes=[RqT])

        def chain(t):
            o = tps[t]
            C, NCH, mid, scant, Rscan = o.C, o.NCH, o.mid, o.scant, o.Rscan
            pb, pc = pA[t % 2], pC[t % 2]
            (qT, RqT), (sk, Rsk) = pb["qT"], pb["sk"]
            (qtl, Rqtl), (ktl, Rktl), (khT, RkhT), (qin, Rqin) = pc["qtl"], pc["ktl"], pc["khT"], pc["qin"]
            (E0, RE0), (E1, RE1), (E2, RE2), (E3, RE3) = Es
            ebend = small[:, 128 + (t % 2) * 64:128 + (t % 2) * 64 + 4 * NCH]
            Reb = rsm(("eb", t % 2))
            tt(kT.rearrange("p (h t) -> p h t", t=P), sk.rearrange("p (h t) -> p h t", t=P),
               oml[:].unsqueeze(2).broadcast_to([P, 4, P]), ALU.mult, reads=[Rsk, RC("oml")], writes=[RkT])
            act(lf, kT, AF.Ln, reads=[RkT], writes=[Rlf], scale=-1.0, bias=1.0)
            yield
            E("dve", lambda e: e.tensor_tensor_scan(out=bT, data0=scant[:], data1=lf, initial=0.0,
                                                    op0=ALU.mult, op1=ALU.add),
              reads=[Rscan, Rlf], writes=[RbT])
            yield
            b4 = bT.rearrange("p (h n c) -> p h n c", h=4, c=C)
            bm14 = bm1.rearrange("p (h n c) -> p h n c", h=4, c=C)
            bm24 = bm2.rearrange("p (h n c) -> p h n c", h=4, c=C)
            act(E0, bT, AF.Exp, reads=[RbT], writes=[RE0])
            tt(bm14, b4, b4[:, :, :, mid:mid + 1].broadcast_to([P, 4, NCH, C]), ALU.subtract,
               reads=[RbT], writes=[Rbm1])
            yield
            act(ebend.rearrange("p (h n) -> p h n", h=4), b4[:, :, :, C - 1], AF.Exp, reads=[RbT], writes=[Reb])
            tt(bm24, b4[:, :, :, C - 1:C].broadcast_to([P, 4, NCH, C]), b4, ALU.subtract,
               reads=[RbT], writes=[Rbm2])
            yield
            act(E1, bm1, AF.Exp, reads=[Rbm1], writes=[RE1])
            tt(qin, qT, E0, ALU.mult, reads=[RqT, RE0], writes=[Rqin])
            yield
            act(E2, bm1, AF.Exp, reads=[Rbm1], writes=[RE2], scale=-1.0)
            tt(qtl, qT, E1, ALU.mult, reads=[RqT, RE1], writes=[Rqtl])
            yield
            act(E3, bm2, AF.Exp, reads=[Rbm2], writes=[RE3])
            tt(ktl, kT, E2, ALU.mult, reads=[RkT, RE2], writes=[Rktl])
            yield
            tt(khT, kT, E3, ALU.mult, reads=[RkT, RE3], writes=[RkhT])
            yield

        def seq(t):
            o = tps[t]
            sample, C, NCH, maskt, Rmask = o.sample, o.C, o.NCH, o.maskt, o.Rmask
            tcols = slice(t * P, (t + 1) * P)
            pb, pc = pA[t % 2], pC[t % 2]
            (sog, Rsog), (vtm, Rvtm) = pb["sog"], pb["vtm"]
            (qtl, Rqtl), (ktl, Rktl), (khT, RkhT), (qin, Rqin) = pc["qtl"], pc["ktl"], pc["khT"], pc["qin"]
            ebend = small[:, 128 + (t % 2) * 64:128 + (t % 2) * 64 + 4 * NCH]
            Reb = rsm(("eb", t % 2))
            for h in range(4):
                tr(psT[:, h * P:(h + 1) * P], khT[:, h * P:(h + 1) * P], identb[:],
                   reads=[RkhT, RC("identb")], writes=[RpsT])
            for h in range(4):
                hs = slice(h * P, (h + 1) * P)
                mm(banks[4][:, hs], ktl[:, hs], qtl[:, hs], start=True, stop=True,
                   reads=[Rktl, Rqtl], writes=[RB[4]])
            act(khat, psT[:, 0:512], AF.Copy, reads=[RpsT], writes=[Rkhat])
            tt(scTm.rearrange("p (h t) -> p h t", t=P), banks[4][:].rearrange("p (h t) -> p h t", t=P),
               maskt[:].unsqueeze(1).broadcast_to([P, 4, P]), ALU.mult, reads=[RB[4], Rmask], writes=[RscTm])
            yield
            if not sample:
                for c in range(NCH):
                    cs_ = slice(c * C, (c + 1) * C)
                    for h in range(4):
                        osl = slice(h * P + c * C, h * P + (c + 1) * C)
                        mm(banks[5][:, osl], vtm[cs_, h * P:(h + 1) * P], scTm[cs_, osl],
                           start=True, stop=False, reads=[Rvtm, RscTm], writes=[RB[5]])
                        mm(banks[5][:, osl], Sb[:, h, :], qin[:, osl], start=False, stop=True,
                           reads=[R("Sb"), Rqin], writes=[RB[5]])
                    for h in range(4):
                        hs = slice(h * P, (h + 1) * P)
                        mm(banks[6][:, hs], khat[cs_, hs], vtm[cs_, hs], start=True, stop=True,
                           reads=[Rkhat, Rvtm], writes=[RB[6]])
                    yield
                    eb = ebend.rearrange("p (h n) -> p h n", h=4)[:, :, c:c + 1].broadcast_to([P, 4, P])
                    tt(Sf[:], Sf[:], eb, ALU.mult, reads=[R("Sf"), Reb], writes=[R("Sf")])
                    tt(Sf[:], Sf[:], banks[6][:].rearrange("p (h v) -> p h v", v=P), ALU.add,
                       reads=[R("Sf"), RB[6]], writes=[R("Sf")])
                    act(Sb[:], Sf[:], AF.Copy, reads=[R("Sf")], writes=[R("Sb")])
                    yield
            else:
                first = [True]

                def mmo(out, lhsT, rhs, last, reads):
                    st = first[0]
                    first[0] = False
                    E("pe", lambda e: e.matmul(out, lhsT, rhs, start=st, stop=last, skip_group_check=True),
                      reads=reads, writes=[RB[5]])

                for h in range(4):
                    hs = slice(h * P, (h + 1) * P)
                    mmo(banks[5][:, hs], vtm[:, hs], scTm[:, hs], False, [Rvtm, RscTm])
                eb3 = ebend.rearrange("p (h n) -> p h n", h=4)
                def ld_state(g):
                    hb = g % 2
                    load(S0f[:, hb * 8:(hb + 1) * 8, :], st_h[g * 2:(g + 1) * 2].rearrange("j h k v -> k (j h) v"),
                         writes=[R("S0f", hb)])

                ld_state(0)
                for g in range(8):
                    hb = g % 2
                    S0fh = S0f[:, hb * 8:(hb + 1) * 8, :]
                    S0bh = S0b[:, hb * 8:(hb + 1) * 8, :]
                    RS0f, RS0b = R("S0f", hb), R("S0b", hb)
                    if g + 1 < 8:
                        ld_state(g + 1)
                    act(S0bh, S0fh, AF.Copy, reads=[RS0f], writes=[RS0b])
                    for jj in range(2):
                        j = g * 2 + jj
                        for h in range(4):
                            osl = slice(h * P + j * LS, h * P + (j + 1) * LS)
                            mmo(banks[5][:, osl], S0bh[:, jj * 4 + h, :], qin[:, osl],
                                (g == 7 and jj == 1 and h == 3), [RS0b, Rqin])
                    for h in range(4):
                        hs = slice(h * P, (h + 1) * P)
                        vi = (g * 4 + h) % 2
                        vb = vblk[:, vi * 2:(vi + 1) * 2, :]
                        Rvb = R("vblk", vi)
                        bk = 6 if vi == 0 else 4
                        tt(vb, vtm[:, hs].unsqueeze(1).broadcast_to([P, 2, P]),
                           bmask[:, g * 2:(g + 1) * 2].unsqueeze(2).broadcast_to([P, 2, P]), ALU.mult,
                           reads=[Rvtm, RC("bmask")], writes=[Rvb])
                        mm(banks[bk][:, 0:2 * P], khat[:, hs], vb.rearrange("p j v -> p (j v)"),
                           start=True, stop=True, reads=[Rkhat, Rvb], writes=[RB[bk]])
                        for jj in range(2):
                            j = g * 2 + jj
                            stt(S0fh[:, jj * 4 + h, :], S0fh[:, jj * 4 + h, :], eb3[:, h, j:j + 1],
                                banks[bk][:, jj * P:(jj + 1) * P], ALU.mult, ALU.add,
                                reads=[RS0f, Reb, RB[bk]], writes=[RS0f])
                    store(o_hs[g * 2:(g + 1) * 2].rearrange("j h k v -> k (j h) v"), S0fh, reads=[RS0f])
                    yield
            tt(ogf, banks[5][:], sog, ALU.mult, reads=[RB[5], Rsog], writes=[Rogf])
            act(tmpf, ogf, AF.Square, reads=[Rogf], writes=[Rtmpf], scale=512.0 ** -0.5)
            yield
            for h in range(4):
                mm(banks[4][:, 0:1], tmpf[:, h * P:(h + 1) * P], onesf[:, 0:1], start=(h == 0), stop=(h == 3),
                   reads=[Rtmpf, RC("onesf")], writes=[RB[4]])
            cp(small[:, 16 + t:17 + t], banks[4][:, 0:1], reads=[RB[4]], writes=[rsm(("ssa", t))])
            tt(ogT[:, :, tcols], ogf.rearrange("p (h t) -> p h t", t=P),
               hgw[:].unsqueeze(2).broadcast_to([P, 4, P]), ALU.mult, reads=[Rogf, RC("hgw")],
               writes=[R("ogT", t)] + RogT)
            yield

        def interleave(*gens):
            gens = [g for g in gens if g is not None]
            while gens:
                for g in list(gens):
                    try:
                        next(g)
                    except StopIteration:
                        gens.remove(g)

        interleave(proj_pieces(0))
        if pending_late[0] is not None:
            pending_late[0]()
            pending_late[0] = None
        evac(0)
        for k in range(nt + 1):
            interleave(chain(k) if k < nt else None,
                       seq(k - 1) if k >= 1 else None,
                       proj_pieces(k + 1) if k + 1 < nt else None)
            if k + 1 < nt:
                evac(k + 1)
        rsqrt_eps(rstda[:, 0:nt], small[:, 16:16 + nt], reads=[rsm(("ssa", t)) for t in range(nt)],
                  writes=[R("rstda")])
        if last_prompt:
            store(o_hp.rearrange("h k v -> k h v"), Sf[:], reads=[R("Sf")])

        prep_spatial()
        Wu, RWu = WS.get((tag, "u"), 3)
        Wv, RWv = WS.get((tag, "v"), 3)
        AR.reset()
        uTall, RuTall = AR.f(4 * TTM)
        uT4 = uTall.rearrange("p (g t) -> p g t", t=TTM)
        RuTt = [AR._res() for _ in range(nt)]
        vgs = [AR.f(512) for _ in range(nt)]
        vns = [AR.f(512) for _ in range(2)]
        tmps = [AR.f(512) for _ in range(2)]
        vvs = [AR.b(512) for _ in range(2)]
        sgs = [AR.f(512) for _ in range(2)]
        (tmpf, Rtmpf) = tmps[0]
        ugroups = []
        if npt > 0:
            ugroups.append((slice(0, npt * P), list(range(npt))))
        if has_s:
            ugroups.append((slice(npt * P, (npt + 1) * P), [npt]))
        for (ucols, utiles) in ugroups:
            nU = ucols.stop - ucols.start
            for g in range(4):
                for dk in range(8):
                    mm(banks[g][:, 0:nU], Wu[:, dk, g * P:(g + 1) * P], xnT[:, dk, ucols],
                       start=(dk == 0), stop=(dk == 7), reads=[RWu] + [R("xnT", q) for q in utiles],
                       writes=[RB[g]])
                act(uT4[:, g, ucols], banks[g][:, 0:nU], AF.Gelu, reads=[RB[g]],
                    writes=[RuTt[q] for q in utiles])
        for t in range(nt):
            tcols = slice(t * P, (t + 1) * P)
            Rxn = R("xnT", t)
            (vg, Rvg) = vgs[t]
            bv = 4 + (t % 2)
            for dk in range(8):
                mm(banks[bv][:, :], xnT[:, dk, tcols], Wv[:, dk, :], start=(dk == 0), stop=(dk == 7),
                   reads=[RWv, Rxn], writes=[RB[bv]])
            act(vg, banks[bv][:], AF.Gelu, reads=[RB[bv]], writes=[Rvg, rsm(("s1", t))],
                accum_out=small[:, 40 + t:41 + t])
            act(tmpf, vg, AF.Square, reads=[Rvg], writes=[Rtmpf, rsm(("s2", t))], accum_out=small[:, 48 + t:49 + t])
        s1 = small[:, 40:40 + nt]; s2 = small[:, 48:48 + nt]
        mean = small[:, 56:56 + nt]; msq = small[:, 64:64 + nt]; var = small[:, 72:72 + nt]
        rs = small[:, 80:80 + nt]; nmr = small[:, 88:88 + nt]
        ts(mean, s1, 1.0 / 512, None, ALU.mult, None, reads=[rsm(("s1", t)) for t in range(nt)], writes=[rsm("mean")])
        tt(msq, mean, mean, ALU.mult, reads=[rsm("mean")], writes=[rsm("msq")])
        stt(var, s2, 1.0 / 512, msq, ALU.mult, ALU.subtract,
            reads=[rsm(("s2", t)) for t in range(nt)] + [rsm("msq")], writes=[rsm("var")])
        rsqrt_eps(rs, var, reads=[rsm("var")], writes=[rsm("lrs")])
        stt(nmr, mean, -1.0, rs, ALU.mult, ALU.mult, reads=[rsm("mean"), rsm("lrs")], writes=[rsm("nmr")])

        Wga0, RWga0 = WS.get((tag, "ga0"), 2)
        Wga1, RWga1 = WS.get((tag, "ga1"), 2)
        Wa, RWa = WS.get((tag, "wa"), 2)
        Wga = [(Wga0, RWga0), (Wga1, RWga1)]
        unit = [0]

        def m1_pe(t, half):
            tcols = slice(t * P, (t + 1) * P)
            u = unit[0]
            bg, ba = (u % 2) * 2, (u % 2) * 2 + 1
            W, RW = Wga[half]
            for dk in range(8):
                mm(bank_ap[bg], xnT[:, dk, tcols], W[:, dk, :], start=(dk == 0), stop=(dk == 7),
                   reads=[RW, R("xnT", t)], writes=[RBall[bg]])
            for h in range(4):
                mm(bank_ap[ba], ogT[:, h, tcols], Wa[:, h, half * 512:(half + 1) * 512],
                   start=(h == 0), stop=(h == 3), reads=[RWa, R("ogT", t)] + RogT, writes=[RBall[ba]])
            unit[0] += 1
            return bg, ba, u

        def m1_ew(t, half, bg, ba, u):
            hsl = slice(half * 512, (half + 1) * 512)
            sga, Rsga = sgs[u % 2]
            act(sga, bank_ap[bg], AF.Sigmoid, reads=[RBall[bg]], writes=[Rsga])
            stt(t1[:, t, hsl], bank_ap[ba], rstda[:, t:t + 1], sga, ALU.mult, ALU.mult,
                reads=[RBall[ba], R("rstda"), Rsga], writes=[R("t1", t)] + Rt1)

        for t in range(nt):
            tcols = slice(t * P, (t + 1) * P)
            (vg, Rvg) = vgs[t]
            RuT = RuTt[t]
            (vn, Rvn) = vns[t % 2]
            (tmpf, Rtmpf) = tmps[t % 2]
            (vv, Rvv) = vvs[t % 2]
            sample, wst, Rwst, bst, Rbst = tps[t].sample, tps[t].wst, tps[t].Rwst, tps[t].bst, tps[t].Rbst
            u0 = m1_pe(t, 0)
            act(vn, vg, AF.Identity, reads=[Rvg, rsm("lrs"), rsm("nmr")], writes=[Rvn],
                scale=rs[:, t:t + 1], bias=nmr[:, t:t + 1])
            tt(vn, vn, lnw[:], ALU.mult, reads=[Rvn, RC("lnw")], writes=[Rvn])
            tt(vn, vn, lnb[:], ALU.add, reads=[Rvn, RC("lnb")], writes=[Rvn])
            act(vv, vn, AF.Copy, reads=[Rvn], writes=[Rvv])
            if sample:
                store(o_gv, vn, reads=[Rvn])
            m1_ew(t, 0, *u0)
            u1 = m1_pe(t, 1)
            for g in range(4):
                gs = slice(g * P, (g + 1) * P)
                mm(banks[6][:, gs], vv[:, gs], wst[:, g, :], start=True, stop=True,
                   reads=[Rvv, Rwst], writes=[RB[6]])
            tt(tmpf, banks[6][:], bst[:], ALU.add, reads=[RB[6], Rbst], writes=[Rtmpf])
            tt(obT[:, :, tcols], tmpf.rearrange("p (g t) -> p g t", t=P), uT4[:, :, tcols],
               ALU.mult, reads=[Rtmpf, RuT], writes=[R("obT", t)] + RobT)
            m1_ew(t, 1, *u1)

        Wgb0, RWgb0 = WS.get((tag, "gb0"), 2)
        Wgb1, RWgb1 = WS.get((tag, "gb1"), 2)
        Wb, RWb = WS.get((tag, "wb"), 2)
        AR.reset()
        sgs2 = [AR.f(D) for _ in range(2)]
        t2s = [AR.f(D) for _ in range(2)]
        mbs = [AR.b(D) for _ in range(2)]
        Rsgb2 = [[AR._res() for _ in range(2)] for _ in range(2)]
        Rt22 = [[AR._res() for _ in range(2)] for _ in range(2)]

        def m2_front(t):
            tcols = slice(t * P, (t + 1) * P)
            Rxn = R("xnT", t)
            sgb, Rsgb = sgs2[t % 2]
            t2, Rt2 = t2s[t % 2]
            mb, Rmb = mbs[t % 2]
            for half, (W, RW) in enumerate([(Wgb0, RWgb0), (Wgb1, RWgb1)]):
                bg = (t % 2) * 2 + half
                for dk in range(8):
                    mm(bank_ap[bg], xnT[:, dk, tcols], W[:, dk, :], start=(dk == 0), stop=(dk == 7),
                       reads=[RW, Rxn], writes=[RBall[bg]])
            for half in range(2):
                ba = 4 + half
                for h in range(4):
                    mm(bank_ap[ba], obT[:, h, tcols], Wb[:, h, half * 512:(half + 1) * 512],
                       start=(h == 0), stop=(h == 3), reads=[RWb, R("obT", t)] + RobT, writes=[RBall[ba]])
            for half in range(2):
                bg = (t % 2) * 2 + half
                hsl = slice(half * 512, (half + 1) * 512)
                act(sgb[:, hsl], bank_ap[bg], AF.Sigmoid, reads=[RBall[bg]], writes=[Rsgb2[t % 2][half]])
            for half in range(2):
                ba = 4 + half
                hsl = slice(half * 512, (half + 1) * 512)
                tt(t2[:, hsl], bank_ap[ba], sgb[:, hsl], ALU.mult, reads=[RBall[ba], Rsgb2[t % 2][half]],
                   writes=[Rt22[t % 2][half]])
            tt(mb, t1[:, t, :], t2, ALU.add, reads=[R("t1", t)] + Rt22[t % 2] + Rt1, writes=[Rmb])

        def m2_back(t):
            tcols = slice(t * P, (t + 1) * P)
            mb, Rmb = mbs[t % 2]
            for j in range(8):
                tr(psT[:, j * P:(j + 1) * P], mb[:, j * P:(j + 1) * P], identb[:],
                   reads=[Rmb, RC("identb")], writes=[RpsT])
            act(mT[:, :, tcols], psT[:].rearrange("p (j t) -> p j t", t=P), AF.Copy,
                reads=[RpsT], writes=[R("xnT", t)])

        for t in range(nt + 1):
            if t < nt:
                m2_front(t)
            if t >= 1:
                m2_back(t - 1)

        Wo0, RWo0 = WS.get((tag, "wo0"), 3)
        Wo1, RWo1 = WS.get((tag, "wo1"), 3)
        AR.reset()
        junk, Rjunk = AR.b(D)
        xnb = [AR.b(D) for _ in range(2)]

        def m3_front(t):
            tcols = slice(t * P, (t + 1) * P)
            for half, (W, RW) in enumerate([(Wo0, RWo0), (Wo1, RWo1)]):
                bo = (t % 2) * 2 + half
                hsl = slice(half * 512, (half + 1) * 512)
                for dk in range(8):
                    mm(bank_ap[bo], mT[:, dk, tcols], W[:, dk, :], start=(dk == 0), stop=(dk == 7),
                       reads=[RW, R("xnT", t)], writes=[RBall[bo]])
                tt(xt[t][:, hsl], xt[t][:, hsl], bank_ap[bo], ALU.add,
                   reads=[R("xt", t), RBall[bo]], writes=[R("xt", t)])

        def ffn_norm(t):
            act(junk, xt[t], AF.Square, reads=[R("xt", t)], writes=[Rjunk, rsm(("ss", t))], accum_out=small[:, t:t + 1],
                scale=float(D) ** -0.5)
            rsqrt_eps(small[:, 8 + t:9 + t], small[:, t:t + 1], reads=[rsm(("ss", t))], writes=[rsm(("rs1", t))])
            xb, Rxb = xnb[t % 2]
            ts(xb, xt[t], small[:, 8 + t:9 + t], None, ALU.mult, None, reads=[R("xt", t), rsm(("rs1", t))],
               writes=[Rxb])
            for j in range(8):
                tr(psT[:, j * P:(j + 1) * P], xb[:, j * P:(j + 1) * P], identb[:],
                   reads=[Rxb, RC("identb")], writes=[RpsT])
            tt(xnT[:, :, t * P:(t + 1) * P], psT[:].rearrange("p (j t) -> p j t", t=P),
               nw[:, 1, :].unsqueeze(2).broadcast_to([P, 8, P]), ALU.mult,
               reads=[RpsT, RC("nw")], writes=[R("xnT", t)])

        for t in range(nt + 1):
            if t < nt:
                m3_front(t)
            if t >= 1:
                ffn_norm(t - 1)

        if has_s:
            for g in range(11):
                stg = cstg[g % 2]; Rstg = R("cstg", g % 2)
                load(stg[:], st_c[:, g * 512:(g + 1) * 512], writes=[Rstg])
                for jj in range(4):
                    tr(banks[g % 2][:, jj * 32:(jj + 1) * 32], stg[0:32, jj * P:(jj + 1) * P], identf[0:32, 0:32],
                       reads=[Rstg, RC("identf")], writes=[RB[g % 2]])
                for jj in range(4):
                    c = g * 4 + jj
                    act(scs[:, c % NJ, c // NJ, :], banks[g % 2][:, jj * 32:(jj + 1) * 32], AF.Copy,
                        reads=[RB[g % 2]], writes=[R("scs")])
        RS0fa = [R("S0f", 0), R("S0f", 1)]
        RS0ba = [R("S0b", 0), R("S0b", 1)]
        pT_ = []
        for t in range(nt):
            pT_.append((S0b[:, 2 * t:2 * t + 2, :].rearrange("p a b -> p (a b)"), RS0ba))

        def p_load(t):
            ptile = S0f[:, 2 * t:2 * t + 2, :].rearrange("p a b -> p (a b)")
            pbf = S0f[:, 10 + t, :].bitcast(BF16)
            load(ptile, tps[t].prows, writes=RS0fa)
            act(pbf, ptile, AF.Copy, reads=RS0fa, writes=RS0fa)

        def p_tr(t):
            pbf = S0f[:, 10 + t, :].bitcast(BF16)
            pTb = pT_[t][0]
            for kk in range(2):
                tr(psT[:, kk * P:(kk + 1) * P], pbf[:, kk * P:(kk + 1) * P], identb[:],
                   reads=RS0fa + [RC("identb")], writes=[RpsT])
            act(pTb, psT[:, 0:256], AF.Copy, reads=[RpsT], writes=RS0ba)

        groups = []
        if npt > 0:
            groups.append((slice(0, npt * P), list(range(npt)), False, 1, npt * P))
        if has_s:
            groups.append((slice(npt * P, (npt + 1) * P), [npt], True, NSEQ_S, LS))
        AR.reset()
        HS_ = {False: 512, True: P}
        npar_k = {False: (2 if has_s else 3), True: 2}
        yall_k = {False: [AR.f(1024)[0] for _ in range(npar_k[False])], True: [AR.f(2 * P)[0] for _ in range(2)]}
        Ryh_k = {kk_: [[AR._res() for _ in range(2)] for _ in range(npar_k[kk_])] for kk_ in (False, True)}
        Ryb_k = {kk_: [[AR._res() for _ in range(2)] for _ in range(npar_k[kk_])] for kk_ in (False, True)}
        bank0_k = {False: 0, True: 4}
        hhs = [AR.f(2 * NSEQ_S * 4)[0] for _ in range(2)] if has_s else None
        ptmp = [[AR.f(2) for _ in range(2)] for _ in range(3)]
        Rhhs = [[AR._res() for _ in range(2)] for _ in range(2)]
        it = 0
        pending_tail = [None]
        for j in range(NJ):
            Wup, RWup2 = WS.get((tag, "up", j), 4)
            if 1 <= j < 1 + nt:
                p_load(j - 1)
            if 8 <= j < 8 + nt:
                p_tr(j - 8)
            for (gcols, gtiles, sample, NS_, L_) in groups:
                TG = NS_ * L_
                Rxg = [R("xnT", q) for q in gtiles]
                par = j % npar_k[sample]
                yall, Ryh, Ryb, HS, b0 = yall_k[sample], Ryh_k[sample], Ryb_k[sample], HS_[sample], bank0_k[sample]
                y4 = yall[par].rearrange("p (a c) -> p a c", a=2)[:, :, 0:NS_ * L_].rearrange(
                    "p a (s l) -> p a s l", l=L_)
                if sample:
                    hh4 = hhs[par].rearrange("p (a s r) -> p a s r", a=2, r=4)
                    Rhh = Rhhs[par]
                    cp(hh4[:, :, :, 0:2], scs[:, j, :, :].rearrange("p a (s r) -> p a s r", r=2),
                       reads=[R("scs")], writes=Rhh)
                else:
                    hh4 = hh_p[:, (sp_idx % 2) * NJ + j, :, :].unsqueeze(2)
                    Rhh = [R("cr", sp_idx % 2, j)] * 2
                pvs = []
                for half in range(2):
                    bk = b0 + par * 2 + half
                    for dk in range(8):
                        mm(bank_ap[bk][:, 0:TG], Wup[:, dk, half * P:(half + 1) * P], xnT[:, dk, gcols],
                           start=(dk == 0), stop=(dk == 7), reads=[RWup2[half]] + Rxg, writes=[RBall[bk]])
                    pvs.append(bank_ap[bk][:, 0:NS_ * L_].rearrange("p (s l) -> p s l", l=L_))
                for half in range(2):
                    bk = b0 + par * 2 + half
                    act(y4[:, half], pvs[half], AF.Identity, reads=[RBall[bk], RC("cw"), RC("cb")],
                        writes=[Ryh[par][half], Ryb[par][half]], scale=cw[:, j, half, 2:3], bias=cb[:, j, half:half + 1])
                    act(hh4[:, half, :, 2:4], pvs[half][:, :, 0:2], AF.Copy, reads=[RBall[bk]], writes=[Rhh[half]])
                    if sample:
                        act(cso[:, j, half, :].rearrange("p (s r) -> p s r", r=2), pvs[half][:, :, L_ - 2:L_],
                            AF.Copy, reads=[RBall[bk]], writes=[R("cso")])
                    else:
                        act(hh_p[:, ((sp_idx + 1) % 2) * NJ + j, half, 0:2].unsqueeze(1), pvs[half][:, :, L_ - 2:L_],
                            AF.Copy, reads=[RBall[bk]], writes=[R("cr", (sp_idx + 1) % 2, j)])
                for half in range(2):
                    bk = b0 + par * 2 + half
                    for k in (1, 0):
                        stt(y4[:, half, :, 2:L_], pvs[half][:, :, k:L_ - 2 + k], cw[:, j, half, k:k + 1],
                            y4[:, half, :, 2:L_], ALU.mult, ALU.add,
                            reads=[RBall[bk], RC("cw"), Ryb[par][half]], writes=[Ryb[par][half]])
                        if k == 0 and not sample:
                            pt, Rpt = ptmp[par][half]
                            ts(pt, hh4[:, half, 0, 0:2], cw[:, j, half, 0:1], None, ALU.mult, None,
                               reads=[Rhh[half], RC("cw")], writes=[Rpt], eng="pool")
                            tt(y4[:, half, 0, 0:2], y4[:, half, 0, 0:2], pt, ALU.add,
                               reads=[Rpt, Ryh[par][half]], writes=[Ryh[par][half]], eng="pool")
                        else:
                            stt(y4[:, half, :, 0:2], hh4[:, half, :, k:k + 2], cw[:, j, half, k:k + 1],
                                y4[:, half, :, 0:2], ALU.mult, ALU.add,
                                reads=[Rhh[half], RC("cw"), Ryh[par][half]], writes=[Ryh[par][half]])

                def tail(j=j, par=par, gcols=gcols, TG=TG, yall=yall, Ryh=Ryh, Ryb=Ryb, HS=HS):
                    ya = yall[par][:, 0:TG]
                    yb = yall[par][:, HS:HS + TG]
                    act(ya, ya, AF.Gelu, reads=[Ryh[par][0], Ryb[par][0]], writes=[Ryh[par][0], Ryb[par][0]])
                    tt(gT[:, j, gcols], ya, yb, ALU.mult,
                       reads=[Ryh[par][0], Ryb[par][0], Ryh[par][1], Ryb[par][1]], writes=[RG[j]], eng="pool")

                if pending_tail[0] is not None:
                    pending_tail[0]()
                pending_tail[0] = tail
                it += 1
        if pending_tail[0] is not None:
            pending_tail[0]()
            pending_tail[0] = None
        def conv_out_groups(bank_ids):
            outs_ = []
            if has_s:
                outs_.append(True)
            if last_prompt:
                outs_.append(False)
            gidx = 0
            for sample in outs_:
                NR = 32 if sample else 2
                odst = o_cs if sample else o_cp
                for g in range(11):
                    bk = bank_ids[gidx % 2]
                    stg = cstg[gidx % 2]; Rstg = R("cstg", gidx % 2)
                    gidx += 1
                    for jj in range(4):
                        c = g * 4 + jj
                        if sample:
                            srcap = cso[:, c % NJ, c // NJ, :]
                            Rsrc = [R("cso")]
                        else:
                            srcap = hh_p[:, (NSP % 2) * NJ + c % NJ, c // NJ, 0:2]
                            Rsrc = [R("cr", NSP % 2, c % NJ)]
                        tr(bank_ap[bk][0:NR, jj * P:(jj + 1) * P], srcap, identf[:],
                           reads=Rsrc + [RC("identf")], writes=[RBall[bk]])
                    act(stg[0:NR, :], bank_ap[bk][0:NR, :], AF.Copy, reads=[RBall[bk]], writes=[Rstg])
                    store(odst[:, g * 512:(g + 1) * 512], stg[0:NR, :], reads=[Rstg])
                    yield

        if nt <= 4:
            passes = [list(range(nt))]
            cgen = conv_out_groups([0, 1])
            for _ in cgen:
                pass
            cgen = None
        else:
            passes = [list(range(3)), list(range(3, nt))]
            cgen = conv_out_groups([6, 7])
        n_pass = len(passes)
        dn_loaded = {}
        for ps, ptiles in enumerate(passes):
            for q in range(6):
                kcs = list(range(q * 4, min(NJ, (q + 1) * 4)))
                if ps == 0:
                    dn_loaded[q] = WS.get((tag, "dn", 0, q), 4 if n_pass == 1 else 5 - q)
                Wd, RWd = dn_loaded[q]
                for ci, kc in enumerate(kcs):
                    for t in ptiles:
                        tcols = slice(t * P, (t + 1) * P)
                        for half in range(2):
                            bk = (t - ptiles[0]) * 2 + half
                            mm(bank_ap[bk], gT[:, kc, tcols], Wd[:, ci, half * 512:(half + 1) * 512],
                               start=(kc == 0), stop=(kc == NJ - 1), reads=[RWd, RG[kc]], writes=[RBall[bk]])
                    if cgen is not None and ps == 0:
                        try:
                            next(cgen)
                        except StopIteration:
                            cgen = None
            for t in ptiles:
                for half in range(2):
                    bk = (t - ptiles[0]) * 2 + half
                    hsl = slice(half * 512, (half + 1) * 512)
                    tt(xt[t][:, hsl], xt[t][:, hsl], bank_ap[bk], ALU.add, reads=[R("xt", t), RBall[bk]],
                       writes=[R("xt", t)])
        if cgen is not None:
            for _ in cgen:
                pass

        Wg0, RWg0 = WS.get((tag, "pg0"), 3)
        Wg1, RWg1 = WS.get((tag, "pg1"), 3)
        Wp, RWp = WS.get((tag, "pp"), 3)
        AR.reset()
        xnbP = [AR.b(D) for _ in range(2)]
        for t in range(nt):
            xb, Rxb = xnbP[t % 2]
            cp(xb, xt[t], reads=[R("xt", t)], writes=[Rxb])
            for j in range(8):
                tr(psT[:, j * P:(j + 1) * P], xb[:, j * P:(j + 1) * P], identb[:],
                   reads=[Rxb, RC("identb")], writes=[RpsT])
            tt(xnT[:, :, t * P:(t + 1) * P], psT[:].rearrange("p (j t) -> p j t", t=P),
               nw[:, 2, :].unsqueeze(2).broadcast_to([P, 8, P]), ALU.mult,
               reads=[RpsT, RC("nw")], writes=[R("xnT", t)])
        sg_ = [AR.f(D) for _ in range(2)]
        yb_ = [AR.f(D) for _ in range(2)]
        junk, Rjunk = AR.b(D)
        xnb2 = [AR.b(D) for _ in range(2)]
        for t in range(nt):
            act(junk, xt[t], AF.Square, reads=[R("xt", t)], writes=[Rjunk, rsm(("pss", t))],
                accum_out=small[:, 24 + t:25 + t], scale=float(D) ** -0.5)
        rsqrt_eps(small[:, 32:32 + nt], small[:, 24:24 + nt], reads=[rsm(("pss", t)) for t in range(nt)],
                  writes=[rsm("prs")])

        def p4_front(t):
            tcols = slice(t * P, (t + 1) * P)
            pTb, RpT = pT_[t]
            for half, (W, RW) in enumerate([(Wg0, RWg0), (Wg1, RWg1)]):
                hsl = slice(half * 512, (half + 1) * 512)
                bg = (t % 2) * 4 + half
                bp = (t % 2) * 4 + 2 + half
                for dk in range(8):
                    mm(bank_ap[bg], xnT[:, dk, tcols], W[:, dk, :], start=(dk == 0), stop=(dk == 7),
                       reads=[RW, R("xnT", t)], writes=[RBall[bg]])
                for kk in range(2):
                    mm(bank_ap[bp], pTb[:, kk * P:(kk + 1) * P], Wp[:, kk, hsl], start=(kk == 0), stop=(kk == 1),
                       reads=[RWp, RpT], writes=[RBall[bp]])

        hf_ = [AR.f(D) for _ in range(nt)]
        n_next = len(next_x) if next_x is not None else 0

        Rsgh = [[AR._res() for _ in range(2)] for _ in range(2)]
        Rtmph = [[AR._res() for _ in range(2)] for _ in range(2)]

        def p4_back_a(t):
            sg, _ = sg_[t % 2]
            tmp, _ = yb_[t % 2]
            hf, Rhf = hf_[t]
            for half in range(2):
                hsl = slice(half * 512, (half + 1) * 512)
                bg = (t % 2) * 4 + half
                act(sg[:, hsl], bank_ap[bg], AF.Sigmoid, reads=[RBall[bg], rsm("prs")], writes=[Rsgh[t % 2][half]],
                    scale=small[:, 32 + t:33 + t])
            for half in range(2):
                hsl = slice(half * 512, (half + 1) * 512)
                bp = (t % 2) * 4 + 2 + half
                tt(tmp[:, hsl], bank_ap[bp], sg[:, hsl], ALU.mult, reads=[RBall[bp], Rsgh[t % 2][half]],
                   writes=[Rtmph[t % 2][half]])
            tt(hf, xt[t], tmp, ALU.add, reads=[R("xt", t)] + Rtmph[t % 2], writes=[Rhf])
            if t < n_next:
                load(xt[t], next_x[t], writes=[R("xt", t)])

        def p4_back_b(t):
            hf, Rhf = hf_[t]
            act(junk, hf, AF.Square, reads=[Rhf], writes=[Rjunk, rsm(("fs", t))], accum_out=small[:, 96 + t:97 + t],
                scale=float(D) ** -0.5)

        def next_sq(t, junk=junk, Rjunk=Rjunk):
            act(junk, xt[t], AF.Square, reads=[R("xt", t)], writes=[Rjunk, rsm(("nss", t))],
                accum_out=small[:, 112 + t:113 + t], scale=float(D) ** -0.5)

        def next_tr(t, xbuf=None):
            xb, Rxb = xbuf if xbuf is not None else xnb2[t % 2]
            ts(xb, xt[t], small[:, 120 + t:121 + t], None, ALU.mult, None, reads=[R("xt", t), rsm("nrs")],
               writes=[Rxb])
            for j in range(8):
                tr(psT[:, j * P:(j + 1) * P], xb[:, j * P:(j + 1) * P], identb[:],
                   reads=[Rxb, RC("identb")], writes=[RpsT])
            tt(xnT[:, :, t * P:(t + 1) * P], psT[:].rearrange("p (j t) -> p j t", t=P),
               nw[:, 0, :].unsqueeze(2).broadcast_to([P, 8, P]), ALU.mult,
               reads=[RpsT, RC("nw")], writes=[R("xnT", t)])

        def fin(t):
            hf, Rhf = hf_[t]
            stt(hf, hf, small[:, 104 + t:105 + t], fnw[:], ALU.mult, ALU.mult,
                reads=[Rhf, rsm("frs"), RC("fnw")], writes=[Rhf])
            store(tps[t].yrows, hf, reads=[Rhf])

        for t in range(nt, n_next):
            load(xt[t], next_x[t], writes=[R("xt", t)])
        p4_front(0)
        for t in range(nt):
            if t + 1 < nt:
                p4_front(t + 1)
            p4_back_a(t)
            if 0 <= t - 1 < n_next:
                next_sq(t - 1)
            p4_back_b(t)
        late = nt - 1 if nt - 1 < n_next else None
        for t in range(nt, n_next):
            next_sq(t)
        lo = min(max(nt - 1, 0), n_next)
        if lo > 0:
            rsqrt_eps(small[:, 120:120 + lo], small[:, 112:112 + lo],
                      reads=[rsm(("nss", t)) for t in range(lo)], writes=[rsm("nrs")])
        if n_next > nt:
            rsqrt_eps(small[:, 120 + nt:120 + n_next], small[:, 112 + nt:112 + n_next],
                      reads=[rsm(("nss", t)) for t in range(nt, n_next)], writes=[rsm("nrs")])
        for t in range(n_next):
            if t != late:
                next_tr(t)
        if late is not None:
            def late_norm(t=late):
                jk = S0f[:, 4:8, :].rearrange("p a b -> p (a b)").bitcast(BF16)
                xbl = S0f[:, 0:4, :].rearrange("p a b -> p (a b)").bitcast(BF16)
                next_sq(t, jk, R("S0f", 0))
                rsqrt_eps(small[:, 120 + t:121 + t], small[:, 112 + t:113 + t], reads=[rsm(("nss", t))],
                          writes=[rsm("nrs")])
                next_tr(t, (xbl, R("S0f", 0)))
            pending_late[0] = late_norm
        rsqrt_eps(small[:, 104:104 + nt], small[:, 96:96 + nt], reads=[rsm(("fs", t)) for t in range(nt)],
                  writes=[rsm("frs")])
        for t in range(nt):
            fin(t)

    NSP = SEQ // TT
    pending_late = [None]
    kinds_of = [["p"] * NT for _ in range(NSP)]
    kinds_of[-1] = kinds_of[-1] + ["s"]
    for sp_idx in range(NSP):
        register_weights(("st", sp_idx), n_dn_pass=1)
    for sp_idx in range(NSP):
        nx = None
        if sp_idx + 1 < NSP:
            nx = [xp[(sp_idx + 1) * TT + t * P:(sp_idx + 1) * TT + (t + 1) * P, :] for t in range(NT)]
            if sp_idx + 1 == NSP - 1:
                nx.append(xs[0:P, :])
        run_supertile(kinds_of[sp_idx], sp_idx, preloaded=(sp_idx > 0), next_x=nx)

    S.finish()
    S.replay()


_NC_CACHE = {}


def _consts():
    i = np.arange(P)
    c = {}
    c["c_identb"] = np.eye(P, dtype=np.float32).astype(ml_dtypes.bfloat16)
    c["c_identf"] = np.eye(P, dtype=np.float32)
    s, t = np.meshgrid(i, i, indexing="ij")
    c["c_maskp"] = ((s // 64 == t // 64) & (s <= t)).astype(np.float32)
    c["c_masks"] = ((s // 8 == t // 8) & (s <= t)).astype(np.float32)
    c["c_maskf"] = (s <= t).astype(np.float32)
    tcol = np.arange(512) % 128
    c["c_scanp"] = np.broadcast_to((tcol % 64 != 0).astype(np.float32), (P, 512)).copy()
    c["c_scans"] = np.broadcast_to((tcol % 8 != 0).astype(np.float32), (P, 512)).copy()
    c["c_bmask"] = (i[:, None] // 8 == np.arange(16)[None, :]).astype(np.float32)
    c["c_onesf"] = np.ones((P, 1), np.float32)
    return c


def _fm(v, nchunk):
    return np.ascontiguousarray(np.asarray(v, np.float32).reshape(nchunk, P).T)


def kernel(x_prompt, x_sample, p_prompt, p_sample, state_hgrn, state_conv, lb_logits,
           norm_mix_w, w_in, hgrn_norm_w, ln_v_w, ln_v_b, w_spatial, b_spatial, w_a_out,
           w_b_out, w_o, norm_ffn_w, w_up, conv_w, conv_b, w_down, norm_ple_w, w_ple_gate,
           w_ple_proj, final_norm_w, _debug=None):
    f32 = np.float32
    A = lambda a: np.ascontiguousarray(np.asarray(a, dtype=f32))
    if "nc" not in _NC_CACHE or _debug is not None:
        nc = build_nc(_debug)
        if _debug is None:
            _NC_CACHE["nc"] = nc
    else:
        nc = _NC_CACHE["nc"]
    shared = _consts()
    shared["w_in"] = A(w_in[0]); shared["w_a"] = A(w_a_out[0]); shared["w_b"] = A(w_b_out[0])
    shared["w_o"] = A(w_o[0]); shared["w_up"] = np.ascontiguousarray(
        np.asarray(w_up[0], dtype=f32).reshape(D, 2, NJ, P).transpose(0, 2, 1, 3).reshape(D, 2 * DFF)); shared["w_dn"] = A(w_down[0])
    shared["w_pg"] = A(w_ple_gate[0]); shared["w_pp"] = A(w_ple_proj[0])
    shared["v_nw"] = np.ascontiguousarray(np.stack(
        [_fm(norm_mix_w[0], 8), _fm(norm_ffn_w[0], 8), _fm(norm_ple_w[0], 8)], axis=1))
    shared["v_hgw"] = _fm(hgrn_norm_w[0], 4)
    lbl = np.asarray(lb_logits, f32)
    shared["v_lbl"] = np.ascontiguousarray(np.stack([_fm(lbl[0], 4), _fm(lbl[1], 4)], axis=1))
    shared["v_lnw"] = A(ln_v_w[0]); shared["v_lnb"] = A(ln_v_b[0]); shared["v_fnw"] = A(final_norm_w)
    bs = np.asarray(b_spatial[0], f32)
    shared["v_bsp"] = np.ascontiguousarray(bs.reshape(512))
    shared["v_bss"] = np.ascontiguousarray(np.tile(bs[:, :LS], (1, NSEQ_S)).reshape(512))
    cwn = np.asarray(conv_w[0], f32)
    shared["v_cw"] = np.ascontiguousarray(cwn.T.reshape(2, NJ, P, 3).transpose(2, 1, 0, 3))
    shared["v_cb"] = np.ascontiguousarray(np.asarray(conv_b[0], f32).reshape(2, NJ, P).transpose(2, 1, 0))
    ws = np.asarray(w_spatial[0], f32)
    shared["v_wsp"] = np.ascontiguousarray(ws.transpose(2, 0, 1))
    wss = np.zeros((P, 4, P), f32)
    sub = ws[:, :LS, :LS].transpose(2, 0, 1)
    for j in range(NSEQ_S):
        wss[j * LS:(j + 1) * LS, :, j * LS:(j + 1) * LS] = sub
    shared["v_wss"] = wss

    xpn = np.asarray(x_prompt, f32); xsn = np.asarray(x_sample, f32)
    ppn = np.asarray(p_prompt, f32)[0]; psn = np.asarray(p_sample, f32)[0]
    sth = np.asarray(state_hgrn, f32)[0]; stc = np.asarray(state_conv, f32)[0]
    in_maps = []
    for c in range(8):
        m = dict(shared)
        m["xp"] = np.ascontiguousarray(xpn[c])
        m["xs"] = np.ascontiguousarray(xsn[c * 16:(c + 1) * 16].reshape(P, D))
        m["pp"] = np.ascontiguousarray(ppn[c])
        m["ps"] = np.ascontiguousarray(psn[c * 16:(c + 1) * 16].reshape(P, 256))
        m["st_h"] = np.ascontiguousarray(sth[c * 16:(c + 1) * 16])
        m["st_c"] = np.ascontiguousarray(stc[c * 16:(c + 1) * 16].reshape(32, 2 * DFF))
        in_maps.append(m)
    res = run_bass_kernel_spmd(nc, in_maps, core_ids=list(range(8)))
    rs = res.results
    y_prompt = np.stack([r["y_p"] for r in rs], 0).astype(f32)
    y_sample = np.concatenate([r["y_s"].reshape(16, LS, D) for r in rs], 0).astype(f32)
    hgrn_p = np.stack([r["o_hp"] for r in rs], 0)[None].astype(f32)
    hgrn_s = np.concatenate([r["o_hs"] for r in rs], 0)[None].astype(f32)
    conv_p = np.stack([r["o_cp"] for r in rs], 0)[None].astype(f32)
    conv_s = np.concatenate([r["o_cs"].reshape(16, 2, 2 * DFF) for r in rs], 0)[None].astype(f32)
    gv_s = np.concatenate([r["o_gv"].reshape(16, LS, 512) for r in rs], 0)[None].astype(f32)
    if _debug is not None:
        return rs
    return (y_prompt, y_sample, hgrn_p, hgrn_s, conv_p, conv_s, gv_s)
```

```python
import contextlib
import numpy as np
import ml_dtypes
import concourse.bass as bass
import concourse.mybir as mybir
from concourse.bass_utils import run_bass_kernel_spmd

F32 = mybir.dt.float32
BF16 = mybir.dt.bfloat16
AF = mybir.ActivationFunctionType
ALU = mybir.AluOpType

P = 128
D = 1024
NIN = 5120
DFF = 2816
NJ = 22
SEQ = 2048
NSEQ_S = 16
LS = 8
EPS = 1e-6
NT_P = 4
NSLOT = 6
NDMA = 24
STRICT_SAME_ENGINE = True


class Res:
    __slots__ = ("name", "writer", "readers")

    def __init__(self, name):
        self.name = name
        self.writer = None
        self.readers = []


class Sched:
    def __init__(self, nc, stack):
        self.nc = nc
        self.names = ["pe", "act", "dve", "pool", "sp"]
        self.lists = {e: [] for e in self.names}
        self.sems = {e: stack.enter_context(nc.semaphore("s_" + e)) for e in self.names}
        self.cnt = {e: 0 for e in self.names}
        self.waited = {e: {} for e in self.names}
        self.dsems = [stack.enter_context(nc.semaphore("d%d" % i)) for i in range(NDMA)]
        self.dcnt = [0] * NDMA
        self.drr = {"sp": 0, "pool": 0}
        self.res = {}
        self.out_tokens = []

    def R(self, *key):
        r = self.res.get(key)
        if r is None:
            r = Res(key)
            self.res[key] = r
        return r

    def _semof(self, tok):
        if tok[0] == "e":
            return ("e", tok[1]), self.sems[tok[1]]
        return ("d", tok[1]), self.dsems[tok[1]]

    @staticmethod
    def _flat(xs):
        out = []
        for x in xs:
            if isinstance(x, (list, tuple)):
                out.extend(Sched._flat(x))
            else:
                out.append(x)
        return out

    def emit(self, eng, fn, reads=(), writes=(), dma=False, is_out=False):
        reads = self._flat(reads)
        writes = self._flat(writes)
        deps = set()
        for r in reads:
            if r.writer is not None:
                deps.add(r.writer)
        for w in writes:
            if w.writer is not None and (STRICT_SAME_ENGINE or not (w.writer[0] == "e" and w.writer[1] == eng and not dma)):
                deps.add(w.writer)
            for t in w.readers:
                if STRICT_SAME_ENGINE or not (t[0] == "e" and t[1] == eng and not dma):
                    deps.add(t)
        need = {}
        for tok in deps:
            if tok[0] == "e" and tok[1] == eng and not dma and eng == "pe":
                continue
            key, sem = self._semof(tok)
            if need.get(key, (None, 0))[1] < tok[2]:
                need[key] = (sem, tok[2])
        waits = []
        wd = self.waited[eng]
        for key, (sem, val) in need.items():
            if wd.get(key, 0) < val:
                waits.append((sem, val))
                wd[key] = val
        if dma:
            if eng == "pool":
                s = 16 + self.drr["pool"]
                self.drr["pool"] = (self.drr["pool"] + 1) % (NDMA - 16)
            else:
                s = self.drr["sp"]
                self.drr["sp"] = (self.drr["sp"] + 1) % 16
            prior = self.dcnt[s] * 16
            key = ("d", s)
            if prior > 0 and wd.get(key, 0) < prior:
                waits.append((self.dsems[s], prior))
                wd[key] = prior
            self.dcnt[s] += 1
            tok = ("d", s, self.dcnt[s] * 16)
            inc = (self.dsems[s], 16)
            if is_out:
                self.out_tokens.append(tok)
        else:
            self.cnt[eng] += 1
            tok = ("e", eng, self.cnt[eng])
            inc = (self.sems[eng], 1)
        self.lists[eng].append((waits, fn, inc))
        for r in reads:
            r.readers.append(tok)
        for w in writes:
            w.writer = tok
            w.readers = []
        return tok

    def finish(self):
        waits = []
        for s in range(NDMA):
            if self.dcnt[s] > 0:
                waits.append((self.dsems[s], self.dcnt[s] * 16))
        self.lists["sp"].append((waits, None, None))

    def replay(self):
        nc = self.nc
        lists = self.lists

        needed = {}
        for name in self.names:
            for waits, fn, inc in lists[name]:
                for sem, val in waits:
                    needed.setdefault(id(sem), set()).add(val)

        def run(e, items):
            pending = 0
            count = 0
            for waits, fn, inc in items:
                for sem, val in waits:
                    e.wait_ge(sem, val)
                if fn is not None:
                    ins = fn(e)
                    if inc[1] == 16:
                        ins.then_inc(inc[0], 16)
                        continue
                    count += 1
                    pending += 1
                    if count in needed.get(id(inc[0]), ()):
                        ins.then_inc(inc[0], pending)
                        pending = 0

        with nc.Block() as block:
            @block.tensor
            def _(e):
                run(e, lists["pe"])

            @block.scalar
            def _(e):
                run(e, lists["act"])

            @block.vector
            def _(e):
                run(e, lists["dve"])

            @block.gpsimd
            def _(e):
                run(e, lists["pool"])

            @block.sync
            def _(e):
                run(e, lists["sp"])


def build_nc(debug=None):
    nc = bass.Bass("TRN2", target_bir_lowering=False)
    stack = contextlib.ExitStack()
    with stack:
        _build(nc, stack, debug)
    return nc


def _build(nc, stack, debug):
    S = Sched(nc, stack)
    R = S.R

    def din(name, shape, dt=F32):
        return nc.dram_tensor(name, list(shape), dt, kind="ExternalInput").ap()

    def dout(name, shape, dt=F32):
        return nc.dram_tensor(name, list(shape), dt, kind="ExternalOutput").ap()

    xp = din("xp", [SEQ, D]); xs = din("xs", [P, D])
    pp = din("pp", [SEQ, 256]); ps_ = din("ps", [P, 256])
    st_h = din("st_h", [NSEQ_S, 4, P, P]); st_c = din("st_c", [32, 2 * DFF])
    w_in = din("w_in", [D, NIN]); w_a = din("w_a", [512, D]); w_b = din("w_b", [512, D])
    w_o = din("w_o", [D, D]); w_up = din("w_up", [D, 2 * DFF]); w_dn = din("w_dn", [DFF, D])
    w_pg = din("w_pg", [D, D]); w_pp = din("w_pp", [256, D])
    c_identb = din("c_identb", [P, P], BF16); c_identf = din("c_identf", [P, P])
    c_maskp = din("c_maskp", [P, P]); c_masks = din("c_masks", [P, P]); c_maskf = din("c_maskf", [P, P])
    c_scanp = din("c_scanp", [P, 512]); c_scans = din("c_scans", [P, 512])
    c_bmask = din("c_bmask", [P, 16]); c_onesf = din("c_onesf", [P, 1])
    v_nw = din("v_nw", [P, 3, 8])
    v_hgw = din("v_hgw", [P, 4]); v_lbl = din("v_lbl", [P, 2, 4])
    v_lnw = din("v_lnw", [512]); v_lnb = din("v_lnb", [512]); v_fnw = din("v_fnw", [D])
    v_bsp = din("v_bsp", [512]); v_bss = din("v_bss", [512])
    v_cw = din("v_cw", [P, NJ, 2, 3]); v_cb = din("v_cb", [P, NJ, 2])
    v_wsp = din("v_wsp", [P, 4, P]); v_wss = din("v_wss", [P, 4, P])

    y_p = dout("y_p", [SEQ, D]); y_s = dout("y_s", [P, D])
    o_hp = dout("o_hp", [4, P, P]); o_hs = dout("o_hs", [NSEQ_S, 4, P, P])
    o_cp = dout("o_cp", [2, 2 * DFF]); o_cs = dout("o_cs", [32, 2 * DFF])
    o_gv = dout("o_gv", [P, 512])
    dbg = {}
    if debug:
        for name, shape in debug.items():
            dbg[name] = dout("dbg_" + name, shape)

    def sb(name, shape, dt=F32):
        return stack.enter_context(nc.sbuf_tensor(name, list(shape), dt))

    NT = NT_P
    TT = NT * P
    NT_MAX = NT + 1
    TTM = NT_MAX * P
    banks = [stack.enter_context(nc.psum_tensor("bank%d" % i, [P, 512], F32)) for i in range(7)]
    psT = stack.enter_context(nc.psum_tensor("psT", [P, 1024], BF16))
    RB = [R("bank", i) for i in range(7)]
    RpsT = R("psT")
    bank_ap = [b[:, :] for b in banks] + [psT[:].bitcast(F32)]
    RBall = RB + [RpsT]

    identb = sb("identb", [P, P], BF16); identf = sb("identf", [P, P])
    maskp = sb("maskp", [P, P]); masks = sb("masks", [P, P]); maskf = sb("maskf", [P, P])
    scanp = sb("scanp", [P, 512]); scans = sb("scans", [P, 512])
    bmask = sb("bmask", [P, 16]); onesf = sb("onesf", [P, 1])
    nw = sb("nw", [P, 3, 8]); hgw = sb("hgw", [P, 4]); lbl = sb("lbl", [P, 2, 4])
    oml = sb("oml", [P, 4]); lbt = sb("lbt", [P, 4])
    lnw = sb("lnw", [P, 512]); lnb = sb("lnb", [P, 512]); fnw = sb("fnw", [P, D])
    bsp = sb("bsp", [P, 512]); bss = sb("bss", [P, 512])
    cw = sb("cw", [P, NJ, 2, 3]); cb = sb("cb", [P, NJ, 2])
    wsp = sb("wsp", [P, 4, P], BF16); wss = sb("wss", [P, 4, P], BF16)
    Sf = sb("Sf", [P, 4, P]); Sb = sb("Sb", [P, 4, P], BF16)
    hh_p = sb("hh_p", [P, 2 * NJ, 2, 4])
    small = sb("small", [P, 256])
    epsc = sb("epsc", [P, 1])
    rstda = sb("rstda", [P, 8])

    xt_all = sb("xt_all", [P, NT_MAX, D])
    xt = [xt_all[:, t, :] for t in range(NT_MAX)]
    xnT = sb("xnT", [P, 8, TTM], BF16)
    mT = xnT
    arena_g = sb("arena_g", [P, NJ * TTM], BF16)
    gT = arena_g[:, :].rearrange("p (j t) -> p j t", t=TTM)
    ogT = arena_g[:, 0:4 * TTM].rearrange("p (h t) -> p h t", t=TTM)
    obT = arena_g[:, 4 * TTM:8 * TTM].rearrange("p (h t) -> p h t", t=TTM)
    t1 = arena_g[:, 8 * TTM:8 * TTM + NT_MAX * D].rearrange("p (n d) -> p n d", d=D)
    RG = [R("arena_g", j) for j in range(NJ)]
    RogT = RG[0:4]; RobT = RG[4:8]
    nt1 = (NT_MAX * D + TTM - 1) // TTM
    Rt1 = RG[8:8 + nt1]
    slots = [sb("slot%d" % i, [P, 4096], BF16) for i in range(NSLOT)]
    S0f = sb("S0f", [P, 16, P]); S0b = sb("S0b", [P, 16, P], BF16)
    vblk = sb("vblk", [P, 4, P], BF16)
    scs = sb("scs", [P, NJ, 2, 32]); cso = sb("cso", [P, NJ, 2, 32])
    cstg = [sb("cstg%d" % i, [32, 512]) for i in range(2)]

    ARENA_COLS = 11776
    arena = sb("arena", [P, ARENA_COLS])

    class Arena:
        def __init__(self):
            self.off = 0
            self.epoch = 0
            self.cur = []
            self.carry = []

        def reset(self):
            toks = set()
            for r in self.cur:
                if r.writer is not None:
                    toks.add(r.writer)
                toks.update(r.readers)
            toks.update(self.carry)
            best = {}
            for tk in toks:
                key = (tk[0], tk[1])
                if key not in best or best[key][2] < tk[2]:
                    best[key] = tk
            self.carry = list(best.values())
            self.cur = []
            self.off = 0
            self.epoch += 1

        def _res(self):
            r = R("arena", self.epoch, len(self.cur))
            r.readers = list(self.carry)
            self.cur.append(r)
            return r

        def f(self, cols):
            a = arena[:, self.off:self.off + cols]
            self.off += cols
            assert self.off <= ARENA_COLS, self.off
            return a, self._res()

        def b(self, cols):
            n = (cols + 1) // 2
            a = arena[:, self.off:self.off + n].bitcast(BF16)[:, 0:cols]
            self.off += n
            assert self.off <= ARENA_COLS, self.off
            return a, self._res()

    AR = Arena()
    Rsmall = {}

    def rsm(i):
        if i not in Rsmall:
            Rsmall[i] = R("small", i)
        return Rsmall[i]

    E = S.emit

    def load(dst_ap, src_ap, writes, eng="sp", reads=()):
        return E(eng, lambda e: e.dma_start(out=dst_ap, in_=src_ap), reads=reads, writes=writes, dma=True)

    def store(dst_ap, src_ap, reads, eng="sp"):
        return E(eng, lambda e: e.dma_start(out=dst_ap, in_=src_ap), reads=reads, writes=(), dma=True, is_out=True)

    def act(out, in_, func, reads, writes, bias=None, scale=None, accum_out=None):
        kw = {}
        if bias is not None:
            kw["bias"] = bias
        if scale is not None:
            kw["scale"] = scale
        if accum_out is not None:
            kw["accum_out"] = accum_out
        return E("act", lambda e: e.activation(out=out, in_=in_, func=func, **kw), reads=reads, writes=writes)

    def tt(out, in0, in1, op, reads, writes, eng="dve"):
        return E(eng, lambda e: e.tensor_tensor(out=out, in0=in0, in1=in1, op=op), reads=reads, writes=writes)

    def ts(out, in0, s1, s2, op0, op1, reads, writes, eng="dve"):
        if op1 is None:
            return E(eng, lambda e: e.tensor_scalar(out=out, in0=in0, scalar1=s1, scalar2=None, op0=op0),
                     reads=reads, writes=writes)
        return E(eng, lambda e: e.tensor_scalar(out=out, in0=in0, scalar1=s1, scalar2=s2, op0=op0, op1=op1),
                 reads=reads, writes=writes)

    def stt(out, in0, scalar, in1, op0, op1, reads, writes, eng="dve"):
        return E(eng, lambda e: e.scalar_tensor_tensor(out=out, in0=in0, scalar=scalar, in1=in1, op0=op0, op1=op1),
                 reads=reads, writes=writes)

    def mm(out, lhsT, rhs, start, stop, reads, writes):
        return E("pe", lambda e: e.matmul(out, lhsT, rhs, start=start, stop=stop), reads=reads, writes=writes)

    def tr(out, in_, ident, reads, writes):
        return E("pe", lambda e: e.transpose(out, in_, ident), reads=reads, writes=writes)

    def cp(out, in_, reads, writes, eng="dve"):
        return E(eng, lambda e: e.tensor_copy(out=out, in_=in_), reads=reads, writes=writes)

    def rsqrt_eps(out, in_, reads, writes):
        act(out, in_, AF.Ln, reads=list(reads) + [R("epsc")], writes=writes, bias=epsc[:, 0:1])
        return act(out, out, AF.Exp, reads=writes, writes=writes, scale=-0.5)

    def memset(ap, val, writes, eng="dve"):
        return E(eng, lambda e: e.memset(ap, val), reads=(), writes=writes)

    def RC(n):
        return R("c", n)

    for t in range(NT):
        load(xt[t], xp[t * P:(t + 1) * P, :], writes=[R("xt", t)])
    for dst, src in [(identb, c_identb), (identf, c_identf), (maskp, c_maskp), (masks, c_masks),
                     (maskf, c_maskf), (scanp, c_scanp), (scans, c_scans), (bmask, c_bmask),
                     (onesf, c_onesf), (nw, v_nw), (hgw, v_hgw), (lbl, v_lbl), (cw, v_cw), (cb, v_cb)]:
        load(dst[:], src, writes=[R("c", dst.name)])
    for dst, src in [(lnw, v_lnw), (lnb, v_lnb), (fnw, v_fnw), (bsp, v_bsp), (bss, v_bss)]:
        load(dst[:], src.partition_broadcast(P), writes=[R("c", dst.name)])
    memset(epsc[:], EPS, writes=[R("epsc")])
    tt(lbt[:], lbl[:, 0, :], lbl[:, 1, :], ALU.subtract, reads=[R("c", "lbl")], writes=[R("c", "lbt")])
    act(lbt[:], lbt[:], AF.Sigmoid, reads=[R("c", "lbt")], writes=[R("c", "lbt")])
    ts(oml[:], lbt[:], -1.0, 1.0, ALU.mult, ALU.add, reads=[R("c", "lbt")], writes=[R("c", "oml")])
    memset(Sf[:], 0.0, writes=[R("Sf")])
    memset(Sb[:], 0.0, writes=[R("Sb")])
    memset(hh_p[:], 0.0, writes=[R("cr", a, j) for a in range(2) for j in range(NJ)])

    spatial_done = [False]

    def prep_spatial():
        if spatial_done[0]:
            return
        spatial_done[0] = True
        AR.reset()
        wstage, Rwstage = AR.f(512)
        wstage3 = wstage.rearrange("p (g t) -> p g t", t=P)
        for dstw, srcw, msk in [(wsp, v_wsp, maskf), (wss, v_wss, masks)]:
            load(wstage3, srcw, writes=[Rwstage])
            tt(dstw[:], wstage3, msk[:].unsqueeze(1).broadcast_to([P, 4, P]), ALU.mult,
               reads=[Rwstage, R("c", msk.name)], writes=[R("c", dstw.name)])

    ring = {"next": 0}

    def wslot():
        i = ring["next"]
        ring["next"] = (i + 1) % NSLOT
        return i

    class WStream:
        def __init__(self):
            self.order = []
            self.emitted = 0
            self.loaded = {}
            self.index = {}

        def add(self, key, kind, src, k, n):
            self.index[key] = len(self.order)
            self.order.append((key, kind, src, k, n))

        def _emit_one(self):
            key, kind, src, k, n = self.order[self.emitted]
            self.emitted += 1
            i = wslot()
            RW = [R("slot", i), R("slot", i, "b")]
            if kind == "plain":
                view = slots[i][:, 0:k * n].rearrange("p (k n) -> p k n", n=n)
                first_reads = [R("xt", t) for t in range(NT)] if self.emitted == 1 else ()
                load(view, src, writes=RW, eng="pool", reads=first_reads)
            else:
                view = slots[i][:, 0:8 * 256].rearrange("p (k n) -> p k n", n=256)
                load(view, src, writes=RW, eng="pool")
            self.loaded[key] = (view, RW)

        def get(self, key, pf):
            n = self.index[key]
            while self.emitted <= min(n + pf, len(self.order) - 1):
                self._emit_one()
            return self.loaded.pop(key)

    WS = WStream()

    w_in_v = w_in.rearrange("(k p) n -> p k n", p=P)
    w_up_v = w_up.rearrange("(k p) n -> p k n", p=P)
    w_o_v = w_o.rearrange("(k p) n -> p k n", p=P)
    w_pg_v = w_pg.rearrange("(k p) n -> p k n", p=P)
    w_a_v = w_a.rearrange("(k p) n -> p k n", p=P)
    w_b_v = w_b.rearrange("(k p) n -> p k n", p=P)
    w_pp_v = w_pp.rearrange("(k p) n -> p k n", p=P)
    w_dn_v = w_dn.rearrange("(k p) n -> p k n", p=P)

    def register_weights(tag, n_dn_pass=1):
        for nm, c0 in [("q", 0), ("f", 512), ("i", 1024), ("og", 1536), ("u", 2048), ("v", 2560),
                       ("ga0", 3072), ("ga1", 3584)]:
            WS.add((tag, nm), "plain", w_in_v[:, :, c0:c0 + 512], 8, 512)
        WS.add((tag, "wa"), "plain", w_a_v, 4, D)
        WS.add((tag, "gb0"), "plain", w_in_v[:, :, 4096:4608], 8, 512)
        WS.add((tag, "gb1"), "plain", w_in_v[:, :, 4608:5120], 8, 512)
        WS.add((tag, "wb"), "plain", w_b_v, 4, D)
        WS.add((tag, "wo0"), "plain", w_o_v[:, :, 0:512], 8, 512)
        WS.add((tag, "wo1"), "plain", w_o_v[:, :, 512:1024], 8, 512)
        for j in range(NJ):
            WS.add((tag, "up", j), "up", w_up_v[:, :, j * 2 * P:(j + 1) * 2 * P], 8, 256)
        for ps in range(n_dn_pass):
            for q in range(6):
                k0, k1 = q * 4, min(NJ, (q + 1) * 4)
                WS.add((tag, "dn", ps, q), "plain", w_dn_v[:, k0:k1, :], k1 - k0, D)
        WS.add((tag, "pg0"), "plain", w_pg_v[:, :, 0:512], 8, 512)
        WS.add((tag, "pg1"), "plain", w_pg_v[:, :, 512:1024], 8, 512)
        WS.add((tag, "pp"), "plain", w_pp_v, 2, D)

    def norm_transpose_all(nt, nwi):
        AR.reset()
        junk, Rjunk = AR.b(D)
        xnb = [AR.b(D) for _ in range(2)]
        for t in range(nt):
            act(junk, xt[t], AF.Square, reads=[R("xt", t)], writes=[Rjunk, rsm(("ss", t))], accum_out=small[:, t:t + 1],
                scale=float(D) ** -0.5)
        rsqrt_eps(small[:, 8:8 + nt], small[:, 0:nt], reads=[rsm(("ss", t)) for t in range(nt)], writes=[rsm("rs")])
        for t in range(nt):
            xb, Rxb = xnb[t % 2]
            ts(xb, xt[t], small[:, 8 + t:9 + t], None, ALU.mult, None, reads=[R("xt", t), rsm("rs")], writes=[Rxb])
            for j in range(8):
                tr(psT[:, j * P:(j + 1) * P], xb[:, j * P:(j + 1) * P], identb[:],
                   reads=[Rxb, RC("identb")], writes=[RpsT])
            tt(xnT[:, :, t * P:(t + 1) * P], psT[:].rearrange("p (j t) -> p j t", t=P),
               nw[:, nwi, :].unsqueeze(2).broadcast_to([P, 8, P]), ALU.mult,
               reads=[RpsT, RC("nw")], writes=[R("xnT", t)])

    def run_supertile(kinds, sp_idx, preloaded=False, next_x=None):
        nt = len(kinds)
        npt = kinds.count("p")
        has_s = "s" in kinds
        last_prompt = (npt > 0) and (sp_idx == NSP - 1)
        tag = ("st", sp_idx)

        class TP:
            pass

        tps = []
        for t_, kd in enumerate(kinds):
            o = TP()
            o.sample = (kd == "s")
            o.C = LS if o.sample else 64
            o.NCH = P // o.C
            o.mid = o.C // 2 - 1
            o.maskt, o.Rmask = (masks, RC("masks")) if o.sample else (maskp, RC("maskp"))
            o.scant, o.Rscan = (scans, RC("scans")) if o.sample else (scanp, RC("scanp"))
            o.wst, o.Rwst = (wss, RC("wss")) if o.sample else (wsp, RC("wsp"))
            o.bst, o.Rbst = (bss, RC("bss")) if o.sample else (bsp, RC("bsp"))
            if o.sample:
                o.xrows, o.prows, o.yrows = xs[0:P, :], ps_[0:P, :], y_s[0:P, :]
            else:
                r0 = sp_idx * TT + t_ * P
                o.xrows, o.prows, o.yrows = xp[r0:r0 + P, :], pp[r0:r0 + P, :], y_p[r0:r0 + P, :]
            tps.append(o)

        if not preloaded:
            norm_transpose_all(nt, 0)

        Wq, RWq = WS.get((tag, "q"), 2)
        Wf, RWf = WS.get((tag, "f"), 2)
        Wi, RWi = WS.get((tag, "i"), 2)
        Wog, RWog = WS.get((tag, "og"), 2)
        AR.reset()
        pA = [dict(qT=AR.f(512), sk=AR.f(512), sog=AR.f(512), vtm=AR.b(512)) for _ in range(2)]
        pC = [dict(qtl=AR.b(512), ktl=AR.b(512), khT=AR.b(512), qin=AR.b(512)) for _ in range(2)]
        (kT, RkT), (lf, Rlf), (bT, RbT), (bm1, Rbm1), (bm2, Rbm2), (ogf, Rogf), (tmpf, Rtmpf) = \
            [AR.f(512) for _ in range(7)]
        Es = [AR.f(512) for _ in range(4)]
        (khat, Rkhat), (scTm, RscTm) = [AR.b(512) for _ in range(2)]

        def proj_pieces(t):
            tcols = slice(t * P, (t + 1) * P)
            Rxn = R("xnT", t)
            for (W, RW, bank_i) in ((Wq, RWq, 0), (Wf, RWf, 1), (Wog, RWog, 2)):
                for h in range(4):
                    for dk in range(8):
                        mm(banks[bank_i][:, h * P:(h + 1) * P], W[:, dk, h * P:(h + 1) * P], xnT[:, dk, tcols],
                           start=(dk == 0), stop=(dk == 7), reads=[RW, Rxn], writes=[RB[bank_i]])
                yield
            for dk in range(8):
                mm(banks[3][:, :], xnT[:, dk, tcols], Wi[:, dk, :],
                   start=(dk == 0), stop=(dk == 7), reads=[RWi, Rxn], writes=[RB[3]])
            yield

        def evac(t):
            pb = pA[t % 2]
            qT, RqT = pb["qT"]
            act(qT, banks[0][:], AF.Sigmoid, reads=[RB[0]], writes=[RqT])
            sk, Rsk = pb["sk"]
            act(sk, banks[1][:], AF.Sigmoid, reads=[RB[1]], writes=[Rsk], scale=-1.0)
            sog, Rsog = pb["sog"]
            act(sog, banks[2][:], AF.Sigmoid, reads=[RB[2]], writes=[Rsog])
            vtm, Rvtm = pb["vtm"]
            act(vtm, banks[3][:], AF.Copy, reads=[RB[3]], writes=[Rvtm])
            tt(qT, banks[0][:], qT, ALU.mult, reads=[RB[0], RqT], writes=[RqT])

        def chain(t):
            o = tps[t]
            C, NCH, mid, scant, Rscan = o.C, o.NCH, o.mid, o.scant, o.Rscan
            pb, pc = pA[t % 2], pC[t % 2]
            (qT, RqT), (sk, Rsk) = pb["qT"], pb["sk"]
            (qtl, Rqtl), (ktl, Rktl), (khT, RkhT), (qin, Rqin) = pc["qtl"], pc["ktl"], pc["khT"], pc["qin"]
            (E0, RE0), (E1, RE1), (E2, RE2), (E3, RE3) = Es
            ebend = small[:, 128 + (t % 2) * 64:128 + (t % 2) * 64 + 4 * NCH]
            Reb = rsm(("eb", t % 2))
            tt(kT.rearrange("p (h t) -> p h t", t=P), sk.rearrange("p (h t) -> p h t", t=P),
               oml[:].unsqueeze(2).broadcast_to([P, 4, P]), ALU.mult, reads=[Rsk, RC("oml")], writes=[RkT])
            act(lf, kT, AF.Ln, reads=[RkT], writes=[Rlf], scale=-1.0, bias=1.0)
            yield
            E("dve", lambda e: e.tensor_tensor_scan(out=bT, data0=scant[:], data1=lf, initial=0.0,
                                                    op0=ALU.mult, op1=ALU.add),
              reads=[Rscan, Rlf], writes=[RbT])
            yield
            b4 = bT.rearrange("p (h n c) -> p h n c", h=4, c=C)
            bm14 = bm1.rearrange("p (h n c) -> p h n c", h=4, c=C)
            bm24 = bm2.rearrange("p (h n c) -> p h n c", h=4, c=C)
            act(E0, bT, AF.Exp, reads=[RbT], writes=[RE0])
            tt(bm14, b4, b4[:, :, :, mid:mid + 1].broadcast_to([P, 4, NCH, C]), ALU.subtract,
               reads=[RbT], writes=[Rbm1])
            yield
            act(ebend.rearrange("p (h n) -> p h n", h=4), b4[:, :, :, C - 1], AF.Exp, reads=[RbT], writes=[Reb])
            tt(bm24, b4[:, :, :, C - 1:C].broadcast_to([P, 4, NCH, C]), b4, ALU.subtract,
               reads=[RbT], writes=[Rbm2])
            yield
            act(E1, bm1, AF.Exp, reads=[Rbm1], writes=[RE1])
            tt(qin, qT, E0, ALU.mult, reads=[RqT, RE0], writes=[Rqin], eng="pool")
            yield
            act(E2, bm1, AF.Exp, reads=[Rbm1], writes=[RE2], scale=-1.0)
            tt(qtl, qT, E1, ALU.mult, reads=[RqT, RE1], writes=[Rqtl])
            yield
            act(E3, bm2, AF.Exp, reads=[Rbm2], writes=[RE3])
            tt(ktl, kT, E2, ALU.mult, reads=[RkT, RE2], writes=[Rktl], eng="pool")
            yield
            tt(khT, kT, E3, ALU.mult, reads=[RkT, RE3], writes=[RkhT])
            yield

        def seq(t):
            o = tps[t]
            sample, C, NCH, maskt, Rmask = o.sample, o.C, o.NCH, o.maskt, o.Rmask
            tcols = slice(t * P, (t + 1) * P)
            pb, pc = pA[t % 2], pC[t % 2]
            (sog, Rsog), (vtm, Rvtm) = pb["sog"], pb["vtm"]
            (qtl, Rqtl), (ktl, Rktl), (khT, RkhT), (qin, Rqin) = pc["qtl"], pc["ktl"], pc["khT"], pc["qin"]
            ebend = small[:, 128 + (t % 2) * 64:128 + (t % 2) * 64 + 4 * NCH]
            Reb = rsm(("eb", t % 2))
            for h in range(4):
                tr(psT[:, h * P:(h + 1) * P], khT[:, h * P:(h + 1) * P], identb[:],
                   reads=[RkhT, RC("identb")], writes=[RpsT])
            for h in range(4):
                hs = slice(h * P, (h + 1) * P)
                mm(banks[4][:, hs], ktl[:, hs], qtl[:, hs], start=True, stop=True,
                   reads=[Rktl, Rqtl], writes=[RB[4]])
            act(khat, psT[:, 0:512], AF.Copy, reads=[RpsT], writes=[Rkhat])
            tt(scTm.rearrange("p (h t) -> p h t", t=P), banks[4][:].rearrange("p (h t) -> p h t", t=P),
               maskt[:].unsqueeze(1).broadcast_to([P, 4, P]), ALU.mult, reads=[RB[4], Rmask], writes=[RscTm])
            yield
            if not sample:
                for c in range(NCH):
                    cs_ = slice(c * C, (c + 1) * C)
                    for h in range(4):
                        osl = slice(h * P + c * C, h * P + (c + 1) * C)
                        mm(banks[5][:, osl], vtm[cs_, h * P:(h + 1) * P], scTm[cs_, osl],
                           start=True, stop=False, reads=[Rvtm, RscTm], writes=[RB[5]])
                        mm(banks[5][:, osl], Sb[:, h, :], qin[:, osl], start=False, stop=True,
                           reads=[R("Sb"), Rqin], writes=[RB[5]])
                    for h in range(4):
                        hs = slice(h * P, (h + 1) * P)
                        mm(banks[6][:, hs], khat[cs_, hs], vtm[cs_, hs], start=True, stop=True,
                           reads=[Rkhat, Rvtm], writes=[RB[6]])
                    yield
                    eb = ebend.rearrange("p (h n) -> p h n", h=4)[:, :, c:c + 1].broadcast_to([P, 4, P])
                    tt(Sf[:], Sf[:], eb, ALU.mult, reads=[R("Sf"), Reb], writes=[R("Sf")])
                    tt(Sf[:], Sf[:], banks[6][:].rearrange("p (h v) -> p h v", v=P), ALU.add,
                       reads=[R("Sf"), RB[6]], writes=[R("Sf")])
                    act(Sb[:], Sf[:], AF.Copy, reads=[R("Sf")], writes=[R("Sb")])
                    yield
            else:
                first = [True]

                def mmo(out, lhsT, rhs, last, reads):
                    st = first[0]
                    first[0] = False
                    E("pe", lambda e: e.matmul(out, lhsT, rhs, start=st, stop=last, skip_group_check=True),
                      reads=reads, writes=[RB[5]])

                for h in range(4):
                    hs = slice(h * P, (h + 1) * P)
                    mmo(banks[5][:, hs], vtm[:, hs], scTm[:, hs], False, [Rvtm, RscTm])
                eb3 = ebend.rearrange("p (h n) -> p h n", h=4)
                def ld_state(g):
                    hb = g % 2
                    load(S0f[:, hb * 8:(hb + 1) * 8, :], st_h[g * 2:(g + 1) * 2].rearrange("j h k v -> k (j h) v"),
                         writes=[R("S0f", hb)])

                ld_state(0)
                for g in range(8):
                    hb = g % 2
                    S0fh = S0f[:, hb * 8:(hb + 1) * 8, :]
                    S0bh = S0b[:, hb * 8:(hb + 1) * 8, :]
                    RS0f, RS0b = R("S0f", hb), R("S0b", hb)
                    if g + 1 < 8:
                        ld_state(g + 1)
                    act(S0bh, S0fh, AF.Copy, reads=[RS0f], writes=[RS0b])
                    for jj in range(2):
                        j = g * 2 + jj
                        for h in range(4):
                            osl = slice(h * P + j * LS, h * P + (j + 1) * LS)
                            mmo(banks[5][:, osl], S0bh[:, jj * 4 + h, :], qin[:, osl],
                                (g == 7 and jj == 1 and h == 3), [RS0b, Rqin])
                    for h in range(4):
                        hs = slice(h * P, (h + 1) * P)
                        vi = (g * 4 + h) % 2
                        vb = vblk[:, vi * 2:(vi + 1) * 2, :]
                        Rvb = R("vblk", vi)
                        bk = 6 if vi == 0 else 4
                        tt(vb, vtm[:, hs].unsqueeze(1).broadcast_to([P, 2, P]),
                           bmask[:, g * 2:(g + 1) * 2].unsqueeze(2).broadcast_to([P, 2, P]), ALU.mult,
                           reads=[Rvtm, RC("bmask")], writes=[Rvb])
                        mm(banks[bk][:, 0:2 * P], khat[:, hs], vb.rearrange("p j v -> p (j v)"),
                           start=True, stop=True, reads=[Rkhat, Rvb], writes=[RB[bk]])
                        for jj in range(2):
                            j = g * 2 + jj
                            stt(S0fh[:, jj * 4 + h, :], S0fh[:, jj * 4 + h, :], eb3[:, h, j:j + 1],
                                banks[bk][:, jj * P:(jj + 1) * P], ALU.mult, ALU.add,
                                reads=[RS0f, Reb, RB[bk]], writes=[RS0f])
                    store(o_hs[g * 2:(g + 1) * 2].rearrange("j h k v -> k (j h) v"), S0fh, reads=[RS0f])
                    yield
            tt(ogf, banks[5][:], sog, ALU.mult, reads=[RB[5], Rsog], writes=[Rogf])
            act(tmpf, ogf, AF.Square, reads=[Rogf], writes=[Rtmpf], scale=512.0 ** -0.5)
            yield
            for h in range(4):
                mm(banks[4][:, 0:1], tmpf[:, h * P:(h + 1) * P], onesf[:, 0:1], start=(h == 0), stop=(h == 3),
                   reads=[Rtmpf, RC("onesf")], writes=[RB[4]])
            cp(small[:, 16 + t:17 + t], banks[4][:, 0:1], reads=[RB[4]], writes=[rsm(("ssa", t))])
            tt(ogT[:, :, tcols], ogf.rearrange("p (h t) -> p h t", t=P),
               hgw[:].unsqueeze(2).broadcast_to([P, 4, P]), ALU.mult, reads=[Rogf, RC("hgw")],
               writes=[R("ogT", t)] + RogT)
            yield

        def interleave(*gens):
            gens = [g for g in gens if g is not None]
            while gens:
                for g in list(gens):
                    try:
                        next(g)
                    except StopIteration:
                        gens.remove(g)

        interleave(proj_pieces(0))
        if pending_late[0] is not None:
            pending_late[0]()
            pending_late[0] = None
        evac(0)
        for k in range(nt + 1):
            interleave(chain(k) if k < nt else None,
                       seq(k - 1) if k >= 1 else None,
                       proj_pieces(k + 1) if k + 1 < nt else None)
            if k + 1 < nt:
                evac(k + 1)
        rsqrt_eps(rstda[:, 0:nt], small[:, 16:16 + nt], reads=[rsm(("ssa", t)) for t in range(nt)],
                  writes=[R("rstda")])
        if last_prompt:
            store(o_hp.rearrange("h k v -> k h v"), Sf[:], reads=[R("Sf")])

        prep_spatial()
        Wu, RWu = WS.get((tag, "u"), 3)
        Wv, RWv = WS.get((tag, "v"), 3)
        AR.reset()
        uTall, RuTall = AR.f(4 * TTM)
        uT4 = uTall.rearrange("p (g t) -> p g t", t=TTM)
        RuTt = [AR._res() for _ in range(nt)]
        vgs = [AR.f(512) for _ in range(nt)]
        vns = [AR.f(512) for _ in range(2)]
        tmps = [AR.f(512) for _ in range(2)]
        vvs = [AR.b(512) for _ in range(2)]
        sgs = [AR.f(512) for _ in range(2)]
        (tmpf, Rtmpf) = tmps[0]
        ugroups = []
        if npt > 0:
            ugroups.append((slice(0, npt * P), list(range(npt))))
        if has_s:
            ugroups.append((slice(npt * P, (npt + 1) * P), [npt]))
        for (ucols, utiles) in ugroups:
            nU = ucols.stop - ucols.start
            for g in range(4):
                for dk in range(8):
                    mm(banks[g][:, 0:nU], Wu[:, dk, g * P:(g + 1) * P], xnT[:, dk, ucols],
                       start=(dk == 0), stop=(dk == 7), reads=[RWu] + [R("xnT", q) for q in utiles],
                       writes=[RB[g]])
                act(uT4[:, g, ucols], banks[g][:, 0:nU], AF.Gelu, reads=[RB[g]],
                    writes=[RuTt[q] for q in utiles])
        for t in range(nt):
            tcols = slice(t * P, (t + 1) * P)
            Rxn = R("xnT", t)
            (vg, Rvg) = vgs[t]
            bv = 4 + (t % 2)
            for dk in range(8):
                mm(banks[bv][:, :], xnT[:, dk, tcols], Wv[:, dk, :], start=(dk == 0), stop=(dk == 7),
                   reads=[RWv, Rxn], writes=[RB[bv]])
            act(vg, banks[bv][:], AF.Gelu, reads=[RB[bv]], writes=[Rvg, rsm(("s1", t))],
                accum_out=small[:, 40 + t:41 + t])
            act(tmpf, vg, AF.Square, reads=[Rvg], writes=[Rtmpf, rsm(("s2", t))], accum_out=small[:, 48 + t:49 + t])
        s1 = small[:, 40:40 + nt]; s2 = small[:, 48:48 + nt]
        mean = small[:, 56:56 + nt]; msq = small[:, 64:64 + nt]; var = small[:, 72:72 + nt]
        rs = small[:, 80:80 + nt]; nmr = small[:, 88:88 + nt]
        ts(mean, s1, 1.0 / 512, None, ALU.mult, None, reads=[rsm(("s1", t)) for t in range(nt)], writes=[rsm("mean")])
        tt(msq, mean, mean, ALU.mult, reads=[rsm("mean")], writes=[rsm("msq")])
        stt(var, s2, 1.0 / 512, msq, ALU.mult, ALU.subtract,
            reads=[rsm(("s2", t)) for t in range(nt)] + [rsm("msq")], writes=[rsm("var")])
        rsqrt_eps(rs, var, reads=[rsm("var")], writes=[rsm("lrs")])
        stt(nmr, mean, -1.0, rs, ALU.mult, ALU.mult, reads=[rsm("mean"), rsm("lrs")], writes=[rsm("nmr")])

        Wga0, RWga0 = WS.get((tag, "ga0"), 2)
        Wga1, RWga1 = WS.get((tag, "ga1"), 2)
        Wa, RWa = WS.get((tag, "wa"), 2)
        Wga = [(Wga0, RWga0), (Wga1, RWga1)]
        unit = [0]

        def m1_pe(t, half):
            tcols = slice(t * P, (t + 1) * P)
            u = unit[0]
            bg, ba = (u % 2) * 2, (u % 2) * 2 + 1
            W, RW = Wga[half]
            for dk in range(8):
                mm(bank_ap[bg], xnT[:, dk, tcols], W[:, dk, :], start=(dk == 0), stop=(dk == 7),
                   reads=[RW, R("xnT", t)], writes=[RBall[bg]])
            for h in range(4):
                mm(bank_ap[ba], ogT[:, h, tcols], Wa[:, h, half * 512:(half + 1) * 512],
                   start=(h == 0), stop=(h == 3), reads=[RWa, R("ogT", t)] + RogT, writes=[RBall[ba]])
            unit[0] += 1
            return bg, ba, u

        def m1_ew(t, half, bg, ba, u):
            hsl = slice(half * 512, (half + 1) * 512)
            sga, Rsga = sgs[u % 2]
            act(sga, bank_ap[bg], AF.Sigmoid, reads=[RBall[bg]], writes=[Rsga])
            stt(t1[:, t, hsl], bank_ap[ba], rstda[:, t:t + 1], sga, ALU.mult, ALU.mult,
                reads=[RBall[ba], R("rstda"), Rsga], writes=[R("t1", t)] + Rt1)

        for t in range(nt):
            tcols = slice(t * P, (t + 1) * P)
            (vg, Rvg) = vgs[t]
            RuT = RuTt[t]
            (vn, Rvn) = vns[t % 2]
            (tmpf, Rtmpf) = tmps[t % 2]
            (vv, Rvv) = vvs[t % 2]
            sample, wst, Rwst, bst, Rbst = tps[t].sample, tps[t].wst, tps[t].Rwst, tps[t].bst, tps[t].Rbst
            u0 = m1_pe(t, 0)
            act(vn, vg, AF.Identity, reads=[Rvg, rsm("lrs"), rsm("nmr")], writes=[Rvn],
                scale=rs[:, t:t + 1], bias=nmr[:, t:t + 1])
            tt(vn, vn, lnw[:], ALU.mult, reads=[Rvn, RC("lnw")], writes=[Rvn])
            tt(vn, vn, lnb[:], ALU.add, reads=[Rvn, RC("lnb")], writes=[Rvn])
            act(vv, vn, AF.Copy, reads=[Rvn], writes=[Rvv])
            if sample:
                store(o_gv, vn, reads=[Rvn])
            m1_ew(t, 0, *u0)
            u1 = m1_pe(t, 1)
            for g in range(4):
                gs = slice(g * P, (g + 1) * P)
                mm(banks[6][:, gs], vv[:, gs], wst[:, g, :], start=True, stop=True,
                   reads=[Rvv, Rwst], writes=[RB[6]])
            tt(tmpf, banks[6][:], bst[:], ALU.add, reads=[RB[6], Rbst], writes=[Rtmpf])
            tt(obT[:, :, tcols], tmpf.rearrange("p (g t) -> p g t", t=P), uT4[:, :, tcols],
               ALU.mult, reads=[Rtmpf, RuT], writes=[R("obT", t)] + RobT)
            m1_ew(t, 1, *u1)

        Wgb0, RWgb0 = WS.get((tag, "gb0"), 2)
        Wgb1, RWgb1 = WS.get((tag, "gb1"), 2)
        Wb, RWb = WS.get((tag, "wb"), 2)
        AR.reset()
        sgs2 = [AR.f(D) for _ in range(2)]
        t2s = [AR.f(D) for _ in range(2)]
        mbs = [AR.b(D) for _ in range(2)]
        Rsgb2 = [[AR._res() for _ in range(2)] for _ in range(2)]
        Rt22 = [[AR._res() for _ in range(2)] for _ in range(2)]

        def m2_front(t):
            tcols = slice(t * P, (t + 1) * P)
            Rxn = R("xnT", t)
            sgb, Rsgb = sgs2[t % 2]
            t2, Rt2 = t2s[t % 2]
            mb, Rmb = mbs[t % 2]
            for half, (W, RW) in enumerate([(Wgb0, RWgb0), (Wgb1, RWgb1)]):
                bg = (t % 2) * 2 + half
                for dk in range(8):
                    mm(bank_ap[bg], xnT[:, dk, tcols], W[:, dk, :], start=(dk == 0), stop=(dk == 7),
                       reads=[RW, Rxn], writes=[RBall[bg]])
            for half in range(2):
                ba = 4 + half
                for h in range(4):
                    mm(bank_ap[ba], obT[:, h, tcols], Wb[:, h, half * 512:(half + 1) * 512],
                       start=(h == 0), stop=(h == 3), reads=[RWb, R("obT", t)] + RobT, writes=[RBall[ba]])
            for half in range(2):
                bg = (t % 2) * 2 + half
                hsl = slice(half * 512, (half + 1) * 512)
                act(sgb[:, hsl], bank_ap[bg], AF.Sigmoid, reads=[RBall[bg]], writes=[Rsgb2[t % 2][half]])
            for half in range(2):
                ba = 4 + half
                hsl = slice(half * 512, (half + 1) * 512)
                tt(t2[:, hsl], bank_ap[ba], sgb[:, hsl], ALU.mult, reads=[RBall[ba], Rsgb2[t % 2][half]],
                   writes=[Rt22[t % 2][half]])
            tt(mb, t1[:, t, :], t2, ALU.add, reads=[R("t1", t)] + Rt22[t % 2] + Rt1, writes=[Rmb])

        def m2_back(t):
            tcols = slice(t * P, (t + 1) * P)
            mb, Rmb = mbs[t % 2]
            for j in range(8):
                tr(psT[:, j * P:(j + 1) * P], mb[:, j * P:(j + 1) * P], identb[:],
                   reads=[Rmb, RC("identb")], writes=[RpsT])
            act(mT[:, :, tcols], psT[:].rearrange("p (j t) -> p j t", t=P), AF.Copy,
                reads=[RpsT], writes=[R("xnT", t)])

        for t in range(nt + 1):
            if t < nt:
                m2_front(t)
            if t >= 1:
                m2_back(t - 1)

        Wo0, RWo0 = WS.get((tag, "wo0"), 3)
        Wo1, RWo1 = WS.get((tag, "wo1"), 3)
        AR.reset()
        junk, Rjunk = AR.b(D)
        xnb = [AR.b(D) for _ in range(2)]

        def m3_front(t):
            tcols = slice(t * P, (t + 1) * P)
            for half, (W, RW) in enumerate([(Wo0, RWo0), (Wo1, RWo1)]):
                bo = (t % 2) * 2 + half
                hsl = slice(half * 512, (half + 1) * 512)
                for dk in range(8):
                    mm(bank_ap[bo], mT[:, dk, tcols], W[:, dk, :], start=(dk == 0), stop=(dk == 7),
                       reads=[RW, R("xnT", t)], writes=[RBall[bo]])
                tt(xt[t][:, hsl], xt[t][:, hsl], bank_ap[bo], ALU.add,
                   reads=[R("xt", t), RBall[bo]], writes=[R("xt", t)])

        def ffn_norm(t):
            act(junk, xt[t], AF.Square, reads=[R("xt", t)], writes=[Rjunk, rsm(("ss", t))], accum_out=small[:, t:t + 1],
                scale=float(D) ** -0.5)
            rsqrt_eps(small[:, 8 + t:9 + t], small[:, t:t + 1], reads=[rsm(("ss", t))], writes=[rsm(("rs1", t))])
            xb, Rxb = xnb[t % 2]
            ts(xb, xt[t], small[:, 8 + t:9 + t], None, ALU.mult, None, reads=[R("xt", t), rsm(("rs1", t))],
               writes=[Rxb])
            for j in range(8):
                tr(psT[:, j * P:(j + 1) * P], xb[:, j * P:(j + 1) * P], identb[:],
                   reads=[Rxb, RC("identb")], writes=[RpsT])
            tt(xnT[:, :, t * P:(t + 1) * P], psT[:].rearrange("p (j t) -> p j t", t=P),
               nw[:, 1, :].unsqueeze(2).broadcast_to([P, 8, P]), ALU.mult,
               reads=[RpsT, RC("nw")], writes=[R("xnT", t)])

        for t in range(nt + 1):
            if t < nt:
                m3_front(t)
            if t >= 1:
                ffn_norm(t - 1)

        if has_s:
            for g in range(11):
                stg = cstg[g % 2]; Rstg = R("cstg", g % 2)
                load(stg[:], st_c[:, g * 512:(g + 1) * 512], writes=[Rstg])
                for jj in range(4):
                    tr(banks[g % 2][:, jj * 32:(jj + 1) * 32], stg[0:32, jj * P:(jj + 1) * P], identf[0:32, 0:32],
                       reads=[Rstg, RC("identf")], writes=[RB[g % 2]])
                for jj in range(4):
                    c = g * 4 + jj
                    act(scs[:, c % NJ, c // NJ, :], banks[g % 2][:, jj * 32:(jj + 1) * 32], AF.Copy,
                        reads=[RB[g % 2]], writes=[R("scs")])
        RS0fa = [R("S0f", 0), R("S0f", 1)]
        RS0ba = [R("S0b", 0), R("S0b", 1)]
        pT_ = []
        for t in range(nt):
            pT_.append((S0b[:, 2 * t:2 * t + 2, :].rearrange("p a b -> p (a b)"), RS0ba))

        def p_load(t):
            ptile = S0f[:, 2 * t:2 * t + 2, :].rearrange("p a b -> p (a b)")
            pbf = S0f[:, 10 + t, :].bitcast(BF16)
            load(ptile, tps[t].prows, writes=RS0fa)
            act(pbf, ptile, AF.Copy, reads=RS0fa, writes=RS0fa)

        def p_tr(t):
            pbf = S0f[:, 10 + t, :].bitcast(BF16)
            pTb = pT_[t][0]
            for kk in range(2):
                tr(psT[:, kk * P:(kk + 1) * P], pbf[:, kk * P:(kk + 1) * P], identb[:],
                   reads=RS0fa + [RC("identb")], writes=[RpsT])
            act(pTb, psT[:, 0:256], AF.Copy, reads=[RpsT], writes=RS0ba)

        groups = []
        if npt > 0:
            groups.append((slice(0, npt * P), list(range(npt)), False, 1, npt * P))
        if has_s:
            groups.append((slice(npt * P, (npt + 1) * P), [npt], True, NSEQ_S, LS))
        AR.reset()
        HS_ = {False: 512, True: P}
        npar_k = {False: (2 if has_s else 3), True: 2}
        yall_k = {False: [AR.f(1024)[0] for _ in range(npar_k[False])], True: [AR.f(2 * P)[0] for _ in range(2)]}
        Ryh_k = {kk_: [[AR._res() for _ in range(2)] for _ in range(npar_k[kk_])] for kk_ in (False, True)}
        Ryb_k = {kk_: [[AR._res() for _ in range(2)] for _ in range(npar_k[kk_])] for kk_ in (False, True)}
        bank0_k = {False: 0, True: 4}
        hhs = [AR.f(2 * NSEQ_S * 4)[0] for _ in range(2)] if has_s else None
        ptmp = [[AR.f(2) for _ in range(2)] for _ in range(3)]
        Rhhs = [[AR._res() for _ in range(2)] for _ in range(2)]
        it = 0
        pending_tail = [None]
        for j in range(NJ):
            Wup, RWup2 = WS.get((tag, "up", j), 4)
            if 1 <= j < 1 + nt:
                p_load(j - 1)
            if 8 <= j < 8 + nt:
                p_tr(j - 8)
            for (gcols, gtiles, sample, NS_, L_) in groups:
                TG = NS_ * L_
                Rxg = [R("xnT", q) for q in gtiles]
                par = j % npar_k[sample]
                yall, Ryh, Ryb, HS, b0 = yall_k[sample], Ryh_k[sample], Ryb_k[sample], HS_[sample], bank0_k[sample]
                y4 = yall[par].rearrange("p (a c) -> p a c", a=2)[:, :, 0:NS_ * L_].rearrange(
                    "p a (s l) -> p a s l", l=L_)
                if sample:
                    hh4 = hhs[par].rearrange("p (a s r) -> p a s r", a=2, r=4)
                    Rhh = Rhhs[par]
                    cp(hh4[:, :, :, 0:2], scs[:, j, :, :].rearrange("p a (s r) -> p a s r", r=2),
                       reads=[R("scs")], writes=Rhh)
                else:
                    hh4 = hh_p[:, (sp_idx % 2) * NJ + j, :, :].unsqueeze(2)
                    Rhh = [R("cr", sp_idx % 2, j)] * 2
                pvs = []
                for half in range(2):
                    bk = b0 + par * 2 + half
                    for dk in range(8):
                        mm(bank_ap[bk][:, 0:TG], Wup[:, dk, half * P:(half + 1) * P], xnT[:, dk, gcols],
                           start=(dk == 0), stop=(dk == 7), reads=[RWup2[half]] + Rxg, writes=[RBall[bk]])
                    pvs.append(bank_ap[bk][:, 0:NS_ * L_].rearrange("p (s l) -> p s l", l=L_))
                for half in range(2):
                    bk = b0 + par * 2 + half
                    act(y4[:, half], pvs[half], AF.Identity, reads=[RBall[bk], RC("cw"), RC("cb")],
                        writes=[Ryh[par][half], Ryb[par][half]], scale=cw[:, j, half, 2:3], bias=cb[:, j, half:half + 1])
                    act(hh4[:, half, :, 2:4], pvs[half][:, :, 0:2], AF.Copy, reads=[RBall[bk]], writes=[Rhh[half]])
                    if sample:
                        act(cso[:, j, half, :].rearrange("p (s r) -> p s r", r=2), pvs[half][:, :, L_ - 2:L_],
                            AF.Copy, reads=[RBall[bk]], writes=[R("cso")])
                    else:
                        act(hh_p[:, ((sp_idx + 1) % 2) * NJ + j, half, 0:2].unsqueeze(1), pvs[half][:, :, L_ - 2:L_],
                            AF.Copy, reads=[RBall[bk]], writes=[R("cr", (sp_idx + 1) % 2, j)])
                for half in range(2):
                    bk = b0 + par * 2 + half
                    for k in (1, 0):
                        stt(y4[:, half, :, 2:L_], pvs[half][:, :, k:L_ - 2 + k], cw[:, j, half, k:k + 1],
                            y4[:, half, :, 2:L_], ALU.mult, ALU.add,
                            reads=[RBall[bk], RC("cw"), Ryb[par][half]], writes=[Ryb[par][half]])
                        if k == 0 and not sample:
                            pt, Rpt = ptmp[par][half]
                            ts(pt, hh4[:, half, 0, 0:2], cw[:, j, half, 0:1], None, ALU.mult, None,
                               reads=[Rhh[half], RC("cw")], writes=[Rpt], eng="pool")
                            tt(y4[:, half, 0, 0:2], y4[:, half, 0, 0:2], pt, ALU.add,
                               reads=[Rpt, Ryh[par][half]], writes=[Ryh[par][half]], eng="pool")
                        else:
                            stt(y4[:, half, :, 0:2], hh4[:, half, :, k:k + 2], cw[:, j, half, k:k + 1],
                                y4[:, half, :, 0:2], ALU.mult, ALU.add,
                                reads=[Rhh[half], RC("cw"), Ryh[par][half]], writes=[Ryh[par][half]])

                def tail(j=j, par=par, gcols=gcols, TG=TG, yall=yall, Ryh=Ryh, Ryb=Ryb, HS=HS):
                    ya = yall[par][:, 0:TG]
                    yb = yall[par][:, HS:HS + TG]
                    act(ya, ya, AF.Gelu, reads=[Ryh[par][0], Ryb[par][0]], writes=[Ryh[par][0], Ryb[par][0]])
                    tt(gT[:, j, gcols], ya, yb, ALU.mult,
                       reads=[Ryh[par][0], Ryb[par][0], Ryh[par][1], Ryb[par][1]], writes=[RG[j]], eng="pool")

                if pending_tail[0] is not None:
                    pending_tail[0]()
                pending_tail[0] = tail
                it += 1
        if pending_tail[0] is not None:
            pending_tail[0]()
            pending_tail[0] = None
        def conv_out_groups(bank_ids):
            outs_ = []
            if has_s:
                outs_.append(True)
            if last_prompt:
                outs_.append(False)
            gidx = 0
            for sample in outs_:
                NR = 32 if sample else 2
                odst = o_cs if sample else o_cp
                for g in range(11):
                    bk = bank_ids[gidx % 2]
                    stg = cstg[gidx % 2]; Rstg = R("cstg", gidx % 2)
                    gidx += 1
                    for jj in range(4):
                        c = g * 4 + jj
                        if sample:
                            srcap = cso[:, c % NJ, c // NJ, :]
                            Rsrc = [R("cso")]
                        else:
                            srcap = hh_p[:, (NSP % 2) * NJ + c % NJ, c // NJ, 0:2]
                            Rsrc = [R("cr", NSP % 2, c % NJ)]
                        tr(bank_ap[bk][0:NR, jj * P:(jj + 1) * P], srcap, identf[:],
                           reads=Rsrc + [RC("identf")], writes=[RBall[bk]])
                    act(stg[0:NR, :], bank_ap[bk][0:NR, :], AF.Copy, reads=[RBall[bk]], writes=[Rstg])
                    store(odst[:, g * 512:(g + 1) * 512], stg[0:NR, :], reads=[Rstg])
                    yield

        if nt <= 4:
            passes = [list(range(nt))]
            cgen = conv_out_groups([0, 1])
            for _ in cgen:
                pass
            cgen = None
        else:
            passes = [list(range(3)), list(range(3, nt))]
            cgen = conv_out_groups([6, 7])
        n_pass = len(passes)
        dn_loaded = {}
        for ps, ptiles in enumerate(passes):
            for q in range(6):
                kcs = list(range(q * 4, min(NJ, (q + 1) * 4)))
                if ps == 0:
                    dn_loaded[q] = WS.get((tag, "dn", 0, q), 4 if n_pass == 1 else 5 - q)
                Wd, RWd = dn_loaded[q]
                for ci, kc in enumerate(kcs):
                    for t in ptiles:
                        tcols = slice(t * P, (t + 1) * P)
                        for half in range(2):
                            bk = (t - ptiles[0]) * 2 + half
                            mm(bank_ap[bk], gT[:, kc, tcols], Wd[:, ci, half * 512:(half + 1) * 512],
                               start=(kc == 0), stop=(kc == NJ - 1), reads=[RWd, RG[kc]], writes=[RBall[bk]])
                    if cgen is not None and ps == 0:
                        try:
                            next(cgen)
                        except StopIteration:
                            cgen = None
            for t in ptiles:
                for half in range(2):
                    bk = (t - ptiles[0]) * 2 + half
                    hsl = slice(half * 512, (half + 1) * 512)
                    tt(xt[t][:, hsl], xt[t][:, hsl], bank_ap[bk], ALU.add, reads=[R("xt", t), RBall[bk]],
                       writes=[R("xt", t)])
        if cgen is not None:
            for _ in cgen:
                pass

        Wg0, RWg0 = WS.get((tag, "pg0"), 3)
        Wg1, RWg1 = WS.get((tag, "pg1"), 3)
        Wp, RWp = WS.get((tag, "pp"), 3)
        AR.reset()
        xnbP = [AR.b(D) for _ in range(2)]
        for t in range(nt):
            xb, Rxb = xnbP[t % 2]
            cp(xb, xt[t], reads=[R("xt", t)], writes=[Rxb])
            for j in range(8):
                tr(psT[:, j * P:(j + 1) * P], xb[:, j * P:(j + 1) * P], identb[:],
                   reads=[Rxb, RC("identb")], writes=[RpsT])
            tt(xnT[:, :, t * P:(t + 1) * P], psT[:].rearrange("p (j t) -> p j t", t=P),
               nw[:, 2, :].unsqueeze(2).broadcast_to([P, 8, P]), ALU.mult,
               reads=[RpsT, RC("nw")], writes=[R("xnT", t)])
        sg_ = [AR.f(D) for _ in range(2)]
        yb_ = [AR.f(D) for _ in range(2)]
        junk, Rjunk = AR.b(D)
        xnb2 = [AR.b(D) for _ in range(2)]
        for t in range(nt):
            act(junk, xt[t], AF.Square, reads=[R("xt", t)], writes=[Rjunk, rsm(("pss", t))],
                accum_out=small[:, 24 + t:25 + t], scale=float(D) ** -0.5)
        rsqrt_eps(small[:, 32:32 + nt], small[:, 24:24 + nt], reads=[rsm(("pss", t)) for t in range(nt)],
                  writes=[rsm("prs")])

        def p4_front(t):
            tcols = slice(t * P, (t + 1) * P)
            pTb, RpT = pT_[t]
            for half, (W, RW) in enumerate([(Wg0, RWg0), (Wg1, RWg1)]):
                hsl = slice(half * 512, (half + 1) * 512)
                bg = (t % 2) * 4 + half
                bp = (t % 2) * 4 + 2 + half
                for dk in range(8):
                    mm(bank_ap[bg], xnT[:, dk, tcols], W[:, dk, :], start=(dk == 0), stop=(dk == 7),
                       reads=[RW, R("xnT", t)], writes=[RBall[bg]])
                for kk in range(2):
                    mm(bank_ap[bp], pTb[:, kk * P:(kk + 1) * P], Wp[:, kk, hsl], start=(kk == 0), stop=(kk == 1),
                       reads=[RWp, RpT], writes=[RBall[bp]])

        hf_ = [AR.f(D) for _ in range(nt)]
        n_next = len(next_x) if next_x is not None else 0

        Rsgh = [[AR._res() for _ in range(2)] for _ in range(2)]
        Rtmph = [[AR._res() for _ in range(2)] for _ in range(2)]

        def p4_back_a(t):
            sg, _ = sg_[t % 2]
            tmp, _ = yb_[t % 2]
            hf, Rhf = hf_[t]
            for half in range(2):
                hsl = slice(half * 512, (half + 1) * 512)
                bg = (t % 2) * 4 + half
                act(sg[:, hsl], bank_ap[bg], AF.Sigmoid, reads=[RBall[bg], rsm("prs")], writes=[Rsgh[t % 2][half]],
                    scale=small[:, 32 + t:33 + t])
            for half in range(2):
                hsl = slice(half * 512, (half + 1) * 512)
                bp = (t % 2) * 4 + 2 + half
                tt(tmp[:, hsl], bank_ap[bp], sg[:, hsl], ALU.mult, reads=[RBall[bp], Rsgh[t % 2][half]],
                   writes=[Rtmph[t % 2][half]])
            tt(hf, xt[t], tmp, ALU.add, reads=[R("xt", t)] + Rtmph[t % 2], writes=[Rhf])
            if t < n_next:
                load(xt[t], next_x[t], writes=[R("xt", t)])

        def p4_back_b(t):
            hf, Rhf = hf_[t]
            act(junk, hf, AF.Square, reads=[Rhf], writes=[Rjunk, rsm(("fs", t))], accum_out=small[:, 96 + t:97 + t],
                scale=float(D) ** -0.5)

        def next_sq(t, junk=junk, Rjunk=Rjunk):
            act(junk, xt[t], AF.Square, reads=[R("xt", t)], writes=[Rjunk, rsm(("nss", t))],
                accum_out=small[:, 112 + t:113 + t], scale=float(D) ** -0.5)

        def next_tr(t, xbuf=None):
            xb, Rxb = xbuf if xbuf is not None else xnb2[t % 2]
            ts(xb, xt[t], small[:, 120 + t:121 + t], None, ALU.mult, None, reads=[R("xt", t), rsm("nrs")],
               writes=[Rxb])
            for j in range(8):
                tr(psT[:, j * P:(j + 1) * P], xb[:, j * P:(j + 1) * P], identb[:],
                   reads=[Rxb, RC("identb")], writes=[RpsT])
            tt(xnT[:, :, t * P:(t + 1) * P], psT[:].rearrange("p (j t) -> p j t", t=P),
               nw[:, 0, :].unsqueeze(2).broadcast_to([P, 8, P]), ALU.mult,
               reads=[RpsT, RC("nw")], writes=[R("xnT", t)])

        def fin(t):
            hf, Rhf = hf_[t]
            stt(hf, hf, small[:, 104 + t:105 + t], fnw[:], ALU.mult, ALU.mult,
                reads=[Rhf, rsm("frs"), RC("fnw")], writes=[Rhf])
            store(tps[t].yrows, hf, reads=[Rhf])

        for t in range(nt, n_next):
            load(xt[t], next_x[t], writes=[R("xt", t)])
        p4_front(0)
        for t in range(nt):
            if t + 1 < nt:
                p4_front(t + 1)
            p4_back_a(t)
            if 0 <= t - 1 < n_next:
                next_sq(t - 1)
            p4_back_b(t)
        late = nt - 1 if nt - 1 < n_next else None
        for t in range(nt, n_next):
            next_sq(t)
        lo = min(max(nt - 1, 0), n_next)
        if lo > 0:
            rsqrt_eps(small[:, 120:120 + lo], small[:, 112:112 + lo],
                      reads=[rsm(("nss", t)) for t in range(lo)], writes=[rsm("nrs")])
        if n_next > nt:
            rsqrt_eps(small[:, 120 + nt:120 + n_next], small[:, 112 + nt:112 + n_next],
                      reads=[rsm(("nss", t)) for t in range(nt, n_next)], writes=[rsm("nrs")])
        for t in range(n_next):
            if t != late:
                next_tr(t)
        if late is not None:
            def late_norm(t=late):
                jk = S0f[:, 4:8, :].rearrange("p a b -> p (a b)").bitcast(BF16)
                xbl = S0f[:, 0:4, :].rearrange("p a b -> p (a b)").bitcast(BF16)
                next_sq(t, jk, R("S0f", 0))
                rsqrt_eps(small[:, 120 + t:121 + t], small[:, 112 + t:113 + t], reads=[rsm(("nss", t))],
                          writes=[rsm("nrs")])
                next_tr(t, (xbl, R("S0f", 0)))
            pending_late[0] = late_norm
        rsqrt_eps(small[:, 104:104 + nt], small[:, 96:96 + nt], reads=[rsm(("fs", t)) for t in range(nt)],
                  writes=[rsm("frs")])
        for t in range(nt):
            fin(t)

    NSP = SEQ // TT
    pending_late = [None]
    kinds_of = [["p"] * NT for _ in range(NSP)]
    kinds_of[-1] = kinds_of[-1] + ["s"]
    for sp_idx in range(NSP):
        register_weights(("st", sp_idx), n_dn_pass=1)
    for sp_idx in range(NSP):
        nx = None
        if sp_idx + 1 < NSP:
            nx = [xp[(sp_idx + 1) * TT + t * P:(sp_idx + 1) * TT + (t + 1) * P, :] for t in range(NT)]
            if sp_idx + 1 == NSP - 1:
                nx.append(xs[0:P, :])
        run_supertile(kinds_of[sp_idx], sp_idx, preloaded=(sp_idx > 0), next_x=nx)

    S.finish()
    S.replay()


_NC_CACHE = {}


def _consts():
    i = np.arange(P)
    c = {}
    c["c_identb"] = np.eye(P, dtype=np.float32).astype(ml_dtypes.bfloat16)
    c["c_identf"] = np.eye(P, dtype=np.float32)
    s, t = np.meshgrid(i, i, indexing="ij")
    c["c_maskp"] = ((s // 64 == t // 64) & (s <= t)).astype(np.float32)
    c["c_masks"] = ((s // 8 == t // 8) & (s <= t)).astype(np.float32)
    c["c_maskf"] = (s <= t).astype(np.float32)
    tcol = np.arange(512) % 128
    c["c_scanp"] = np.broadcast_to((tcol % 64 != 0).astype(np.float32), (P, 512)).copy()
    c["c_scans"] = np.broadcast_to((tcol % 8 != 0).astype(np.float32), (P, 512)).copy()
    c["c_bmask"] = (i[:, None] // 8 == np.arange(16)[None, :]).astype(np.float32)
    c["c_onesf"] = np.ones((P, 1), np.float32)
    return c


def _fm(v, nchunk):
    return np.ascontiguousarray(np.asarray(v, np.float32).reshape(nchunk, P).T)


def kernel(x_prompt, x_sample, p_prompt, p_sample, state_hgrn, state_conv, lb_logits,
           norm_mix_w, w_in, hgrn_norm_w, ln_v_w, ln_v_b, w_spatial, b_spatial, w_a_out,
           w_b_out, w_o, norm_ffn_w, w_up, conv_w, conv_b, w_down, norm_ple_w, w_ple_gate,
           w_ple_proj, final_norm_w, _debug=None):
    f32 = np.float32
    A = lambda a: np.ascontiguousarray(np.asarray(a, dtype=f32))
    if "nc" not in _NC_CACHE or _debug is not None:
        nc = build_nc(_debug)
        if _debug is None:
            _NC_CACHE["nc"] = nc
    else:
        nc = _NC_CACHE["nc"]
    shared = _consts()
    shared["w_in"] = A(w_in[0]); shared["w_a"] = A(w_a_out[0]); shared["w_b"] = A(w_b_out[0])
    shared["w_o"] = A(w_o[0]); shared["w_up"] = np.ascontiguousarray(
        np.asarray(w_up[0], dtype=f32).reshape(D, 2, NJ, P).transpose(0, 2, 1, 3).reshape(D, 2 * DFF)); shared["w_dn"] = A(w_down[0])
    shared["w_pg"] = A(w_ple_gate[0]); shared["w_pp"] = A(w_ple_proj[0])
    shared["v_nw"] = np.ascontiguousarray(np.stack(
        [_fm(norm_mix_w[0], 8), _fm(norm_ffn_w[0], 8), _fm(norm_ple_w[0], 8)], axis=1))
    shared["v_hgw"] = _fm(hgrn_norm_w[0], 4)
    lbl = np.asarray(lb_logits, f32)
    shared["v_lbl"] = np.ascontiguousarray(np.stack([_fm(lbl[0], 4), _fm(lbl[1], 4)], axis=1))
    shared["v_lnw"] = A(ln_v_w[0]); shared["v_lnb"] = A(ln_v_b[0]); shared["v_fnw"] = A(final_norm_w)
    bs = np.asarray(b_spatial[0], f32)
    shared["v_bsp"] = np.ascontiguousarray(bs.reshape(512))
    shared["v_bss"] = np.ascontiguousarray(np.tile(bs[:, :LS], (1, NSEQ_S)).reshape(512))
    cwn = np.asarray(conv_w[0], f32)
    shared["v_cw"] = np.ascontiguousarray(cwn.T.reshape(2, NJ, P, 3).transpose(2, 1, 0, 3))
    shared["v_cb"] = np.ascontiguousarray(np.asarray(conv_b[0], f32).reshape(2, NJ, P).transpose(2, 1, 0))
    ws = np.asarray(w_spatial[0], f32)
    shared["v_wsp"] = np.ascontiguousarray(ws.transpose(2, 0, 1))
    wss = np.zeros((P, 4, P), f32)
    sub = ws[:, :LS, :LS].transpose(2, 0, 1)
    for j in range(NSEQ_S):
        wss[j * LS:(j + 1) * LS, :, j * LS:(j + 1) * LS] = sub
    shared["v_wss"] = wss

    xpn = np.asarray(x_prompt, f32); xsn = np.asarray(x_sample, f32)
    ppn = np.asarray(p_prompt, f32)[0]; psn = np.asarray(p_sample, f32)[0]
    sth = np.asarray(state_hgrn, f32)[0]; stc = np.asarray(state_conv, f32)[0]
    in_maps = []
    for c in range(8):
        m = dict(shared)
        m["xp"] = np.ascontiguousarray(xpn[c])
        m["xs"] = np.ascontiguousarray(xsn[c * 16:(c + 1) * 16].reshape(P, D))
        m["pp"] = np.ascontiguousarray(ppn[c])
        m["ps"] = np.ascontiguousarray(psn[c * 16:(c + 1) * 16].reshape(P, 256))
        m["st_h"] = np.ascontiguousarray(sth[c * 16:(c + 1) * 16])
        m["st_c"] = np.ascontiguousarray(stc[c * 16:(c + 1) * 16].reshape(32, 2 * DFF))
        in_maps.append(m)
    res = run_bass_kernel_spmd(nc, in_maps, core_ids=list(range(8)))
    rs = res.results
    y_prompt = np.stack([r["y_p"] for r in rs], 0).astype(f32)
    y_sample = np.concatenate([r["y_s"].reshape(16, LS, D) for r in rs], 0).astype(f32)
    hgrn_p = np.stack([r["o_hp"] for r in rs], 0)[None].astype(f32)
    hgrn_s = np.concatenate([r["o_hs"] for r in rs], 0)[None].astype(f32)
    conv_p = np.stack([r["o_cp"] for r in rs], 0)[None].astype(f32)
    conv_s = np.concatenate([r["o_cs"].reshape(16, 2, 2 * DFF) for r in rs], 0)[None].astype(f32)
    gv_s = np.concatenate([r["o_gv"].reshape(16, LS, 512) for r in rs], 0)[None].astype(f32)
    if _debug is not None:
        return rs
    return (y_prompt, y_sample, hgrn_p, hgrn_s, conv_p, conv_s, gv_s)
```

```python
import contextlib
import numpy as np
import ml_dtypes
import concourse.bass as bass
import concourse.mybir as mybir
from concourse.bass_utils import run_bass_kernel_spmd

F32 = mybir.dt.float32
BF16 = mybir.dt.bfloat16
AF = mybir.ActivationFunctionType
ALU = mybir.AluOpType

P = 128
D = 1024
NIN = 5120
DFF = 2816
NJ = 22
SEQ = 2048
NSEQ_S = 16
LS = 8
EPS = 1e-6
NT_P = 4
NSLOT = 6
NDMA = 24
STRICT_SAME_ENGINE = True


class Res:
    __slots__ = ("name", "writer", "readers")

    def __init__(self, name):
        self.name = name
        self.writer = None
        self.readers = []


class Sched:
    def __init__(self, nc, stack):
        self.nc = nc
        self.names = ["pe", "act", "dve", "pool", "sp"]
        self.lists = {e: [] for e in self.names}
        self.sems = {e: stack.enter_context(nc.semaphore("s_" + e)) for e in self.names}
        self.cnt = {e: 0 for e in self.names}
        self.waited = {e: {} for e in self.names}
        self.dsems = [stack.enter_context(nc.semaphore("d%d" % i)) for i in range(NDMA)]
        self.dcnt = [0] * NDMA
        self.drr = {"sp": 0, "pool": 0}
        self.res = {}
        self.out_tokens = []

    def R(self, *key):
        r = self.res.get(key)
        if r is None:
            r = Res(key)
            self.res[key] = r
        return r

    def _semof(self, tok):
        if tok[0] == "e":
            return ("e", tok[1]), self.sems[tok[1]]
        return ("d", tok[1]), self.dsems[tok[1]]

    @staticmethod
    def _flat(xs):
        out = []
        for x in xs:
            if isinstance(x, (list, tuple)):
                out.extend(Sched._flat(x))
            else:
                out.append(x)
        return out

    def emit(self, eng, fn, reads=(), writes=(), dma=False, is_out=False):
        reads = self._flat(reads)
        writes = self._flat(writes)
        deps = set()
        for r in reads:
            if r.writer is not None:
                deps.add(r.writer)
        for w in writes:
            if w.writer is not None and (STRICT_SAME_ENGINE or not (w.writer[0] == "e" and w.writer[1] == eng and not dma)):
                deps.add(w.writer)
            for t in w.readers:
                if STRICT_SAME_ENGINE or not (t[0] == "e" and t[1] == eng and not dma):
                    deps.add(t)
        need = {}
        for tok in deps:
            if tok[0] == "e" and tok[1] == eng and not dma and eng == "pe":
                continue
            key, sem = self._semof(tok)
            if need.get(key, (None, 0))[1] < tok[2]:
                need[key] = (sem, tok[2])
        waits = []
        wd = self.waited[eng]
        for key, (sem, val) in need.items():
            if wd.get(key, 0) < val:
                waits.append((sem, val))
                wd[key] = val
        if dma:
            if eng == "pool":
                s = 16 + self.drr["pool"]
                self.drr["pool"] = (self.drr["pool"] + 1) % (NDMA - 16)
            else:
                s = self.drr["sp"]
                self.drr["sp"] = (self.drr["sp"] + 1) % 16
            prior = self.dcnt[s] * 16
            key = ("d", s)
            if prior > 0 and wd.get(key, 0) < prior:
                waits.append((self.dsems[s], prior))
                wd[key] = prior
            self.dcnt[s] += 1
            tok = ("d", s, self.dcnt[s] * 16)
            inc = (self.dsems[s], 16)
            if is_out:
                self.out_tokens.append(tok)
        else:
            self.cnt[eng] += 1
            tok = ("e", eng, self.cnt[eng])
            inc = (self.sems[eng], 1)
        self.lists[eng].append((waits, fn, inc))
        for r in reads:
            r.readers.append(tok)
        for w in writes:
            w.writer = tok
            w.readers = []
        return tok

    def finish(self):
        waits = []
        for s in range(NDMA):
            if self.dcnt[s] > 0:
                waits.append((self.dsems[s], self.dcnt[s] * 16))
        self.lists["sp"].append((waits, None, None))

    def replay(self):
        nc = self.nc
        lists = self.lists

        needed = {}
        for name in self.names:
            for waits, fn, inc in lists[name]:
                for sem, val in waits:
                    needed.setdefault(id(sem), set()).add(val)

        def run(e, items):
            pending = 0
            count = 0
            for waits, fn, inc in items:
                for sem, val in waits:
                    e.wait_ge(sem, val)
                if fn is not None:
                    ins = fn(e)
                    if inc[1] == 16:
                        ins.then_inc(inc[0], 16)
                        continue
                    count += 1
                    pending += 1
                    if count in needed.get(id(inc[0]), ()):
                        ins.then_inc(inc[0], pending)
                        pending = 0

        with nc.Block() as block:
            @block.tensor
            def _(e):
                run(e, lists["pe"])

            @block.scalar
            def _(e):
                run(e, lists["act"])

            @block.vector
            def _(e):
                run(e, lists["dve"])

            @block.gpsimd
            def _(e):
                run(e, lists["pool"])

            @block.sync
            def _(e):
                run(e, lists["sp"])


def build_nc(debug=None):
    nc = bass.Bass("TRN2", target_bir_lowering=False)
    stack = contextlib.ExitStack()
    with stack:
        _build(nc, stack, debug)
    return nc


def _build(nc, stack, debug):
    S = Sched(nc, stack)
    R = S.R

    def din(name, shape, dt=F32):
        return nc.dram_tensor(name, list(shape), dt, kind="ExternalInput").ap()

    def dout(name, shape, dt=F32):
        return nc.dram_tensor(name, list(shape), dt, kind="ExternalOutput").ap()

    xp = din("xp", [SEQ, D]); xs = din("xs", [P, D])
    pp = din("pp", [SEQ, 256]); ps_ = din("ps", [P, 256])
    st_h = din("st_h", [NSEQ_S, 4, P, P]); st_c = din("st_c", [32, 2 * DFF])
    w_in = din("w_in", [D, NIN]); w_a = din("w_a", [512, D]); w_b = din("w_b", [512, D])
    w_o = din("w_o", [D, D]); w_up = din("w_up", [D, 2 * DFF]); w_dn = din("w_dn", [DFF, D])
    w_pg = din("w_pg", [D, D]); w_pp = din("w_pp", [256, D])
    c_identb = din("c_identb", [P, P], BF16); c_identf = din("c_identf", [P, P])
    c_maskp = din("c_maskp", [P, P]); c_masks = din("c_masks", [P, P]); c_maskf = din("c_maskf", [P, P])
    c_scanp = din("c_scanp", [P, 512]); c_scans = din("c_scans", [P, 512])
    c_bmask = din("c_bmask", [P, 16]); c_onesf = din("c_onesf", [P, 1])
    v_nw = din("v_nw", [P, 3, 8])
    v_hgw = din("v_hgw", [P, 4]); v_lbl = din("v_lbl", [P, 2, 4])
    v_lnw = din("v_lnw", [512]); v_lnb = din("v_lnb", [512]); v_fnw = din("v_fnw", [D])
    v_bsp = din("v_bsp", [512]); v_bss = din("v_bss", [512])
    v_cw = din("v_cw", [P, NJ, 2, 3]); v_cb = din("v_cb", [P, NJ, 2])
    v_wsp = din("v_wsp", [P, 4, P]); v_wss = din("v_wss", [P, 4, P])

    y_p = dout("y_p", [SEQ, D]); y_s = dout("y_s", [P, D])
    o_hp = dout("o_hp", [4, P, P]); o_hs = dout("o_hs", [NSEQ_S, 4, P, P])
    o_cp = dout("o_cp", [2, 2 * DFF]); o_cs = dout("o_cs", [32, 2 * DFF])
    o_gv = dout("o_gv", [P, 512])
    dbg = {}
    if debug:
        for name, shape in debug.items():
            dbg[name] = dout("dbg_" + name, shape)

    def sb(name, shape, dt=F32):
        return stack.enter_context(nc.sbuf_tensor(name, list(shape), dt))

    NT = NT_P
    TT = NT * P
    NT_MAX = NT + 1
    TTM = NT_MAX * P
    banks = [stack.enter_context(nc.psum_tensor("bank%d" % i, [P, 512], F32)) for i in range(7)]
    psT = stack.enter_context(nc.psum_tensor("psT", [P, 1024], BF16))
    RB = [R("bank", i) for i in range(7)]
    RpsT = R("psT")
    bank_ap = [b[:, :] for b in banks] + [psT[:].bitcast(F32)]
    RBall = RB + [RpsT]

    identb = sb("identb", [P, P], BF16); identf = sb("identf", [P, P])
    maskp = sb("maskp", [P, P]); masks = sb("masks", [P, P]); maskf = sb("maskf", [P, P])
    scanp = sb("scanp", [P, 512]); scans = sb("scans", [P, 512])
    bmask = sb("bmask", [P, 16]); onesf = sb("onesf", [P, 1])
    nw = sb("nw", [P, 3, 8]); hgw = sb("hgw", [P, 4]); lbl = sb("lbl", [P, 2, 4])
    oml = sb("oml", [P, 4]); lbt = sb("lbt", [P, 4])
    lnw = sb("lnw", [P, 512]); lnb = sb("lnb", [P, 512]); fnw = sb("fnw", [P, D])
    bsp = sb("bsp", [P, 512]); bss = sb("bss", [P, 512])
    cw = sb("cw", [P, NJ, 2, 3]); cb = sb("cb", [P, NJ, 2])
    wsp = sb("wsp", [P, 4, P], BF16); wss = sb("wss", [P, 4, P], BF16)
    Sf = sb("Sf", [P, 4, P]); Sb = sb("Sb", [P, 4, P], BF16)
    hh_p = sb("hh_p", [P, 2 * NJ, 2, 4])
    small = sb("small", [P, 256])
    epsc = sb("epsc", [P, 1])
    rstda = sb("rstda", [P, 8])

    xt_all = sb("xt_all", [P, NT_MAX, D])
    xt = [xt_all[:, t, :] for t in range(NT_MAX)]
    xnT = sb("xnT", [P, 8, TTM], BF16)
    mT = xnT
    arena_g = sb("arena_g", [P, NJ * TTM], BF16)
    gT = arena_g[:, :].rearrange("p (j t) -> p j t", t=TTM)
    ogT = arena_g[:, 0:4 * TTM].rearrange("p (h t) -> p h t", t=TTM)
    obT = arena_g[:, 4 * TTM:8 * TTM].rearrange("p (h t) -> p h t", t=TTM)
    t1 = arena_g[:, 8 * TTM:8 * TTM + NT_MAX * D].rearrange("p (n d) -> p n d", d=D)
    RG = [R("arena_g", j) for j in range(NJ)]
    RogT = RG[0:4]; RobT = RG[4:8]
    nt1 = (NT_MAX * D + TTM - 1) // TTM
    Rt1 = RG[8:8 + nt1]
    slots = [sb("slot%d" % i, [P, 4096], BF16) for i in range(NSLOT)]
    S0f = sb("S0f", [P, 16, P]); S0b = sb("S0b", [P, 16, P], BF16)
    vblk = sb("vblk", [P, 4, P], BF16)
    scs = sb("scs", [P, NJ, 2, 32]); cso = sb("cso", [P, NJ, 2, 32])
    cstg = [sb("cstg%d" % i, [32, 512]) for i in range(2)]

    ARENA_COLS = 11776
    arena = sb("arena", [P, ARENA_COLS])

    class Arena:
        def __init__(self):
            self.off = 0
            self.epoch = 0
            self.cur = []
            self.carry = []

        def reset(self):
            toks = set()
            for r in self.cur:
                if r.writer is not None:
                    toks.add(r.writer)
                toks.update(r.readers)
            toks.update(self.carry)
            best = {}
            for tk in toks:
                key = (tk[0], tk[1])
                if key not in best or best[key][2] < tk[2]:
                    best[key] = tk
            self.carry = list(best.values())
            self.cur = []
            self.off = 0
            self.epoch += 1

        def _res(self):
            r = R("arena", self.epoch, len(self.cur))
            r.readers = list(self.carry)
            self.cur.append(r)
            return r

        def f(self, cols):
            a = arena[:, self.off:self.off + cols]
            self.off += cols
            assert self.off <= ARENA_COLS, self.off
            return a, self._res()

        def b(self, cols):
            n = (cols + 1) // 2
            a = arena[:, self.off:self.off + n].bitcast(BF16)[:, 0:cols]
            self.off += n
            assert self.off <= ARENA_COLS, self.off
            return a, self._res()

    AR = Arena()
    Rsmall = {}

    def rsm(i):
        if i not in Rsmall:
            Rsmall[i] = R("small", i)
        return Rsmall[i]

    E = S.emit

    def load(dst_ap, src_ap, writes, eng="sp", reads=()):
        return E(eng, lambda e: e.dma_start(out=dst_ap, in_=src_ap), reads=reads, writes=writes, dma=True)

    def store(dst_ap, src_ap, reads, eng="sp"):
        return E(eng, lambda e: e.dma_start(out=dst_ap, in_=src_ap), reads=reads, writes=(), dma=True, is_out=True)

    def act(out, in_, func, reads, writes, bias=None, scale=None, accum_out=None):
        kw = {}
        if bias is not None:
            kw["bias"] = bias
        if scale is not None:
            kw["scale"] = scale
        if accum_out is not None:
            kw["accum_out"] = accum_out
        return E("act", lambda e: e.activation(out=out, in_=in_, func=func, **kw), reads=reads, writes=writes)

    def tt(out, in0, in1, op, reads, writes, eng="dve"):
        return E(eng, lambda e: e.tensor_tensor(out=out, in0=in0, in1=in1, op=op), reads=reads, writes=writes)

    def ts(out, in0, s1, s2, op0, op1, reads, writes, eng="dve"):
        if op1 is None:
            return E(eng, lambda e: e.tensor_scalar(out=out, in0=in0, scalar1=s1, scalar2=None, op0=op0),
                     reads=reads, writes=writes)
        return E(eng, lambda e: e.tensor_scalar(out=out, in0=in0, scalar1=s1, scalar2=s2, op0=op0, op1=op1),
                 reads=reads, writes=writes)

    def stt(out, in0, scalar, in1, op0, op1, reads, writes, eng="dve"):
        return E(eng, lambda e: e.scalar_tensor_tensor(out=out, in0=in0, scalar=scalar, in1=in1, op0=op0, op1=op1),
                 reads=reads, writes=writes)

    def mm(out, lhsT, rhs, start, stop, reads, writes):
        return E("pe", lambda e: e.matmul(out, lhsT, rhs, start=start, stop=stop), reads=reads, writes=writes)

    def tr(out, in_, ident, reads, writes):
        return E("pe", lambda e: e.transpose(out, in_, ident), reads=reads, writes=writes)

    def cp(out, in_, reads, writes, eng="dve"):
        return E(eng, lambda e: e.tensor_copy(out=out, in_=in_), reads=reads, writes=writes)

    def rsqrt_eps(out, in_, reads, writes):
        act(out, in_, AF.Ln, reads=list(reads) + [R("epsc")], writes=writes, bias=epsc[:, 0:1])
        return act(out, out, AF.Exp, reads=writes, writes=writes, scale=-0.5)

    def memset(ap, val, writes, eng="dve"):
        return E(eng, lambda e: e.memset(ap, val), reads=(), writes=writes)

    def RC(n):
        return R("c", n)

    for t in range(NT):
        load(xt[t], xp[t * P:(t + 1) * P, :], writes=[R("xt", t)])
    for dst, src in [(identb, c_identb), (identf, c_identf), (maskp, c_maskp), (masks, c_masks),
                     (maskf, c_maskf), (scanp, c_scanp), (scans, c_scans), (bmask, c_bmask),
                     (onesf, c_onesf), (nw, v_nw), (hgw, v_hgw), (lbl, v_lbl), (cw, v_cw), (cb, v_cb)]:
        load(dst[:], src, writes=[R("c", dst.name)])
    for dst, src in [(lnw, v_lnw), (lnb, v_lnb), (fnw, v_fnw), (bsp, v_bsp), (bss, v_bss)]:
        load(dst[:], src.partition_broadcast(P), writes=[R("c", dst.name)])
    memset(epsc[:], EPS, writes=[R("epsc")])
    tt(lbt[:], lbl[:, 0, :], lbl[:, 1, :], ALU.subtract, reads=[R("c", "lbl")], writes=[R("c", "lbt")])
    act(lbt[:], lbt[:], AF.Sigmoid, reads=[R("c", "lbt")], writes=[R("c", "lbt")])
    ts(oml[:], lbt[:], -1.0, 1.0, ALU.mult, ALU.add, reads=[R("c", "lbt")], writes=[R("c", "oml")])
    memset(Sf[:], 0.0, writes=[R("Sf")])
    memset(Sb[:], 0.0, writes=[R("Sb")])
    memset(hh_p[:], 0.0, writes=[R("cr", a, j) for a in range(2) for j in range(NJ)])

    spatial_done = [False]

    def prep_spatial():
        if spatial_done[0]:
            return
        spatial_done[0] = True
        AR.reset()
        wstage, Rwstage = AR.f(512)
        wstage3 = wstage.rearrange("p (g t) -> p g t", t=P)
        for dstw, srcw, msk in [(wsp, v_wsp, maskf), (wss, v_wss, masks)]:
            load(wstage3, srcw, writes=[Rwstage])
            tt(dstw[:], wstage3, msk[:].unsqueeze(1).broadcast_to([P, 4, P]), ALU.mult,
               reads=[Rwstage, R("c", msk.name)], writes=[R("c", dstw.name)])

    ring = {"next": 0}

    def wslot():
        i = ring["next"]
        ring["next"] = (i + 1) % NSLOT
        return i

    class WStream:
        def __init__(self):
            self.order = []
            self.emitted = 0
            self.loaded = {}
            self.index = {}

        def add(self, key, kind, src, k, n):
            self.index[key] = len(self.order)
            self.order.append((key, kind, src, k, n))

        def _emit_one(self):
            key, kind, src, k, n = self.order[self.emitted]
            self.emitted += 1
            i = wslot()
            RW = [R("slot", i), R("slot", i, "b")]
            if kind == "plain":
                view = slots[i][:, 0:k * n].rearrange("p (k n) -> p k n", n=n)
                first_reads = [R("xt", t) for t in range(NT)] if self.emitted == 1 else ()
                load(view, src, writes=RW, eng="pool", reads=first_reads)
            else:
                view = slots[i][:, 0:8 * 256].rearrange("p (k n) -> p k n", n=256)
                load(view, src, writes=RW, eng="pool")
            self.loaded[key] = (view, RW)

        def get(self, key, pf):
            n = self.index[key]
            while self.emitted <= min(n + pf, len(self.order) - 1):
                self._emit_one()
            return self.loaded.pop(key)

    WS = WStream()

    w_in_v = w_in.rearrange("(k p) n -> p k n", p=P)
    w_up_v = w_up.rearrange("(k p) n -> p k n", p=P)
    w_o_v = w_o.rearrange("(k p) n -> p k n", p=P)
    w_pg_v = w_pg.rearrange("(k p) n -> p k n", p=P)
    w_a_v = w_a.rearrange("(k p) n -> p k n", p=P)
    w_b_v = w_b.rearrange("(k p) n -> p k n", p=P)
    w_pp_v = w_pp.rearrange("(k p) n -> p k n", p=P)
    w_dn_v = w_dn.rearrange("(k p) n -> p k n", p=P)

    def register_weights(tag, n_dn_pass=1):
        for nm, c0 in [("q", 0), ("f", 512), ("i", 1024), ("og", 1536), ("u", 2048), ("v", 2560),
                       ("ga0", 3072), ("ga1", 3584)]:
            WS.add((tag, nm), "plain", w_in_v[:, :, c0:c0 + 512], 8, 512)
        WS.add((tag, "wa"), "plain", w_a_v, 4, D)
        WS.add((tag, "gb0"), "plain", w_in_v[:, :, 4096:4608], 8, 512)
        WS.add((tag, "gb1"), "plain", w_in_v[:, :, 4608:5120], 8, 512)
        WS.add((tag, "wb"), "plain", w_b_v, 4, D)
        WS.add((tag, "wo0"), "plain", w_o_v[:, :, 0:512], 8, 512)
        WS.add((tag, "wo1"), "plain", w_o_v[:, :, 512:1024], 8, 512)
        for j in range(NJ):
            WS.add((tag, "up", j), "up", w_up_v[:, :, j * 2 * P:(j + 1) * 2 * P], 8, 256)
        for ps in range(n_dn_pass):
            for q in range(6):
                k0, k1 = q * 4, min(NJ, (q + 1) * 4)
                WS.add((tag, "dn", ps, q), "plain", w_dn_v[:, k0:k1, :], k1 - k0, D)
        WS.add((tag, "pg0"), "plain", w_pg_v[:, :, 0:512], 8, 512)
        WS.add((tag, "pg1"), "plain", w_pg_v[:, :, 512:1024], 8, 512)
        WS.add((tag, "pp"), "plain", w_pp_v, 2, D)

    def norm_transpose_all(nt, nwi):
        AR.reset()
        junk, Rjunk = AR.b(D)
        xnb = [AR.b(D) for _ in range(2)]
        for t in range(nt):
            act(junk, xt[t], AF.Square, reads=[R("xt", t)], writes=[Rjunk, rsm(("ss", t))], accum_out=small[:, t:t + 1],
                scale=float(D) ** -0.5)
        rsqrt_eps(small[:, 8:8 + nt], small[:, 0:nt], reads=[rsm(("ss", t)) for t in range(nt)], writes=[rsm("rs")])
        for t in range(nt):
            xb, Rxb = xnb[t % 2]
            ts(xb, xt[t], small[:, 8 + t:9 + t], None, ALU.mult, None, reads=[R("xt", t), rsm("rs")], writes=[Rxb])
            for j in range(8):
                tr(psT[:, j * P:(j + 1) * P], xb[:, j * P:(j + 1) * P], identb[:],
                   reads=[Rxb, RC("identb")], writes=[RpsT])
            tt(xnT[:, :, t * P:(t + 1) * P], psT[:].rearrange("p (j t) -> p j t", t=P),
               nw[:, nwi, :].unsqueeze(2).broadcast_to([P, 8, P]), ALU.mult,
               reads=[RpsT, RC("nw")], writes=[R("xnT", t)])

    def run_supertile(kinds, sp_idx, preloaded=False, next_x=None):
        nt = len(kinds)
        npt = kinds.count("p")
        has_s = "s" in kinds
        last_prompt = (npt > 0) and (sp_idx == NSP - 1)
        tag = ("st", sp_idx)

        class TP:
            pass

        tps = []
        for t_, kd in enumerate(kinds):
            o = TP()
            o.sample = (kd == "s")
            o.C = LS if o.sample else 64
            o.NCH = P // o.C
            o.mid = o.C // 2 - 1
            o.maskt, o.Rmask = (masks, RC("masks")) if o.sample else (maskp, RC("maskp"))
            o.scant, o.Rscan = (scans, RC("scans")) if o.sample else (scanp, RC("scanp"))
            o.wst, o.Rwst = (wss, RC("wss")) if o.sample else (wsp, RC("wsp"))
            o.bst, o.Rbst = (bss, RC("bss")) if o.sample else (bsp, RC("bsp"))
            if o.sample:
                o.xrows, o.prows, o.yrows = xs[0:P, :], ps_[0:P, :], y_s[0:P, :]
            else:
                r0 = sp_idx * TT + t_ * P
                o.xrows, o.prows, o.yrows = xp[r0:r0 + P, :], pp[r0:r0 + P, :], y_p[r0:r0 + P, :]
            tps.append(o)

        if not preloaded:
            norm_transpose_all(nt, 0)

        Wq, RWq = WS.get((tag, "q"), 2)
        Wf, RWf = WS.get((tag, "f"), 2)
        Wi, RWi = WS.get((tag, "i"), 2)
        Wog, RWog = WS.get((tag, "og"), 2)
        AR.reset()
        pA = [dict(qT=AR.f(512), sk=AR.f(512), sog=AR.f(512), vtm=AR.b(512)) for _ in range(2)]
        pC = [dict(qtl=AR.b(512), ktl=AR.b(512), khT=AR.b(512), qin=AR.b(512)) for _ in range(2)]
        (kT, RkT), (lf, Rlf), (bT, RbT), (bm1, Rbm1), (bm2, Rbm2), (ogf, Rogf), (tmpf, Rtmpf) = \
            [AR.f(512) for _ in range(7)]
        Es = [AR.f(512) for _ in range(4)]
        (khat, Rkhat), (scTm, RscTm) = [AR.b(512) for _ in range(2)]

        def proj_pieces(t):
            tcols = slice(t * P, (t + 1) * P)
            Rxn = R("xnT", t)
            for (W, RW, bank_i) in ((Wq, RWq, 0), (Wf, RWf, 1), (Wog, RWog, 2)):
                for h in range(4):
                    for dk in range(8):
                        mm(banks[bank_i][:, h * P:(h + 1) * P], W[:, dk, h * P:(h + 1) * P], xnT[:, dk, tcols],
                           start=(dk == 0), stop=(dk == 7), reads=[RW, Rxn], writes=[RB[bank_i]])
                yield
            for dk in range(8):
                mm(banks[3][:, :], xnT[:, dk, tcols], Wi[:, dk, :],
                   start=(dk == 0), stop=(dk == 7), reads=[RWi, Rxn], writes=[RB[3]])
            yield

        def evac(t):
            pb = pA[t % 2]
            qT, RqT = pb["qT"]
            act(qT, banks[0][:], AF.Sigmoid, reads=[RB[0]], writes=[RqT])
            sk, Rsk = pb["sk"]
            act(sk, banks[1][:], AF.Sigmoid, reads=[RB[1]], writes=[Rsk], scale=-1.0)
            sog, Rsog = pb["sog"]
            act(sog, banks[2][:], AF.Sigmoid, reads=[RB[2]], writes=[Rsog])
            vtm, Rvtm = pb["vtm"]
            act(vtm, banks[3][:], AF.Copy, reads=[RB[3]], writes=[Rvtm])
            tt(qT, banks[0][:], qT, ALU.mult, reads=[RB[0], RqT], writes=[RqT])

        def chain(t):
            o = tps[t]
            C, NCH, mid, scant, Rscan = o.C, o.NCH, o.mid, o.scant, o.Rscan
            pb, pc = pA[t % 2], pC[t % 2]
            (qT, RqT), (sk, Rsk) = pb["qT"], pb["sk"]
            (qtl, Rqtl), (ktl, Rktl), (khT, RkhT), (qin, Rqin) = pc["qtl"], pc["ktl"], pc["khT"], pc["qin"]
            (E0, RE0), (E1, RE1), (E2, RE2), (E3, RE3) = Es
            ebend = small[:, 128 + (t % 2) * 64:128 + (t % 2) * 64 + 4 * NCH]
            Reb = rsm(("eb", t % 2))
            tt(kT.rearrange("p (h t) -> p h t", t=P), sk.rearrange("p (h t) -> p h t", t=P),
               oml[:].unsqueeze(2).broadcast_to([P, 4, P]), ALU.mult, reads=[Rsk, RC("oml")], writes=[RkT])
            act(lf, kT, AF.Ln, reads=[RkT], writes=[Rlf], scale=-1.0, bias=1.0)
            yield
            E("dve", lambda e: e.tensor_tensor_scan(out=bT, data0=scant[:], data1=lf, initial=0.0,
                                                    op0=ALU.mult, op1=ALU.add),
              reads=[Rscan, Rlf], writes=[RbT])
            yield
            b4 = bT.rearrange("p (h n c) -> p h n c", h=4, c=C)
            bm14 = bm1.rearrange("p (h n c) -> p h n c", h=4, c=C)
            bm24 = bm2.rearrange("p (h n c) -> p h n c", h=4, c=C)
            act(E0, bT, AF.Exp, reads=[RbT], writes=[RE0])
            tt(bm14, b4, b4[:, :, :, mid:mid + 1].broadcast_to([P, 4, NCH, C]), ALU.subtract,
               reads=[RbT], writes=[Rbm1])
            yield
            act(ebend.rearrange("p (h n) -> p h n", h=4), b4[:, :, :, C - 1], AF.Exp, reads=[RbT], writes=[Reb])
            tt(bm24, b4[:, :, :, C - 1:C].broadcast_to([P, 4, NCH, C]), b4, ALU.subtract,
               reads=[RbT], writes=[Rbm2])
            yield
            act(E1, bm1, AF.Exp, reads=[Rbm1], writes=[RE1])
            tt(qin, qT, E0, ALU.mult, reads=[RqT, RE0], writes=[Rqin])
            yield
            act(E2, bm1, AF.Exp, reads=[Rbm1], writes=[RE2], scale=-1.0)
            tt(qtl, qT, E1, ALU.mult, reads=[RqT, RE1], writes=[Rqtl])
            yield
            act(E3, bm2, AF.Exp, reads=[Rbm2], writes=[RE3])
            tt(ktl, kT, E2, ALU.mult, reads=[RkT, RE2], writes=[Rktl])
            yield
            tt(khT, kT, E3, ALU.mult, reads=[RkT, RE3], writes=[RkhT])
            yield

        def seq(t):
            o = tps[t]
            sample, C, NCH, maskt, Rmask = o.sample, o.C, o.NCH, o.maskt, o.Rmask
            tcols = slice(t * P, (t + 1) * P)
            pb, pc = pA[t % 2], pC[t % 2]
            (sog, Rsog), (vtm, Rvtm) = pb["sog"], pb["vtm"]
            (qtl, Rqtl), (ktl, Rktl), (khT, RkhT), (qin, Rqin) = pc["qtl"], pc["ktl"], pc["khT"], pc["qin"]
            ebend = small[:, 128 + (t % 2) * 64:128 + (t % 2) * 64 + 4 * NCH]
            Reb = rsm(("eb", t % 2))
            for h in range(4):
                tr(psT[:, h * P:(h + 1) * P], khT[:, h * P:(h + 1) * P], identb[:],
                   reads=[RkhT, RC("identb")], writes=[RpsT])
            for h in range(4):
                hs = slice(h * P, (h + 1) * P)
                mm(banks[4][:, hs], ktl[:, hs], qtl[:, hs], start=True, stop=True,
                   reads=[Rktl, Rqtl], writes=[RB[4]])
            act(khat, psT[:, 0:512], AF.Copy, reads=[RpsT], writes=[Rkhat])
            tt(scTm.rearrange("p (h t) -> p h t", t=P), banks[4][:].rearrange("p (h t) -> p h t", t=P),
               maskt[:].unsqueeze(1).broadcast_to([P, 4, P]), ALU.mult, reads=[RB[4], Rmask], writes=[RscTm])
            yield
            if not sample:
                for c in range(NCH):
                    cs_ = slice(c * C, (c + 1) * C)
                    for h in range(4):
                        osl = slice(h * P + c * C, h * P + (c + 1) * C)
                        mm(banks[5][:, osl], vtm[cs_, h * P:(h + 1) * P], scTm[cs_, osl],
                           start=True, stop=False, reads=[Rvtm, RscTm], writes=[RB[5]])
                        mm(banks[5][:, osl], Sb[:, h, :], qin[:, osl], start=False, stop=True,
                           reads=[R("Sb"), Rqin], writes=[RB[5]])
                    for h in range(4):
                        hs = slice(h * P, (h + 1) * P)
                        mm(banks[6][:, hs], khat[cs_, hs], vtm[cs_, hs], start=True, stop=True,
                           reads=[Rkhat, Rvtm], writes=[RB[6]])
                    yield
                    eb = ebend.rearrange("p (h n) -> p h n", h=4)[:, :, c:c + 1].broadcast_to([P, 4, P])
                    tt(Sf[:], Sf[:], eb, ALU.mult, reads=[R("Sf"), Reb], writes=[R("Sf")])
                    tt(Sf[:], Sf[:], banks[6][:].rearrange("p (h v) -> p h v", v=P), ALU.add,
                       reads=[R("Sf"), RB[6]], writes=[R("Sf")])
                    act(Sb[:], Sf[:], AF.Copy, reads=[R("Sf")], writes=[R("Sb")])
                    yield
            else:
                first = [True]

                def mmo(out, lhsT, rhs, last, reads):
                    st = first[0]
                    first[0] = False
                    E("pe", lambda e: e.matmul(out, lhsT, rhs, start=st, stop=last, skip_group_check=True),
                      reads=reads, writes=[RB[5]])

                for h in range(4):
                    hs = slice(h * P, (h + 1) * P)
                    mmo(banks[5][:, hs], vtm[:, hs], scTm[:, hs], False, [Rvtm, RscTm])
                eb3 = ebend.rearrange("p (h n) -> p h n", h=4)
                def ld_state(g):
                    hb = g % 2
                    load(S0f[:, hb * 8:(hb + 1) * 8, :], st_h[g * 2:(g + 1) * 2].rearrange("j h k v -> k (j h) v"),
                         writes=[R("S0f", hb)])

                ld_state(0)
                for g in range(8):
                    hb = g % 2
                    S0fh = S0f[:, hb * 8:(hb + 1) * 8, :]
                    S0bh = S0b[:, hb * 8:(hb + 1) * 8, :]
                    RS0f, RS0b = R("S0f", hb), R("S0b", hb)
                    if g + 1 < 8:
                        ld_state(g + 1)
                    act(S0bh, S0fh, AF.Copy, reads=[RS0f], writes=[RS0b])
                    for jj in range(2):
                        j = g * 2 + jj
                        for h in range(4):
                            osl = slice(h * P + j * LS, h * P + (j + 1) * LS)
                            mmo(banks[5][:, osl], S0bh[:, jj * 4 + h, :], qin[:, osl],
                                (g == 7 and jj == 1 and h == 3), [RS0b, Rqin])
                    for h in range(4):
                        hs = slice(h * P, (h + 1) * P)
                        vi = (g * 4 + h) % 2
                        vb = vblk[:, vi * 2:(vi + 1) * 2, :]
                        Rvb = R("vblk", vi)
                        bk = 6 if vi == 0 else 4
                        tt(vb, vtm[:, hs].unsqueeze(1).broadcast_to([P, 2, P]),
                           bmask[:, g * 2:(g + 1) * 2].unsqueeze(2).broadcast_to([P, 2, P]), ALU.mult,
                           reads=[Rvtm, RC("bmask")], writes=[Rvb])
                        mm(banks[bk][:, 0:2 * P], khat[:, hs], vb.rearrange("p j v -> p (j v)"),
                           start=True, stop=True, reads=[Rkhat, Rvb], writes=[RB[bk]])
                        for jj in range(2):
                            j = g * 2 + jj
                            stt(S0fh[:, jj * 4 + h, :], S0fh[:, jj * 4 + h, :], eb3[:, h, j:j + 1],
                                banks[bk][:, jj * P:(jj + 1) * P], ALU.mult, ALU.add,
                                reads=[RS0f, Reb, RB[bk]], writes=[RS0f])
                    store(o_hs[g * 2:(g + 1) * 2].rearrange("j h k v -> k (j h) v"), S0fh, reads=[RS0f])
                    yield
            tt(ogf, banks[5][:], sog, ALU.mult, reads=[RB[5], Rsog], writes=[Rogf])
            act(tmpf, ogf, AF.Square, reads=[Rogf], writes=[Rtmpf], scale=512.0 ** -0.5)
            yield
            for h in range(4):
                mm(banks[4][:, 0:1], tmpf[:, h * P:(h + 1) * P], onesf[:, 0:1], start=(h == 0), stop=(h == 3),
                   reads=[Rtmpf, RC("onesf")], writes=[RB[4]])
            cp(small[:, 16 + t:17 + t], banks[4][:, 0:1], reads=[RB[4]], writes=[rsm(("ssa", t))])
            tt(ogT[:, :, tcols], ogf.rearrange("p (h t) -> p h t", t=P),
               hgw[:].unsqueeze(2).broadcast_to([P, 4, P]), ALU.mult, reads=[Rogf, RC("hgw")],
               writes=[R("ogT", t)] + RogT)
            yield

        def interleave(*gens):
            gens = [g for g in gens if g is not None]
            while gens:
                for g in list(gens):
                    try:
                        next(g)
                    except StopIteration:
                        gens.remove(g)

        interleave(proj_pieces(0))
        if pending_late[0] is not None:
            pending_late[0]()
            pending_late[0] = None
        evac(0)
        for k in range(nt + 1):
            interleave(chain(k) if k < nt else None,
                       seq(k - 1) if k >= 1 else None,
                       proj_pieces(k + 1) if k + 1 < nt else None)
            if k + 1 < nt:
                evac(k + 1)
        rsqrt_eps(rstda[:, 0:nt], small[:, 16:16 + nt], reads=[rsm(("ssa", t)) for t in range(nt)],
                  writes=[R("rstda")])
        if last_prompt:
            store(o_hp.rearrange("h k v -> k h v"), Sf[:], reads=[R("Sf")])

        prep_spatial()
        Wu, RWu = WS.get((tag, "u"), 3)
        Wv, RWv = WS.get((tag, "v"), 3)
        AR.reset()
        uTall, RuTall = AR.f(4 * TTM)
        uT4 = uTall.rearrange("p (g t) -> p g t", t=TTM)
        RuTt = [AR._res() for _ in range(nt)]
        vgs = [AR.f(512) for _ in range(nt)]
        vns = [AR.f(512) for _ in range(2)]
        tmps = [AR.f(512) for _ in range(2)]
        vvs = [AR.b(512) for _ in range(2)]
        sgs = [AR.f(512) for _ in range(2)]
        (tmpf, Rtmpf) = tmps[0]
        ugroups = []
        if npt > 0:
            ugroups.append((slice(0, npt * P), list(range(npt))))
        if has_s:
            ugroups.append((slice(npt * P, (npt + 1) * P), [npt]))
        for (ucols, utiles) in ugroups:
            nU = ucols.stop - ucols.start
            for g in range(4):
                for dk in range(8):
                    mm(banks[g][:, 0:nU], Wu[:, dk, g * P:(g + 1) * P], xnT[:, dk, ucols],
                       start=(dk == 0), stop=(dk == 7), reads=[RWu] + [R("xnT", q) for q in utiles],
                       writes=[RB[g]])
                act(uT4[:, g, ucols], banks[g][:, 0:nU], AF.Gelu, reads=[RB[g]],
                    writes=[RuTt[q] for q in utiles])
        for t in range(nt):
            tcols = slice(t * P, (t + 1) * P)
            Rxn = R("xnT", t)
            (vg, Rvg) = vgs[t]
            bv = 4 + (t % 2)
            for dk in range(8):
                mm(banks[bv][:, :], xnT[:, dk, tcols], Wv[:, dk, :], start=(dk == 0), stop=(dk == 7),
                   reads=[RWv, Rxn], writes=[RB[bv]])
            act(vg, banks[bv][:], AF.Gelu, reads=[RB[bv]], writes=[Rvg, rsm(("s1", t))],
                accum_out=small[:, 40 + t:41 + t])
            act(tmpf, vg, AF.Square, reads=[Rvg], writes=[Rtmpf, rsm(("s2", t))], accum_out=small[:, 48 + t:49 + t])
        s1 = small[:, 40:40 + nt]; s2 = small[:, 48:48 + nt]
        mean = small[:, 56:56 + nt]; msq = small[:, 64:64 + nt]; var = small[:, 72:72 + nt]
        rs = small[:, 80:80 + nt]; nmr = small[:, 88:88 + nt]
        ts(mean, s1, 1.0 / 512, None, ALU.mult, None, reads=[rsm(("s1", t)) for t in range(nt)], writes=[rsm("mean")])
        tt(msq, mean, mean, ALU.mult, reads=[rsm("mean")], writes=[rsm("msq")])
        stt(var, s2, 1.0 / 512, msq, ALU.mult, ALU.subtract,
            reads=[rsm(("s2", t)) for t in range(nt)] + [rsm("msq")], writes=[rsm("var")])
        rsqrt_eps(rs, var, reads=[rsm("var")], writes=[rsm("lrs")])
        stt(nmr, mean, -1.0, rs, ALU.mult, ALU.mult, reads=[rsm("mean"), rsm("lrs")], writes=[rsm("nmr")])

        Wga0, RWga0 = WS.get((tag, "ga0"), 2)
        Wga1, RWga1 = WS.get((tag, "ga1"), 2)
        Wa, RWa = WS.get((tag, "wa"), 2)
        Wga = [(Wga0, RWga0), (Wga1, RWga1)]
        unit = [0]

        def m1_pe(t, half):
            tcols = slice(t * P, (t + 1) * P)
            u = unit[0]
            bg, ba = (u % 2) * 2, (u % 2) * 2 + 1
            W, RW = Wga[half]
            for dk in range(8):
                mm(bank_ap[bg], xnT[:, dk, tcols], W[:, dk, :], start=(dk == 0), stop=(dk == 7),
                   reads=[RW, R("xnT", t)], writes=[RBall[bg]])
            for h in range(4):
                mm(bank_ap[ba], ogT[:, h, tcols], Wa[:, h, half * 512:(half + 1) * 512],
                   start=(h == 0), stop=(h == 3), reads=[RWa, R("ogT", t)] + RogT, writes=[RBall[ba]])
            unit[0] += 1
            return bg, ba, u

        def m1_ew(t, half, bg, ba, u):
            hsl = slice(half * 512, (half + 1) * 512)
            sga, Rsga = sgs[u % 2]
            act(sga, bank_ap[bg], AF.Sigmoid, reads=[RBall[bg]], writes=[Rsga])
            stt(t1[:, t, hsl], bank_ap[ba], rstda[:, t:t + 1], sga, ALU.mult, ALU.mult,
                reads=[RBall[ba], R("rstda"), Rsga], writes=[R("t1", t)] + Rt1)

        for t in range(nt):
            tcols = slice(t * P, (t + 1) * P)
            (vg, Rvg) = vgs[t]
            RuT = RuTt[t]
            (vn, Rvn) = vns[t % 2]
            (tmpf, Rtmpf) = tmps[t % 2]
            (vv, Rvv) = vvs[t % 2]
            sample, wst, Rwst, bst, Rbst = tps[t].sample, tps[t].wst, tps[t].Rwst, tps[t].bst, tps[t].Rbst
            u0 = m1_pe(t, 0)
            act(vn, vg, AF.Identity, reads=[Rvg, rsm("lrs"), rsm("nmr")], writes=[Rvn],
                scale=rs[:, t:t + 1], bias=nmr[:, t:t + 1])
            tt(vn, vn, lnw[:], ALU.mult, reads=[Rvn, RC("lnw")], writes=[Rvn])
            tt(vn, vn, lnb[:], ALU.add, reads=[Rvn, RC("lnb")], writes=[Rvn])
            act(vv, vn, AF.Copy, reads=[Rvn], writes=[Rvv])
            if sample:
                store(o_gv, vn, reads=[Rvn])
            m1_ew(t, 0, *u0)
            u1 = m1_pe(t, 1)
            for g in range(4):
                gs = slice(g * P, (g + 1) * P)
                mm(banks[6][:, gs], vv[:, gs], wst[:, g, :], start=True, stop=True,
                   reads=[Rvv, Rwst], writes=[RB[6]])
            tt(tmpf, banks[6][:], bst[:], ALU.add, reads=[RB[6], Rbst], writes=[Rtmpf])
            tt(obT[:, :, tcols], tmpf.rearrange("p (g t) -> p g t", t=P), uT4[:, :, tcols],
               ALU.mult, reads=[Rtmpf, RuT], writes=[R("obT", t)] + RobT)
            m1_ew(t, 1, *u1)

        Wgb0, RWgb0 = WS.get((tag, "gb0"), 2)
        Wgb1, RWgb1 = WS.get((tag, "gb1"), 2)
        Wb, RWb = WS.get((tag, "wb"), 2)
        AR.reset()
        sgs2 = [AR.f(D) for _ in range(2)]
        t2s = [AR.f(D) for _ in range(2)]
        mbs = [AR.b(D) for _ in range(2)]
        Rsgb2 = [[AR._res() for _ in range(2)] for _ in range(2)]
        Rt22 = [[AR._res() for _ in range(2)] for _ in range(2)]

        def m2_front(t):
            tcols = slice(t * P, (t + 1) * P)
            Rxn = R("xnT", t)
            sgb, Rsgb = sgs2[t % 2]
            t2, Rt2 = t2s[t % 2]
            mb, Rmb = mbs[t % 2]
            for half, (W, RW) in enumerate([(Wgb0, RWgb0), (Wgb1, RWgb1)]):
                bg = (t % 2) * 2 + half
                for dk in range(8):
                    mm(bank_ap[bg], xnT[:, dk, tcols], W[:, dk, :], start=(dk == 0), stop=(dk == 7),
                       reads=[RW, Rxn], writes=[RBall[bg]])
            for half in range(2):
                ba = 4 + half
                for h in range(4):
                    mm(bank_ap[ba], obT[:, h, tcols], Wb[:, h, half * 512:(half + 1) * 512],
                       start=(h == 0), stop=(h == 3), reads=[RWb, R("obT", t)] + RobT, writes=[RBall[ba]])
            for half in range(2):
                bg = (t % 2) * 2 + half
                hsl = slice(half * 512, (half + 1) * 512)
                act(sgb[:, hsl], bank_ap[bg], AF.Sigmoid, reads=[RBall[bg]], writes=[Rsgb2[t % 2][half]])
            for half in range(2):
                ba = 4 + half
                hsl = slice(half * 512, (half + 1) * 512)
                tt(t2[:, hsl], bank_ap[ba], sgb[:, hsl], ALU.mult, reads=[RBall[ba], Rsgb2[t % 2][half]],
                   writes=[Rt22[t % 2][half]])
            tt(mb, t1[:, t, :], t2, ALU.add, reads=[R("t1", t)] + Rt22[t % 2] + Rt1, writes=[Rmb])

        def m2_back(t):
            tcols = slice(t * P, (t + 1) * P)
            mb, Rmb = mbs[t % 2]
            for j in range(8):
                tr(psT[:, j * P:(j + 1) * P], mb[:, j * P:(j + 1) * P], identb[:],
                   reads=[Rmb, RC("identb")], writes=[RpsT])
            act(mT[:, :, tcols], psT[:].rearrange("p (j t) -> p j t", t=P), AF.Copy,
                reads=[RpsT], writes=[R("xnT", t)])

        for t in range(nt + 1):
            if t < nt:
                m2_front(t)
            if t >= 1:
                m2_back(t - 1)

        Wo0, RWo0 = WS.get((tag, "wo0"), 3)
        Wo1, RWo1 = WS.get((tag, "wo1"), 3)
        AR.reset()
        junk, Rjunk = AR.b(D)
        xnb = [AR.b(D) for _ in range(2)]

        def m3_front(t):
            tcols = slice(t * P, (t + 1) * P)
            for half, (W, RW) in enumerate([(Wo0, RWo0), (Wo1, RWo1)]):
                bo = (t % 2) * 2 + half
                hsl = slice(half * 512, (half + 1) * 512)
                for dk in range(8):
                    mm(bank_ap[bo], mT[:, dk, tcols], W[:, dk, :], start=(dk == 0), stop=(dk == 7),
                       reads=[RW, R("xnT", t)], writes=[RBall[bo]])
                tt(xt[t][:, hsl], xt[t][:, hsl], bank_ap[bo], ALU.add,
                   reads=[R("xt", t), RBall[bo]], writes=[R("xt", t)])

        def ffn_norm(t):
            act(junk, xt[t], AF.Square, reads=[R("xt", t)], writes=[Rjunk, rsm(("ss", t))], accum_out=small[:, t:t + 1],
                scale=float(D) ** -0.5)
            rsqrt_eps(small[:, 8 + t:9 + t], small[:, t:t + 1], reads=[rsm(("ss", t))], writes=[rsm(("rs1", t))])
            xb, Rxb = xnb[t % 2]
            ts(xb, xt[t], small[:, 8 + t:9 + t], None, ALU.mult, None, reads=[R("xt", t), rsm(("rs1", t))],
               writes=[Rxb])
            for j in range(8):
                tr(psT[:, j * P:(j + 1) * P], xb[:, j * P:(j + 1) * P], identb[:],
                   reads=[Rxb, RC("identb")], writes=[RpsT])
            tt(xnT[:, :, t * P:(t + 1) * P], psT[:].rearrange("p (j t) -> p j t", t=P),
               nw[:, 1, :].unsqueeze(2).broadcast_to([P, 8, P]), ALU.mult,
               reads=[RpsT, RC("nw")], writes=[R("xnT", t)])

        for t in range(nt + 1):
            if t < nt:
                m3_front(t)
            if t >= 1:
                ffn_norm(t - 1)

        if has_s:
            for g in range(11):
                stg = cstg[g % 2]; Rstg = R("cstg", g % 2)
                load(stg[:], st_c[:, g * 512:(g + 1) * 512], writes=[Rstg])
                for jj in range(4):
                    tr(banks[g % 2][:, jj * 32:(jj + 1) * 32], stg[0:32, jj * P:(jj + 1) * P], identf[0:32, 0:32],
                       reads=[Rstg, RC("identf")], writes=[RB[g % 2]])
                for jj in range(4):
                    c = g * 4 + jj
                    act(scs[:, c % NJ, c // NJ, :], banks[g % 2][:, jj * 32:(jj + 1) * 32], AF.Copy,
                        reads=[RB[g % 2]], writes=[R("scs")])
        RS0fa = [R("S0f", 0), R("S0f", 1)]
        RS0ba = [R("S0b", 0), R("S0b", 1)]
        pT_ = []
        for t in range(nt):
            pT_.append((S0b[:, 2 * t:2 * t + 2, :].rearrange("p a b -> p (a b)"), RS0ba))

        def p_load(t):
            ptile = S0f[:, 2 * t:2 * t + 2, :].rearrange("p a b -> p (a b)")
            pbf = S0f[:, 10 + t, :].bitcast(BF16)
            load(ptile, tps[t].prows, writes=RS0fa)
            act(pbf, ptile, AF.Copy, reads=RS0fa, writes=RS0fa)

        def p_tr(t):
            pbf = S0f[:, 10 + t, :].bitcast(BF16)
            pTb = pT_[t][0]
            for kk in range(2):
                tr(psT[:, kk * P:(kk + 1) * P], pbf[:, kk * P:(kk + 1) * P], identb[:],
                   reads=RS0fa + [RC("identb")], writes=[RpsT])
            act(pTb, psT[:, 0:256], AF.Copy, reads=[RpsT], writes=RS0ba)

        groups = []
        if npt > 0:
            groups.append((slice(0, npt * P), list(range(npt)), False, 1, npt * P))
        if has_s:
            groups.append((slice(npt * P, (npt + 1) * P), [npt], True, NSEQ_S, LS))
        AR.reset()
        HS_ = {False: 512, True: P}
        npar_k = {False: (2 if has_s else 3), True: 2}
        yall_k = {False: [AR.f(1024)[0] for _ in range(npar_k[False])], True: [AR.f(2 * P)[0] for _ in range(2)]}
        Ryh_k = {kk_: [[AR._res() for _ in range(2)] for _ in range(npar_k[kk_])] for kk_ in (False, True)}
        Ryb_k = {kk_: [[AR._res() for _ in range(2)] for _ in range(npar_k[kk_])] for kk_ in (False, True)}
        bank0_k = {False: 0, True: 4}
        hhs = [AR.f(2 * NSEQ_S * 4)[0] for _ in range(2)] if has_s else None
        ptmp = [[AR.f(2) for _ in range(2)] for _ in range(3)]
        Rhhs = [[AR._res() for _ in range(2)] for _ in range(2)]
        it = 0
        pending_tail = [None]
        for j in range(NJ):
            Wup, RWup2 = WS.get((tag, "up", j), 5)
            if 1 <= j < 1 + nt:
                p_load(j - 1)
            if 8 <= j < 8 + nt:
                p_tr(j - 8)
            for (gcols, gtiles, sample, NS_, L_) in groups:
                TG = NS_ * L_
                Rxg = [R("xnT", q) for q in gtiles]
                par = j % npar_k[sample]
                yall, Ryh, Ryb, HS, b0 = yall_k[sample], Ryh_k[sample], Ryb_k[sample], HS_[sample], bank0_k[sample]
                y4 = yall[par].rearrange("p (a c) -> p a c", a=2)[:, :, 0:NS_ * L_].rearrange(
                    "p a (s l) -> p a s l", l=L_)
                if sample:
                    hh4 = hhs[par].rearrange("p (a s r) -> p a s r", a=2, r=4)
                    Rhh = Rhhs[par]
                    cp(hh4[:, :, :, 0:2], scs[:, j, :, :].rearrange("p a (s r) -> p a s r", r=2),
                       reads=[R("scs")], writes=Rhh)
                else:
                    hh4 = hh_p[:, (sp_idx % 2) * NJ + j, :, :].unsqueeze(2)
                    Rhh = [R("cr", sp_idx % 2, j)] * 2
                pvs = []
                for half in range(2):
                    bk = b0 + par * 2 + half
                    for dk in range(8):
                        mm(bank_ap[bk][:, 0:TG], Wup[:, dk, half * P:(half + 1) * P], xnT[:, dk, gcols],
                           start=(dk == 0), stop=(dk == 7), reads=[RWup2[half]] + Rxg, writes=[RBall[bk]])
                    pvs.append(bank_ap[bk][:, 0:NS_ * L_].rearrange("p (s l) -> p s l", l=L_))
                for half in range(2):
                    bk = b0 + par * 2 + half
                    act(y4[:, half], pvs[half], AF.Identity, reads=[RBall[bk], RC("cw"), RC("cb")],
                        writes=[Ryh[par][half], Ryb[par][half]], scale=cw[:, j, half, 2:3], bias=cb[:, j, half:half + 1])
                    act(hh4[:, half, :, 2:4], pvs[half][:, :, 0:2], AF.Copy, reads=[RBall[bk]], writes=[Rhh[half]])
                    if sample:
                        act(cso[:, j, half, :].rearrange("p (s r) -> p s r", r=2), pvs[half][:, :, L_ - 2:L_],
                            AF.Copy, reads=[RBall[bk]], writes=[R("cso")])
                    else:
                        act(hh_p[:, ((sp_idx + 1) % 2) * NJ + j, half, 0:2].unsqueeze(1), pvs[half][:, :, L_ - 2:L_],
                            AF.Copy, reads=[RBall[bk]], writes=[R("cr", (sp_idx + 1) % 2, j)])
                for half in range(2):
                    bk = b0 + par * 2 + half
                    for k in (1, 0):
                        stt(y4[:, half, :, 2:L_], pvs[half][:, :, k:L_ - 2 + k], cw[:, j, half, k:k + 1],
                            y4[:, half, :, 2:L_], ALU.mult, ALU.add,
                            reads=[RBall[bk], RC("cw"), Ryb[par][half]], writes=[Ryb[par][half]])
                        if k == 0 and not sample:
                            pt, Rpt = ptmp[par][half]
                            ts(pt, hh4[:, half, 0, 0:2], cw[:, j, half, 0:1], None, ALU.mult, None,
                               reads=[Rhh[half], RC("cw")], writes=[Rpt], eng="pool")
                            tt(y4[:, half, 0, 0:2], y4[:, half, 0, 0:2], pt, ALU.add,
                               reads=[Rpt, Ryh[par][half]], writes=[Ryh[par][half]], eng="pool")
                        else:
                            stt(y4[:, half, :, 0:2], hh4[:, half, :, k:k + 2], cw[:, j, half, k:k + 1],
                                y4[:, half, :, 0:2], ALU.mult, ALU.add,
                                reads=[Rhh[half], RC("cw"), Ryh[par][half]], writes=[Ryh[par][half]])

                def tail(j=j, par=par, gcols=gcols, TG=TG, yall=yall, Ryh=Ryh, Ryb=Ryb, HS=HS):
                    ya = yall[par][:, 0:TG]
                    yb = yall[par][:, HS:HS + TG]
                    act(ya, ya, AF.Gelu, reads=[Ryh[par][0], Ryb[par][0]], writes=[Ryh[par][0], Ryb[par][0]])
                    tt(gT[:, j, gcols], ya, yb, ALU.mult,
                       reads=[Ryh[par][0], Ryb[par][0], Ryh[par][1], Ryb[par][1]], writes=[RG[j]], eng="pool")

                if pending_tail[0] is not None:
                    pending_tail[0]()
                pending_tail[0] = tail
                it += 1
        if pending_tail[0] is not None:
            pending_tail[0]()
            pending_tail[0] = None
        def conv_out_groups(bank_ids):
            outs_ = []
            if has_s:
                outs_.append(True)
            if last_prompt:
                outs_.append(False)
            gidx = 0
            for sample in outs_:
                NR = 32 if sample else 2
                odst = o_cs if sample else o_cp
                for g in range(11):
                    bk = bank_ids[gidx % 2]
                    stg = cstg[gidx % 2]; Rstg = R("cstg", gidx % 2)
                    gidx += 1
                    for jj in range(4):
                        c = g * 4 + jj
                        if sample:
                            srcap = cso[:, c % NJ, c // NJ, :]
                            Rsrc = [R("cso")]
                        else:
                            srcap = hh_p[:, (NSP % 2) * NJ + c % NJ, c // NJ, 0:2]
                            Rsrc = [R("cr", NSP % 2, c % NJ)]
                        tr(bank_ap[bk][0:NR, jj * P:(jj + 1) * P], srcap, identf[:],
                           reads=Rsrc + [RC("identf")], writes=[RBall[bk]])
                    act(stg[0:NR, :], bank_ap[bk][0:NR, :], AF.Copy, reads=[RBall[bk]], writes=[Rstg])
                    store(odst[:, g * 512:(g + 1) * 512], stg[0:NR, :], reads=[Rstg])
                    yield

        if nt <= 4:
            passes = [list(range(nt))]
            cgen = conv_out_groups([0, 1])
            for _ in cgen:
                pass
            cgen = None
        else:
            passes = [list(range(3)), list(range(3, nt))]
            cgen = conv_out_groups([6, 7])
        n_pass = len(passes)
        dn_loaded = {}
        for ps, ptiles in enumerate(passes):
            for q in range(6):
                kcs = list(range(q * 4, min(NJ, (q + 1) * 4)))
                if ps == 0:
                    dn_loaded[q] = WS.get((tag, "dn", 0, q), 4 if n_pass == 1 else 5 - q)
                Wd, RWd = dn_loaded[q]
                for ci, kc in enumerate(kcs):
                    for t in ptiles:
                        tcols = slice(t * P, (t + 1) * P)
                        for half in range(2):
                            bk = (t - ptiles[0]) * 2 + half
                            mm(bank_ap[bk], gT[:, kc, tcols], Wd[:, ci, half * 512:(half + 1) * 512],
                               start=(kc == 0), stop=(kc == NJ - 1), reads=[RWd, RG[kc]], writes=[RBall[bk]])
                    if cgen is not None and ps == 0:
                        try:
                            next(cgen)
                        except StopIteration:
                            cgen = None
            for t in ptiles:
                for half in range(2):
                    bk = (t - ptiles[0]) * 2 + half
                    hsl = slice(half * 512, (half + 1) * 512)
                    tt(xt[t][:, hsl], xt[t][:, hsl], bank_ap[bk], ALU.add, reads=[R("xt", t), RBall[bk]],
                       writes=[R("xt", t)])
        if cgen is not None:
            for _ in cgen:
                pass

        Wg0, RWg0 = WS.get((tag, "pg0"), 3)
        Wg1, RWg1 = WS.get((tag, "pg1"), 3)
        Wp, RWp = WS.get((tag, "pp"), 3)
        AR.reset()
        xnbP = [AR.b(D) for _ in range(2)]
        for t in range(nt):
            xb, Rxb = xnbP[t % 2]
            cp(xb, xt[t], reads=[R("xt", t)], writes=[Rxb])
            for j in range(8):
                tr(psT[:, j * P:(j + 1) * P], xb[:, j * P:(j + 1) * P], identb[:],
                   reads=[Rxb, RC("identb")], writes=[RpsT])
            tt(xnT[:, :, t * P:(t + 1) * P], psT[:].rearrange("p (j t) -> p j t", t=P),
               nw[:, 2, :].unsqueeze(2).broadcast_to([P, 8, P]), ALU.mult,
               reads=[RpsT, RC("nw")], writes=[R("xnT", t)])
        sg_ = [AR.f(D) for _ in range(2)]
        yb_ = [AR.f(D) for _ in range(2)]
        junk, Rjunk = AR.b(D)
        xnb2 = [AR.b(D) for _ in range(2)]
        for t in range(nt):
            act(junk, xt[t], AF.Square, reads=[R("xt", t)], writes=[Rjunk, rsm(("pss", t))],
                accum_out=small[:, 24 + t:25 + t], scale=float(D) ** -0.5)
        rsqrt_eps(small[:, 32:32 + nt], small[:, 24:24 + nt], reads=[rsm(("pss", t)) for t in range(nt)],
                  writes=[rsm("prs")])

        def p4_front(t):
            tcols = slice(t * P, (t + 1) * P)
            pTb, RpT = pT_[t]
            for half, (W, RW) in enumerate([(Wg0, RWg0), (Wg1, RWg1)]):
                hsl = slice(half * 512, (half + 1) * 512)
                bg = (t % 2) * 4 + half
                bp = (t % 2) * 4 + 2 + half
                for dk in range(8):
                    mm(bank_ap[bg], xnT[:, dk, tcols], W[:, dk, :], start=(dk == 0), stop=(dk == 7),
                       reads=[RW, R("xnT", t)], writes=[RBall[bg]])
                for kk in range(2):
                    mm(bank_ap[bp], pTb[:, kk * P:(kk + 1) * P], Wp[:, kk, hsl], start=(kk == 0), stop=(kk == 1),
                       reads=[RWp, RpT], writes=[RBall[bp]])

        hf_ = [AR.f(D) for _ in range(nt)]
        n_next = len(next_x) if next_x is not None else 0

        Rsgh = [[AR._res() for _ in range(2)] for _ in range(2)]
        Rtmph = [[AR._res() for _ in range(2)] for _ in range(2)]

        def p4_back_a(t):
            sg, _ = sg_[t % 2]
            tmp, _ = yb_[t % 2]
            hf, Rhf = hf_[t]
            for half in range(2):
                hsl = slice(half * 512, (half + 1) * 512)
                bg = (t % 2) * 4 + half
                act(sg[:, hsl], bank_ap[bg], AF.Sigmoid, reads=[RBall[bg], rsm("prs")], writes=[Rsgh[t % 2][half]],
                    scale=small[:, 32 + t:33 + t])
            for half in range(2):
                hsl = slice(half * 512, (half + 1) * 512)
                bp = (t % 2) * 4 + 2 + half
                tt(tmp[:, hsl], bank_ap[bp], sg[:, hsl], ALU.mult, reads=[RBall[bp], Rsgh[t % 2][half]],
                   writes=[Rtmph[t % 2][half]])
            tt(hf, xt[t], tmp, ALU.add, reads=[R("xt", t)] + Rtmph[t % 2], writes=[Rhf])
            if t < n_next:
                load(xt[t], next_x[t], writes=[R("xt", t)])

        def p4_back_b(t):
            hf, Rhf = hf_[t]
            act(junk, hf, AF.Square, reads=[Rhf], writes=[Rjunk, rsm(("fs", t))], accum_out=small[:, 96 + t:97 + t],
                scale=float(D) ** -0.5)

        def next_sq(t, junk=junk, Rjunk=Rjunk):
            act(junk, xt[t], AF.Square, reads=[R("xt", t)], writes=[Rjunk, rsm(("nss", t))],
                accum_out=small[:, 112 + t:113 + t], scale=float(D) ** -0.5)

        def next_tr(t, xbuf=None):
            xb, Rxb = xbuf if xbuf is not None else xnb2[t % 2]
            ts(xb, xt[t], small[:, 120 + t:121 + t], None, ALU.mult, None, reads=[R("xt", t), rsm("nrs")],
               writes=[Rxb])
            for j in range(8):
                tr(psT[:, j * P:(j + 1) * P], xb[:, j * P:(j + 1) * P], identb[:],
                   reads=[Rxb, RC("identb")], writes=[RpsT])
            tt(xnT[:, :, t * P:(t + 1) * P], psT[:].rearrange("p (j t) -> p j t", t=P),
               nw[:, 0, :].unsqueeze(2).broadcast_to([P, 8, P]), ALU.mult,
               reads=[RpsT, RC("nw")], writes=[R("xnT", t)])

        def fin(t):
            hf, Rhf = hf_[t]
            stt(hf, hf, small[:, 104 + t:105 + t], fnw[:], ALU.mult, ALU.mult,
                reads=[Rhf, rsm("frs"), RC("fnw")], writes=[Rhf])
            store(tps[t].yrows, hf, reads=[Rhf])

        for t in range(nt, n_next):
            load(xt[t], next_x[t], writes=[R("xt", t)])
        p4_front(0)
        for t in range(nt):
            if t + 1 < nt:
                p4_front(t + 1)
            p4_back_a(t)
            if 0 <= t - 1 < n_next:
                next_sq(t - 1)
            p4_back_b(t)
        late = nt - 1 if nt - 1 < n_next else None
        for t in range(nt, n_next):
            next_sq(t)
        lo = min(max(nt - 1, 0), n_next)
        if lo > 0:
            rsqrt_eps(small[:, 120:120 + lo], small[:, 112:112 + lo],
                      reads=[rsm(("nss", t)) for t in range(lo)], writes=[rsm("nrs")])
        if n_next > nt:
            rsqrt_eps(small[:, 120 + nt:120 + n_next], small[:, 112 + nt:112 + n_next],
                      reads=[rsm(("nss", t)) for t in range(nt, n_next)], writes=[rsm("nrs")])
        for t in range(n_next):
            if t != late:
                next_tr(t)
        if late is not None:
            def late_norm(t=late):
                jk = S0f[:, 4:8, :].rearrange("p a b -> p (a b)").bitcast(BF16)
                xbl = S0f[:, 0:4, :].rearrange("p a b -> p (a b)").bitcast(BF16)
                next_sq(t, jk, R("S0f", 0))
                rsqrt_eps(small[:, 120 + t:121 + t], small[:, 112 + t:113 + t], reads=[rsm(("nss", t))],
                          writes=[rsm("nrs")])
                next_tr(t, (xbl, R("S0f", 0)))
            pending_late[0] = late_norm
        rsqrt_eps(small[:, 104:104 + nt], small[:, 96:96 + nt], reads=[rsm(("fs", t)) for t in range(nt)],
                  writes=[rsm("frs")])
        for t in range(nt):
            fin(t)

    NSP = SEQ // TT
    pending_late = [None]
    kinds_of = [["p"] * NT for _ in range(NSP)]
    kinds_of[-1] = kinds_of[-1] + ["s"]
    for sp_idx in range(NSP):
        register_weights(("st", sp_idx), n_dn_pass=1)
    for sp_idx in range(NSP):
        nx = None
        if sp_idx + 1 < NSP:
            nx = [xp[(sp_idx + 1) * TT + t * P:(sp_idx + 1) * TT + (t + 1) * P, :] for t in range(NT)]
            if sp_idx + 1 == NSP - 1:
                nx.append(xs[0:P, :])
        run_supertile(kinds_of[sp_idx], sp_idx, preloaded=(sp_idx > 0), next_x=nx)

    S.finish()
    S.replay()


_NC_CACHE = {}


def _consts():
    i = np.arange(P)
    c = {}
    c["c_identb"] = np.eye(P, dtype=np.float32).astype(ml_dtypes.bfloat16)
    c["c_identf"] = np.eye(P, dtype=np.float32)
    s, t = np.meshgrid(i, i, indexing="ij")
    c["c_maskp"] = ((s // 64 == t // 64) & (s <= t)).astype(np.float32)
    c["c_masks"] = ((s // 8 == t // 8) & (s <= t)).astype(np.float32)
    c["c_maskf"] = (s <= t).astype(np.float32)
    tcol = np.arange(512) % 128
    c["c_scanp"] = np.broadcast_to((tcol % 64 != 0).astype(np.float32), (P, 512)).copy()
    c["c_scans"] = np.broadcast_to((tcol % 8 != 0).astype(np.float32), (P, 512)).copy()
    c["c_bmask"] = (i[:, None] // 8 == np.arange(16)[None, :]).astype(np.float32)
    c["c_onesf"] = np.ones((P, 1), np.float32)
    return c


def _fm(v, nchunk):
    return np.ascontiguousarray(np.asarray(v, np.float32).reshape(nchunk, P).T)


def kernel(x_prompt, x_sample, p_prompt, p_sample, state_hgrn, state_conv, lb_logits,
           norm_mix_w, w_in, hgrn_norm_w, ln_v_w, ln_v_b, w_spatial, b_spatial, w_a_out,
           w_b_out, w_o, norm_ffn_w, w_up, conv_w, conv_b, w_down, norm_ple_w, w_ple_gate,
           w_ple_proj, final_norm_w, _debug=None):
    f32 = np.float32
    A = lambda a: np.ascontiguousarray(np.asarray(a, dtype=f32))
    if "nc" not in _NC_CACHE or _debug is not None:
        nc = build_nc(_debug)
        if _debug is None:
            _NC_CACHE["nc"] = nc
    else:
        nc = _NC_CACHE["nc"]
    shared = _consts()
    shared["w_in"] = A(w_in[0]); shared["w_a"] = A(w_a_out[0]); shared["w_b"] = A(w_b_out[0])
    shared["w_o"] = A(w_o[0]); shared["w_up"] = np.ascontiguousarray(
        np.asarray(w_up[0], dtype=f32).reshape(D, 2, NJ, P).transpose(0, 2, 1, 3).reshape(D, 2 * DFF)); shared["w_dn"] = A(w_down[0])
    shared["w_pg"] = A(w_ple_gate[0]); shared["w_pp"] = A(w_ple_proj[0])
    shared["v_nw"] = np.ascontiguousarray(np.stack(
        [_fm(norm_mix_w[0], 8), _fm(norm_ffn_w[0], 8), _fm(norm_ple_w[0], 8)], axis=1))
    shared["v_hgw"] = _fm(hgrn_norm_w[0], 4)
    lbl = np.asarray(lb_logits, f32)
    shared["v_lbl"] = np.ascontiguousarray(np.stack([_fm(lbl[0], 4), _fm(lbl[1], 4)], axis=1))
    shared["v_lnw"] = A(ln_v_w[0]); shared["v_lnb"] = A(ln_v_b[0]); shared["v_fnw"] = A(final_norm_w)
    bs = np.asarray(b_spatial[0], f32)
    shared["v_bsp"] = np.ascontiguousarray(bs.reshape(512))
    shared["v_bss"] = np.ascontiguousarray(np.tile(bs[:, :LS], (1, NSEQ_S)).reshape(512))
    cwn = np.asarray(conv_w[0], f32)
    shared["v_cw"] = np.ascontiguousarray(cwn.T.reshape(2, NJ, P, 3).transpose(2, 1, 0, 3))
    shared["v_cb"] = np.ascontiguousarray(np.asarray(conv_b[0], f32).reshape(2, NJ, P).transpose(2, 1, 0))
    ws = np.asarray(w_spatial[0], f32)
    shared["v_wsp"] = np.ascontiguousarray(ws.transpose(2, 0, 1))
    wss = np.zeros((P, 4, P), f32)
    sub = ws[:, :LS, :LS].transpose(2, 0, 1)
    for j in range(NSEQ_S):
        wss[j * LS:(j + 1) * LS, :, j * LS:(j + 1) * LS] = sub
    shared["v_wss"] = wss

    xpn = np.asarray(x_prompt, f32); xsn = np.asarray(x_sample, f32)
    ppn = np.asarray(p_prompt, f32)[0]; psn = np.asarray(p_sample, f32)[0]
    sth = np.asarray(state_hgrn, f32)[0]; stc = np.asarray(state_conv, f32)[0]
    in_maps = []
    for c in range(8):
        m = dict(shared)
        m["xp"] = np.ascontiguousarray(xpn[c])
        m["xs"] = np.ascontiguousarray(xsn[c * 16:(c + 1) * 16].reshape(P, D))
        m["pp"] = np.ascontiguousarray(ppn[c])
        m["ps"] = np.ascontiguousarray(psn[c * 16:(c + 1) * 16].reshape(P, 256))
        m["st_h"] = np.ascontiguousarray(sth[c * 16:(c + 1) * 16])
        m["st_c"] = np.ascontiguousarray(stc[c * 16:(c + 1) * 16].reshape(32, 2 * DFF))
        in_maps.append(m)
    res = run_bass_kernel_spmd(nc, in_maps, core_ids=list(range(8)))
    rs = res.results
    y_prompt = np.stack([r["y_p"] for r in rs], 0).astype(f32)
    y_sample = np.concatenate([r["y_s"].reshape(16, LS, D) for r in rs], 0).astype(f32)
    hgrn_p = np.stack([r["o_hp"] for r in rs], 0)[None].astype(f32)
    hgrn_s = np.concatenate([r["o_hs"] for r in rs], 0)[None].astype(f32)
    conv_p = np.stack([r["o_cp"] for r in rs], 0)[None].astype(f32)
    conv_s = np.concatenate([r["o_cs"].reshape(16, 2, 2 * DFF) for r in rs], 0)[None].astype(f32)
    gv_s = np.concatenate([r["o_gv"].reshape(16, LS, 512) for r in rs], 0)[None].astype(f32)
    if _debug is not None:
        return rs
    return (y_prompt, y_sample, hgrn_p, hgrn_s, conv_p, conv_s, gv_s)
```
